# Optimizing a Trainium2 kernel written in Bass

```python
import jax, jax.numpy as jnp
from jax import lax
import numpy as np

D_MODEL = 1024
BATCH = 32
SEQ = 256
DEPTH = 4
DEC_BATCH = 4
DEC_SEQ = 4096
PAST_LEN = 512

GRID_W = 64
N_AB_LAYERS = (DEPTH + 1) // 2
N_C_LAYERS = DEPTH // 2
EPS = 1e-6

MLA_HEADS = 8
MLA_NOPE = 64
MLA_ROPE = 32
MLA_V = 64
Q_LORA = 256
KV_LORA = 128
ROPE_BASE = 10000.0
QB = 128

RET_HEADS = 4
RET_DK = 128
RET_DV = 128
RET_CHUNK = 128

GM_CHUNK = 128
GM_GROUPS = 8
GM_WIDTH = D_MODEL

D_FF = -(-8 * D_MODEL // (3 * 256)) * 256

AB_SIZES = (Q_LORA, KV_LORA, MLA_ROPE, RET_HEADS * RET_DK, RET_HEADS * RET_DK, RET_HEADS * RET_DV, RET_HEADS * RET_DV)
AB_IN = sum(AB_SIZES)
AB_SPLITS = tuple(int(s) for s in np.cumsum(AB_SIZES)[:-1])
MIX_WIDTH = MLA_HEADS * MLA_V + RET_HEADS * RET_DV

kernel_name = 'hybrid_mla_retention_gmlp_dit_step'


def rmsnorm(x, g):
    xf = x.astype(jnp.float32)
    y = xf * lax.rsqrt(jnp.mean(xf * xf, axis=-1, keepdims=True) + EPS)
    return (y * g.astype(jnp.float32)).astype(x.dtype)


def rms_noaffine(x):
    xf = x.astype(jnp.float32)
    return (xf * lax.rsqrt(jnp.mean(xf * xf, axis=-1, keepdims=True) + EPS)).astype(x.dtype)


def layernorm(x, g, b):
    xf = x.astype(jnp.float32)
    mu = jnp.mean(xf, axis=-1, keepdims=True)
    var = jnp.mean(jnp.square(xf - mu), axis=-1, keepdims=True)
    y = (xf - mu) * lax.rsqrt(var + EPS) * g.astype(jnp.float32) + b.astype(jnp.float32)
    return y.astype(x.dtype)


def modulate(x, shift, scale):
    return x * (1 + scale) + shift


def ada_mod(cond, w, b):
    m = (jax.nn.silu(cond) @ w + b)[:, None, :]
    return jnp.split(m, 6, axis=-1)


def axial_rope_tables(length, dtype):
    rows = length // GRID_W
    row = jnp.repeat(jnp.arange(rows), GRID_W).astype(jnp.float32)
    col = jnp.tile(jnp.arange(GRID_W), rows).astype(jnp.float32)
    half = MLA_ROPE // 2
    inv = 1.0 / jnp.power(ROPE_BASE, jnp.arange(0, half, 2, dtype=jnp.float32) / half)
    ang = jnp.stack([row[:, None] * inv, col[:, None] * inv], axis=1)
    ang = jnp.stack([ang, ang], axis=2).reshape(length, MLA_ROPE)
    return jnp.cos(ang).astype(dtype), jnp.sin(ang).astype(dtype)


def rotate_half_axial(x):
    xs = x.reshape(x.shape[:-1] + (2, 2, MLA_ROPE // 4))
    return jnp.stack([-xs[..., 1, :], xs[..., 0, :]], axis=-2).reshape(x.shape)


def apply_rope(x, cos, sin):
    return x * cos + rotate_half_axial(x) * sin


def to_blocks(t):
    b, l = t.shape[:2]
    return jnp.moveaxis(t.reshape((b, l // QB, QB) + t.shape[2:]), 1, 0)


def from_blocks(t):
    t = jnp.moveaxis(t, 0, 1)
    return t.reshape((t.shape[0], t.shape[1] * t.shape[2]) + t.shape[3:])


def mla_attention(q_nope, q_ropes, key_groups):
    v_all = jnp.concatenate([g[2] for g in key_groups], axis=1)
    scale = (MLA_NOPE + MLA_ROPE) ** -0.5

    def block(args):
        qn, qrs = args
        s = jnp.concatenate(
            [jnp.einsum('bqhd,bkhd->bhqk', qn, kn) + jnp.einsum('bqhr,bkr->bhqk', qr, kr)
             for qr, (kn, kr, _) in zip(qrs, key_groups)], axis=-1)
        p = jax.nn.softmax(s.astype(jnp.float32) * scale, axis=-1).astype(v_all.dtype)
        return jnp.einsum('bhqk,bkhd->bqhd', p, v_all)

    out = lax.map(block, (to_blocks(q_nope), tuple(to_blocks(q) for q in q_ropes)))
    return from_blocks(out)


def retention_scan(q, k, v, log_gamma, s0):
    b, l, h = q.shape[:3]
    n = l // RET_CHUNK

    def chunks(t):
        return jnp.moveaxis(t.reshape(b, n, RET_CHUNK, h, t.shape[3]), 1, 0).transpose(0, 1, 3, 2, 4)

    lg = log_gamma.astype(jnp.float32)
    idx = jnp.arange(RET_CHUNK, dtype=jnp.float32)
    rel = idx[:, None] - idx[None, :]
    dmat = jnp.where(rel >= 0, jnp.exp(lg[:, None, None] * jnp.maximum(rel, 0.0)), 0.0).astype(q.dtype)
    q_decay = jnp.exp(lg[:, None] * (idx + 1.0)).astype(q.dtype)
    k_decay = jnp.exp(lg[:, None] * (RET_CHUNK - 1.0 - idx)).astype(k.dtype)
    chunk_decay = jnp.exp(lg * RET_CHUNK).astype(s0.dtype)

    def step(s, qkv):
        qc, kc, vc = qkv
        inner = jnp.einsum('bhij,bhjd->bhid', jnp.einsum('bhid,bhjd->bhij', qc, kc) * dmat, vc)
        cross = jnp.einsum('bhid,bhde->bhie', qc, s) * q_decay[..., None]
        s_new = s * chunk_decay[:, None, None] + jnp.einsum('bhjd,bhje->bhde', kc * k_decay[..., None], vc)
        return s_new.astype(s.dtype), inner + cross

    s_fin, o = lax.scan(step, s0, (chunks(q), chunks(k), chunks(v)))
    o = o.transpose(1, 0, 3, 2, 4).reshape(b, l, h, v.shape[3])
    return o, s_fin


def bi_retention(q, k, v, log_decay, s0_fwd, s0_bwd):
    lg = -jnp.exp(log_decay.astype(jnp.float32))
    o_f, s_f = retention_scan(q, k, v, lg[0], s0_fwd)
    o_b, s_b = retention_scan(q[:, ::-1], k[:, ::-1], v[:, ::-1], lg[1], s0_bwd)
    return o_f + o_b[:, ::-1], jnp.stack([s_f, s_b], axis=1)


def ab_project(h, w_in, q_norm_g, w_uq, kv_norm_g, w_ukv):
    b, l, _ = h.shape
    cq, ckv, kr, rq, rk, rv, rg = jnp.split(h @ w_in, AB_SPLITS, axis=-1)
    q = (rmsnorm(cq, q_norm_g) @ w_uq).reshape(b, l, MLA_HEADS, MLA_NOPE + MLA_ROPE)
    ckv = rmsnorm(ckv, kv_norm_g)
    kn, v = mla_kv_up(ckv, w_ukv)
    rq = rq.reshape(b, l, RET_HEADS, RET_DK)
    rk = rk.reshape(b, l, RET_HEADS, RET_DK) * (RET_DK ** -0.5)
    rv = rv.reshape(b, l, RET_HEADS, RET_DV)
    return q[..., :MLA_NOPE], q[..., MLA_NOPE:], ckv, kr, kn, v, rq, rk, rv, rg


def mla_kv_up(ckv, w_ukv):
    kv = (ckv @ w_ukv).reshape(ckv.shape[:2] + (MLA_HEADS, MLA_NOPE + MLA_V))
    return kv[..., :MLA_NOPE], kv[..., MLA_NOPE:]


def ab_merge(o_attn, o_ret, rg, w_o):
    b, l = o_attn.shape[:2]
    ret = rms_noaffine(o_ret).reshape(b, l, RET_HEADS * RET_DV) * jax.nn.silu(rg)
    return jnp.concatenate([o_attn.reshape(b, l, MLA_HEADS * MLA_V), ret], axis=-1) @ w_o


def ab_context(h, w_in, q_norm_g, w_uq, kv_norm_g, w_ukv, ret_log_decay, w_o):
    b = h.shape[0]
    qn, qr, ckv, kr, kn, v, rq, rk, rv, rg = ab_project(h, w_in, q_norm_g, w_uq, kv_norm_g, w_ukv)
    o_attn = mla_attention(qn, (qr,), ((kn, kr, v),))
    zeros = jnp.zeros((b, RET_HEADS, RET_DK, RET_DV), h.dtype)
    o_ret, s_ret = bi_retention(rq, rk, rv, ret_log_decay, zeros, zeros)
    return ab_merge(o_attn, o_ret, rg, w_o), ckv, kr, s_ret


def ab_latent(h, ctx_ckv, ctx_kr, ctx_state, w_in, q_norm_g, w_uq, kv_norm_g, w_ukv, ret_log_decay, w_o):
    l = h.shape[1]
    qn, qr, ckv, kr, kn, v, rq, rk, rv, rg = ab_project(h, w_in, q_norm_g, w_uq, kv_norm_g, w_ukv)
    cos, sin = axial_rope_tables(l, h.dtype)
    qr_rot = apply_rope(qr, cos[:, None, :], sin[:, None, :])
    kr_rot = apply_rope(kr, cos, sin)
    kn_c, v_c = mla_kv_up(ctx_ckv, w_ukv)
    o_attn = mla_attention(qn, (qr, qr_rot), ((kn_c, ctx_kr, v_c), (kn, kr_rot, v)))
    o_ret, _ = bi_retention(rq, rk, rv, ret_log_decay, ctx_state[:, 0], ctx_state[:, 1])
    return ab_merge(o_attn, o_ret, rg, w_o)


def chunk_gmlp(h, w_in, ln_g, ln_b, w_s, b_s, w_out):
    b, l, _ = h.shape
    u, v = jnp.split(jax.nn.gelu(h @ w_in), 2, axis=-1)
    v = layernorm(v, ln_g, ln_b)
    v = v.reshape(b, l // GM_CHUNK, GM_CHUNK, GM_GROUPS, GM_WIDTH // GM_GROUPS)
    v = jnp.einsum('gij,bnjgd->bnigd', w_s, v) + b_s.T[:, :, None]
    return (u * v.reshape(b, l, GM_WIDTH)) @ w_out


def swiglu(h, w_gate, w_up, w_down):
    return (jax.nn.silu(h @ w_gate) * (h @ w_up)) @ w_down


def setup_inputs(seed: int = 0) -> dict:
    key = jax.random.key(seed)
    ks = jax.random.split(key, 32)

    def nrm(i, shape, scale=1.0):
        return jax.random.normal(ks[i], shape, jnp.float32) * scale

    decay_base = (-5.0 - jnp.arange(RET_HEADS, dtype=jnp.float32)) * jnp.log(2.0)
    return {
        'x_prompt': nrm(0, (BATCH, SEQ, D_MODEL)),
        'x_sample': nrm(1, (DEC_BATCH, DEC_SEQ, D_MODEL)),
        'cache_mla_ckv': nrm(2, (DEC_BATCH, N_AB_LAYERS, PAST_LEN, KV_LORA)),
        'cache_mla_krope': nrm(3, (DEC_BATCH, N_AB_LAYERS, PAST_LEN, MLA_ROPE)),
        'state_ret': nrm(4, (DEC_BATCH, N_AB_LAYERS, 2, RET_HEADS, RET_DK, RET_DV), 0.5),
        'c': nrm(5, (DEC_BATCH, D_MODEL)),
        'c_ctx': nrm(6, (D_MODEL,)),
        'w_ada': nrm(7, (DEPTH, D_MODEL, 6 * D_MODEL), D_MODEL ** -0.5),
        'b_ada': nrm(8, (DEPTH, 6 * D_MODEL), 0.01),
        'norm_g': 1.0 + nrm(9, (DEPTH, 2, D_MODEL), 0.02),
        'w_in_ab': nrm(10, (N_AB_LAYERS, D_MODEL, AB_IN), D_MODEL ** -0.5),
        'q_norm_g': 1.0 + nrm(11, (N_AB_LAYERS, Q_LORA), 0.02),
        'w_uq': nrm(12, (N_AB_LAYERS, Q_LORA, MLA_HEADS * (MLA_NOPE + MLA_ROPE)), Q_LORA ** -0.5),
        'kv_norm_g': 1.0 + nrm(13, (N_AB_LAYERS, KV_LORA), 0.02),
        'w_ukv': nrm(14, (N_AB_LAYERS, KV_LORA, MLA_HEADS * (MLA_NOPE + MLA_V)), KV_LORA ** -0.5),
        'ret_log_decay': decay_base + nrm(15, (N_AB_LAYERS, 2, RET_HEADS), 0.1),
        'w_o_ab': nrm(16, (N_AB_LAYERS, MIX_WIDTH, D_MODEL), MIX_WIDTH ** -0.5),
        'w_in_c': nrm(17, (N_C_LAYERS, D_MODEL, 2 * GM_WIDTH), D_MODEL ** -0.5),
        'ln_g_c': 1.0 + nrm(18, (N_C_LAYERS, GM_WIDTH), 0.02),
        'ln_b_c': nrm(19, (N_C_LAYERS, GM_WIDTH), 0.01),
        'w_s_c': nrm(20, (N_C_LAYERS, GM_GROUPS, GM_CHUNK, GM_CHUNK), GM_CHUNK ** -0.5),
        'b_s_c': 1.0 + nrm(21, (N_C_LAYERS, GM_GROUPS, GM_CHUNK), 0.02),
        'w_out_c': nrm(22, (N_C_LAYERS, GM_WIDTH, D_MODEL), GM_WIDTH ** -0.5),
        'w_ffn_gate': nrm(23, (DEPTH, D_MODEL, D_FF), D_MODEL ** -0.5),
        'w_ffn_up': nrm(24, (DEPTH, D_MODEL, D_FF), D_MODEL ** -0.5),
        'w_ffn_down': nrm(25, (DEPTH, D_FF, D_MODEL), D_FF ** -0.5),
        'final_norm_g': 1.0 + nrm(26, (D_MODEL,), 0.02),
    }


def reference(x_prompt, x_sample, cache_mla_ckv, cache_mla_krope, state_ret, c, c_ctx,
              w_ada, b_ada, norm_g, w_in_ab, q_norm_g, w_uq, kv_norm_g, w_ukv, ret_log_decay, w_o_ab,
              w_in_c, ln_g_c, ln_b_c, w_s_c, b_s_c, w_out_c, w_ffn_gate, w_ffn_up, w_ffn_down, final_norm_g):
    xp, xs = x_prompt, x_sample
    ckv_list, kr_list, s_list = [], [], []
    for l in range(DEPTH):
        j = l // 2
        sh1p, sc1p, g1p, sh2p, sc2p, g2p = ada_mod(c_ctx[None, :], w_ada[l], b_ada[l])
        sh1s, sc1s, g1s, sh2s, sc2s, g2s = ada_mod(c, w_ada[l], b_ada[l])
        hp = modulate(rmsnorm(xp, norm_g[l, 0]), sh1p, sc1p)
        hs = modulate(rmsnorm(xs, norm_g[l, 0]), sh1s, sc1s)
        if l % 2 == 0:
            yp, ckv_l, kr_l, s_l = ab_context(hp, w_in_ab[j], q_norm_g[j], w_uq[j], kv_norm_g[j], w_ukv[j],
                                              ret_log_decay[j], w_o_ab[j])
            ys = ab_latent(hs, cache_mla_ckv[:, j], cache_mla_krope[:, j], state_ret[:, j],
                           w_in_ab[j], q_norm_g[j], w_uq[j], kv_norm_g[j], w_ukv[j], ret_log_decay[j], w_o_ab[j])
            ckv_list.append(ckv_l)
            kr_list.append(kr_l)
            s_list.append(s_l)
        else:
            yp = chunk_gmlp(hp, w_in_c[j], ln_g_c[j], ln_b_c[j], w_s_c[j], b_s_c[j], w_out_c[j])
            ys = chunk_gmlp(hs, w_in_c[j], ln_g_c[j], ln_b_c[j], w_s_c[j], b_s_c[j], w_out_c[j])
        xp = xp + g1p * yp
        xs = xs + g1s * ys
        hp = modulate(rmsnorm(xp, norm_g[l, 1]), sh2p, sc2p)
        hs = modulate(rmsnorm(xs, norm_g[l, 1]), sh2s, sc2s)
        xp = xp + g2p * swiglu(hp, w_ffn_gate[l], w_ffn_up[l], w_ffn_down[l])
        xs = xs + g2s * swiglu(hs, w_ffn_gate[l], w_ffn_up[l], w_ffn_down[l])
    y_prompt = rmsnorm(xp, final_norm_g)
    y_sample = rmsnorm(xs, final_norm_g)
    new_cache_mla_ckv = jnp.stack(ckv_list, axis=1)
    new_cache_mla_krope = jnp.stack(kr_list, axis=1)
    new_state_ret = jnp.stack(s_list, axis=1)
    return (y_prompt, y_sample, new_cache_mla_ckv, new_cache_mla_krope, new_state_ret)
```

```python
import contextlib
import os
import numpy as np
import concourse.bass as bass
import concourse.mybir as mybir
from concourse.bass_utils import run_bass_kernel_spmd

F32 = mybir.dt.float32
BF16 = mybir.dt.bfloat16
AF = mybir.ActivationFunctionType
ALU = mybir.AluOpType
AX = mybir.AxisListType

D = 1024
DEPTH = 4
DFF = 2816
NF = 22
NT = 24
NG = 6
EPS = 1e-6
ABIN = 2464
SCALE = 96 ** -0.5
WINDOW = 3


class Res:
    __slots__ = ("w", "r", "excl")

    def __init__(self, excl=False):
        self.w = None
        self.r = {}
        self.excl = excl


def RL(n):
    return [Res() for _ in range(n)]


class Eng:
    def __init__(self, key, eng, sems):
        self.key = key
        self.eng = eng
        self.sems = sems
        self.sem = sems[0]
        self.count = 0
        self.waited = {}


class DSem:
    def __init__(self, key, sem):
        self.key = key
        self.sem = sem
        self.total = 0


class Builder:
    def __init__(self, nc, es):
        self.nc = nc
        self.es = es
        mk = lambda n: es.enter_context(nc.semaphore(n))
        self.PE = Eng("pe", nc.tensor, [mk("s_pe0"), mk("s_pe1"), mk("s_pe2")])
        self.ACT = Eng("act", nc.scalar, [mk("s_act0"), mk("s_act1"), mk("s_act2")])
        self.DVE = Eng("dve", nc.vector, [mk("s_dve0"), mk("s_dve1"), mk("s_dve2")])
        self.POOL = Eng("pool", nc.gpsimd, [mk("s_pool0"), mk("s_pool1"), mk("s_pool2")])
        self.SP = Eng("sp", nc.sync, [mk("s_sp0"), mk("s_sp1"), mk("s_sp2")])
        self.engs = [self.PE, self.ACT, self.DVE, self.POOL, self.SP]
        self.ekeys = {E.key for E in self.engs}
        self.epoch = 0
        self.dsems = [DSem(f"d{i}", mk(f"s_d{i}")) for i in range(72)]
        self.dfree_hw = self.dsems[:44]
        self.dfree_sw = self.dsems[44:]
        self.cc_sem = mk("s_cc")
        self.cc_count = 0
        self.ps = [es.enter_context(nc.psum_tensor(f"ps{i}", [128, 512], F32)) for i in range(8)]
        self.psr = [Res(excl=True) for _ in range(8)]
        self.bank_i = 0
        self.uid = 0
        self.phase_es = None
        self.phase_ds = []

    def sb(self, name, shape, dtype, persistent=False):
        self.uid += 1
        stack = self.es if (persistent or self.phase_es is None) else self.phase_es
        return stack.enter_context(self.nc.sbuf_tensor(f"{name}_{self.uid}", shape, dtype))

    def ds(self, persistent=False, sw=False):
        pool = self.dfree_sw if sw else self.dfree_hw
        d = pool.pop()
        if not persistent and self.phase_es is not None:
            self.phase_ds.append((d, pool))
        return d

    def dsw(self):
        return self.ds(sw=True)

    def bank(self):
        i = self.bank_i
        self.bank_i = (i + 1) % 8
        return i

    @contextlib.contextmanager
    def phase(self):
        assert self.phase_es is None
        with contextlib.ExitStack() as pes:
            self.phase_es = pes
            self.phase_ds = []
            yield
            self.barrier()
            for d, pool in self.phase_ds:
                pool.append(d)
            self.phase_ds = []
            self.phase_es = None

    def _wait(self, E, deps):
        for key, (sem, val) in deps.items():
            if key == E.key:
                if E is self.PE or E is self.SP:
                    continue
                if val <= E.count - WINDOW:
                    continue
            if E.waited.get(key, 0) >= val:
                continue
            E.eng.wait_ge(sem, val)
            E.waited[key] = val

    def _collect(self, reads, writes):
        deps = {}

        def add(rec):
            if rec is None:
                return
            key, sem, val, ep = rec
            if ep is not None and ep < self.epoch:
                return
            if key not in deps or deps[key][1] < val:
                deps[key] = (sem, val)

        for r in reads:
            add(r.w)
            if r.excl:
                for rec in r.r.values():
                    add(rec)
        for w in writes:
            add(w.w)
            for rec in w.r.values():
                add(rec)
        return deps

    @staticmethod
    def _commit(rec, reads, writes):
        for w in writes:
            w.w = rec
            w.r = {}
        for r in reads:
            r.r[rec[0]] = rec

    def op(self, E, fn, reads=(), writes=()):
        self._wait(E, self._collect(reads, writes))
        ins = fn(E.eng)
        E.count += 1
        ins.then_inc(E.sem, 1)
        self._commit((E.key, E.sem, E.count, self.epoch), reads, writes)

    def dma(self, Q, out, in_, reads, writes, d):
        assert (Q is self.POOL) == (d in self.dsems[44:]), "SW/HW DMA semaphore pools must not mix"
        deps = self._collect(reads, writes)
        if d.total:
            deps[d.key] = (d.sem, d.total)
        self._wait(Q, deps)
        ins = Q.eng.dma_start(out=out, in_=in_)
        d.total += 16
        ins.then_inc(d.sem, 16)
        self._commit((d.key, d.sem, d.total, None), reads, writes)

    def barrier(self):
        for E in self.engs:
            deps = {}
            for o in self.engs:
                if o is not E and o.count:
                    deps[o.key] = (o.sem, o.count)
            for d in self.dsems:
                if d.total:
                    deps[d.key] = (d.sem, d.total)
            if self.cc_count:
                deps["cc"] = (self.cc_sem, self.cc_count)
            self._wait(E, deps)
        self.epoch += 1
        for E in self.engs:
            old = E.sem
            E.sem = E.sems[self.epoch % 3]
            E.count = 0
            for k in self.ekeys:
                E.waited.pop(k, None)
        for E in self.engs:
            E.eng.sem_clear(E.sems[(self.epoch + 1) % 3])

    def allgather(self, in_t, out_t, in_res, out_res):
        P = self.POOL
        self._wait(P, self._collect([in_res], [out_res]))
        ins = P.eng.collective_compute(
            "AllGather", ALU.bypass, replica_groups=[[0, 1], [2, 3], [4, 5], [6, 7]],
            ins=[in_t.ap().opt()], outs=[out_t.ap().opt()])
        self.cc_count += 1
        ins.then_inc(self.cc_sem, 1)
        self._commit(("cc", self.cc_sem, self.cc_count, None), [in_res], [out_res])


def build_program():
    nc = bass.Bass("TRN2", target_bir_lowering=False)

    def din(name, shape):
        return nc.dram_tensor(name, list(shape), F32, kind="ExternalInput").ap()

    def dout(name, shape):
        return nc.dram_tensor(name, list(shape), F32, kind="ExternalOutput").ap()

    I = {}
    for name, shape in [
        ("x_in", (3072, D)), ("cond2T", (128, 8, 2)), ("ctx_ckv", (2, 512, 128)), ("ctx_kr", (2, 512, 32)),
        ("s0_lead", (2, 4, 128, 128)), ("ldec", (1, 16)), ("ropecs", (2, 32, 2048)), ("pw", (128, 2)),
        ("w_ada", (4, D, 6 * D)), ("b_ada", (4, 6 * D)), ("b_adaT", (4, 128, 48)), ("norm_gT", (4, 2, 128, 8)),
        ("w_in_ab", (2, D, ABIN)), ("q_norm_gT", (2, 128, 2)), ("w_uq", (2, 256, 768)), ("kv_norm_g", (2, 128)),
        ("w_ukv", (2, 128, 1024)), ("w_o_ab", (2, D, D)), ("w_in_c", (2, D, 2 * D)), ("ln_g_c", (2, D)),
        ("ln_b_c", (2, D)), ("wsT", (2, 128, 8, 128)), ("b_sT", (2, 128, 8)), ("w_out_c", (2, D, D)),
        ("w_ffn_gate", (4, D, DFF)), ("w_ffn_up", (4, D, DFF)), ("w_ffn_down", (4, DFF, D)),
        ("final_norm_g", (D,)), ("consts", (128, 8, 128)), ("rmT", (32, 32)),
    ]:
        I[name] = din(name, shape)
    O = {
        "y": dout("y", (3072, D)), "o_ckv": dout("o_ckv", (2, 1024, 128)), "o_kr": dout("o_kr", (2, 1024, 32)),
        "o_state": dout("o_state", (4, 2, 2, 4, 128, 128)),
    }
    gscr = nc.dram_tensor("gscr", [4, 2, 2, D], F32)
    gscr_r = Res()
    cc1_in = nc.dram_tensor("cc1_in", [160, 2048], BF16)
    cc1_out = nc.dram_tensor("cc1_out", [320, 2048], BF16)
    cc2_in = nc.dram_tensor("cc2_in", [512, 128], F32)
    cc2_out = nc.dram_tensor("cc2_out", [1024, 128], F32)
    qscr = nc.dram_tensor("qscr", [8, 2, 96, 2048], BF16)
    rscr = nc.dram_tensor("rscr", [16, 128, 5, 4, 128], BF16)
    oscr = nc.dram_tensor("oscr", [16, 128, 512], F32)
    gscr2 = nc.dram_tensor("gscr2", [16, 128, 512], BF16)
    ascr = nc.dram_tensor("ascr", [16, 128, 512], F32)
    mscr = nc.dram_tensor("mscr", [8, 128, D], F32)

    with contextlib.ExitStack() as es:
        b = Builder(nc, es)
        PE, ACT, DVE, POOL, SP = b.PE, b.ACT, b.DVE, b.POOL, b.SP
        ps, psr = b.ps, b.psr

        x = b.sb("x", [128, NT, D], F32, True)
        xr = RL(NT)
        consts = b.sb("consts", [128, 8, 128], F32, True)
        cr = Res()
        modc = b.sb("modc", [128, 4, 2, 4, 8], F32, True)
        modr = RL(4)
        gb = b.sb("gb", [128, 2, D], F32, True)
        gbr = Res()
        ident = consts[:, 0, :]

        d0 = b.ds(True)
        b.dma(SP, consts[:], I["consts"][:, :, :], [], [cr], d0)
        xin = I["x_in"].rearrange("(t p) d -> p t d", p=128)
        xds = [b.ds(True) for _ in range(NG)]
        for g in range(NG):
            b.dma(SP, x[:, 4 * g:4 * g + 4, :], xin[:, 4 * g:4 * g + 4, :], [], xr[4 * g:4 * g + 4], xds[g])

        def cond_of(t):
            return 0 if t < 8 else 1

        def rstd_from_ss(ss, n, inv_n, rs_res):
            b.op(DVE, lambda e: e.tensor_scalar(out=ss[:, :n], in0=ss[:, :n], scalar1=inv_n, scalar2=EPS,
                                               op0=ALU.mult, op1=ALU.add), [rs_res], [rs_res])
            b.op(ACT, lambda e: e.activation(out=ss[:, :n], in_=ss[:, :n], func=AF.Sqrt), [rs_res], [rs_res])
            b.op(DVE, lambda e: e.reciprocal(out=ss[:, :n], in_=ss[:, :n]), [rs_res], [rs_res])

        def norm_group(t0, n, l, which, hT, hTres, xs4, xs4r, ss, ssr):
            c = cond_of(t0)
            for tt in range(n):
                t = t0 + tt
                b.op(ACT, lambda e: e.activation(out=xs4[:, tt, :], in_=x[:, t, :], func=AF.Square,
                                                 accum_out=ss[:, tt:tt + 1]), [xr[t]], [xs4r[tt], ssr])
            rstd_from_ss(ss, n, 1.0 / D, ssr)
            for tt in range(n):
                t = t0 + tt
                b.op(DVE, lambda e: e.tensor_scalar(out=xs4[:, tt, :], in0=x[:, t, :], scalar1=ss[:, tt:tt + 1],
                                                    scalar2=None, op0=ALU.mult), [xr[t], ssr], [xs4r[tt]])
            for kc in range(8):
                bk = b.bank()
                for tt in range(n):
                    b.op(PE, lambda e: e.transpose(out=ps[bk][:, tt * 128:(tt + 1) * 128],
                                                   in_=xs4[:, tt, kc * 128:(kc + 1) * 128], identity=ident),
                         [xs4r[tt], cr], [psr[bk]])
                b.op(ACT, lambda e: e.activation(out=hT[:, kc, :n * 128], in_=ps[bk][:, :n * 128], func=AF.Identity,
                                                 scale=modc[:, l, c, 2 * which + 1, kc:kc + 1],
                                                 bias=modc[:, l, c, 2 * which, kc:kc + 1]),
                     [psr[bk], modr[l]], [hTres])

        def resid_update(t, half, bk, gate, tmp, tmpr):
            c = cond_of(t)
            sl = slice(half * 512, (half + 1) * 512)
            b.op(DVE, lambda e: e.tensor_tensor(out=tmp[:], in0=ps[bk][:, :], in1=gb[:, c, sl], op=ALU.mult),
                 [psr[bk], gbr], [tmpr])
            b.op(DVE, lambda e: e.tensor_tensor(out=x[:, t, sl], in0=x[:, t, sl], in1=tmp[:], op=ALU.add),
                 [tmpr, xr[t]], [xr[t]])

        with b.phase():
            scT = b.sb("scT", [128, 8, 2], BF16)
            scTf = b.sb("scTf", [128, 8, 2], F32)
            scr = Res()
            wsl = [b.sb("wada", [128, 8, 512], BF16) for _ in range(3)]
            wslr = RL(3)
            wsd = [b.dsw() for _ in range(3)]
            badT = b.sb("badT", [128, 4, 48], F32)
            ngT = b.sb("ngT", [128, 4, 2, 8], F32)
            brow = b.sb("brow", [2, 4, 2, D], F32)
            grow = b.sb("grow", [2, 4, 2, D], F32)
            miscr = Res()
            growr = Res()
            dd = b.ds()
            b.dma(SP, scTf[:], I["cond2T"][:, :, :], [], [scr], dd)
            b.dma(SP, badT[:], I["b_adaT"].rearrange("l p c -> p l c"), [], [miscr], dd)
            b.dma(SP, ngT[:], I["norm_gT"].rearrange("l w p k -> p l w k"), [], [miscr], dd)
            for j in range(2):
                for gi in range(2):
                    b.dma(SP, brow[j:j + 1, :, gi, :], I["b_ada"][None, :, (2 + 3 * gi) * D:(3 + 3 * gi) * D], [], [miscr], dd)
            b.op(ACT, lambda e: e.activation(out=scT[:], in_=scTf[:], func=AF.Silu), [scr], [scr])
            ui = 0
            for l in range(DEPTH):
                wv = I["w_ada"][l].rearrange("(kc p) n -> p kc n", p=128)
                for cb in range(12):
                    s = ui % 3
                    ui += 1
                    b.dma(POOL, wsl[s][:], wv[:, :, cb * 512:(cb + 1) * 512], [], [wslr[s]], wsd[s])
                    v = cb // 2
                    if v in (2, 5):
                        bk = b.bank()
                        for kc in range(8):
                            b.op(PE, lambda e: e.matmul(ps[bk][0:2, :], lhsT=scT[:, kc, :], rhs=wsl[s][:, kc, :],
                                                        start=(kc == 0), stop=(kc == 7)), [scr, wslr[s]], [psr[bk]])
                        gi = 0 if v == 2 else 1
                        hs = (cb % 2) * 512
                        b.op(DVE, lambda e: e.tensor_tensor(out=grow[:, l, gi, hs:hs + 512], in0=ps[bk][0:2, :],
                                                            in1=brow[:, l, gi, hs:hs + 512], op=ALU.add),
                             [psr[bk], miscr], [growr])
                    else:
                        vi = {0: 0, 1: 1, 3: 2, 4: 3}[v]
                        bk = b.bank()
                        for q in range(4):
                            for kc in range(8):
                                b.op(PE, lambda e: e.matmul(ps[bk][:, 2 * q:2 * q + 2],
                                                            lhsT=wsl[s][:, kc, q * 128:(q + 1) * 128],
                                                            rhs=scT[:, kc, :], start=(kc == 0), stop=(kc == 7)),
                                     [scr, wslr[s]], [psr[bk]])
                        k0 = (cb % 2) * 4
                        for c in range(2):
                            b.op(DVE, lambda e: e.tensor_tensor(
                                out=modc[:, l, c, vi, k0:k0 + 4],
                                in0=ps[bk][:, 0:8].rearrange("p (q c) -> p q c", c=2)[:, :, c],
                                in1=badT[:, l, cb * 4:cb * 4 + 4], op=ALU.add), [psr[bk], miscr], [modr[l]])
                for c in range(2):
                    for w in range(2):
                        b.op(DVE, lambda e: e.scalar_tensor_tensor(
                            out=modc[:, l, c, 2 * w + 1, :], in0=modc[:, l, c, 2 * w + 1, :], scalar=1.0,
                            in1=ngT[:, l, w, :], op0=ALU.add, op1=ALU.mult), [modr[l], miscr], [modr[l]])
            for j in range(2):
                b.dma(SP, gscr[:, j, :, :], grow[j:j + 1, :, :, :], [growr], [gscr_r], dd)

        gbd = b.ds(True)

        def load_gates(l, gi):
            for c in range(2):
                b.dma(SP, gb[:, c, :], gscr[l, c, gi, :].partition_broadcast(128), [gscr_r], [gbr], gbd)

        def ffn_phase(l):
            with b.phase():
                hT = [b.sb("hT", [128, 8, 512], BF16) for _ in range(2)]
                hTr = RL(2)
                xs4 = b.sb("xs4", [128, 4, D], F32)
                xs4r = RL(4)
                junk = junkr = None
                ss = b.sb("ss", [128, 4], F32)
                ssr = Res()
                aT = b.sb("aT", [128, NF, 512], BF16)
                aTr = RL(NF)
                wgu = [b.sb("wgu", [128, 2, 8, 256], BF16) for _ in range(3)]
                wgr = [RL(2) for _ in range(3)]
                wgd = [[b.dsw(), b.dsw()] for _ in range(3)]
                wd = [b.sb("wd", [128, 2, 512], BF16) for _ in range(3)]
                wdr = RL(3)
                wdd = [b.dsw() for _ in range(3)]
                sg = [b.sb("sg", [128, 512], BF16) for _ in range(2)]
                sgr = RL(2)
                tmp = [b.sb("tmp", [128, 512], F32) for _ in range(2)]
                tmpr = RL(2)
                wgv = I["w_ffn_gate"][l].rearrange("(kc p) n -> p kc n", p=128)
                wuv = I["w_ffn_up"][l].rearrange("(kc p) n -> p kc n", p=128)
                wdv = I["w_ffn_down"][l].rearrange("(f p) n -> p f n", p=128)
                norm_group(0, 4, l, 1, hT[0], hTr[0], xs4, xs4r, ss, ssr)
                ug = 0
                ud = 0
                si = 0
                for g in range(NG):
                    h = hT[g % 2]
                    hr = hTr[g % 2]
                    for u in range(11):
                        s = ug % 3
                        ug += 1
                        b.dma(POOL, wgu[s][:, 0], wgv[:, :, u * 256:(u + 1) * 256], [], [wgr[s][0]], wgd[s][0])
                        b.dma(POOL, wgu[s][:, 1], wuv[:, :, u * 256:(u + 1) * 256], [], [wgr[s][1]], wgd[s][1])
                        for fi in range(2):
                            f = 2 * u + fi
                            bg = b.bank()
                            bu = b.bank()
                            for kc in range(8):
                                b.op(PE, lambda e: e.matmul(ps[bg][:, :], lhsT=wgu[s][:, 0, kc, fi * 128:(fi + 1) * 128],
                                                            rhs=h[:, kc, :], start=(kc == 0), stop=(kc == 7)),
                                     [wgr[s][0], hr], [psr[bg]])
                            for kc in range(8):
                                b.op(PE, lambda e: e.matmul(ps[bu][:, :], lhsT=wgu[s][:, 1, kc, fi * 128:(fi + 1) * 128],
                                                            rhs=h[:, kc, :], start=(kc == 0), stop=(kc == 7)),
                                     [wgr[s][1], hr], [psr[bu]])
                            k = si % 2
                            si += 1
                            b.op(ACT, lambda e: e.activation(out=sg[k][:], in_=ps[bg][:, :], func=AF.Silu),
                                 [psr[bg]], [sgr[k]])
                            b.op(DVE, lambda e: e.tensor_tensor(out=aT[:, f, :], in0=sg[k][:], in1=ps[bu][:, :],
                                                                op=ALU.mult), [sgr[k], psr[bu]], [aTr[f]])
                        if u == 5 and g + 1 < NG:
                            norm_group(4 * (g + 1), 4, l, 1, hT[(g + 1) % 2], hTr[(g + 1) % 2], xs4, xs4r, ss, ssr)
                    for half in range(2):
                        bks = [b.bank() for _ in range(4)]
                        for u in range(11):
                            s = ud % 3
                            ud += 1
                            b.dma(POOL, wd[s][:], wdv[:, 2 * u:2 * u + 2, half * 512:(half + 1) * 512], [], [wdr[s]], wdd[s])
                            for fi in range(2):
                                f = 2 * u + fi
                                for tt in range(4):
                                    b.op(PE, lambda e: e.matmul(ps[bks[tt]][:, :], lhsT=aT[:, f, tt * 128:(tt + 1) * 128],
                                                                rhs=wd[s][:, fi, :], start=(f == 0), stop=(f == NF - 1)),
                                         [aTr[f], wdr[s]], [psr[bks[tt]]])
                        for tt in range(4):
                            resid_update(4 * g + tt, half, bks[tt], 1, tmp[tt % 2], tmpr[tt % 2])

        def c_phase(l):
            j = l // 2
            NTC = 2
            W = NTC * 128
            NGC = NT // NTC
            with b.phase():
                hT = [b.sb("hT", [128, 8, W], BF16) for _ in range(2)]
                hTr = RL(2)
                ss = b.sb("ss", [128, 4], F32)
                ssr = Res()
                win = [b.sb("winc", [128, 8, 512], BF16) for _ in range(2)]
                winr = RL(2)
                wind = [b.dsw() for _ in range(2)]
                wout = b.sb("woutc", [128, 8, D], BF16)
                woutr = Res()
                wst = b.sb("wst", [128, 8, 128], BF16)
                bst = b.sb("bst", [128, 8], F32)
                lng = b.sb("lng", [128, D], F32)
                lnb = b.sb("lnb", [128, D], F32)
                cwr = Res()
                u_sb = [b.sb("u_sb", [128, NTC, D], BF16) for _ in range(2)]
                ur = [RL(NTC) for _ in range(2)]
                v_sb = [b.sb("v_sb", [128, NTC, D], F32) for _ in range(2)]
                vr = [RL(NTC) for _ in range(2)]
                vln = b.sb("vln", [128, NTC, D], BF16)
                vlr = RL(NTC)
                aT = b.sb("aTc", [128, 8, W], BF16)
                aTr = RL(8)
                st = b.sb("bnst", [128, 2, 6], F32)
                mv = b.sb("bnmv", [128, 2], F32)
                str_ = Res()
                tmp = [b.sb("tmp", [128, 512], F32) for _ in range(2)]
                tmpr = RL(2)
                dd = b.ds()
                b.dma(POOL, wout[:], I["w_out_c"][j].rearrange("(kc p) n -> p kc n", p=128), [], [woutr], b.dsw())
                b.dma(POOL, wst[:], I["wsT"][j], [], [cwr], b.dsw())
                b.dma(SP, bst[:], I["b_sT"][j], [], [cwr], dd)
                b.dma(SP, lng[:], I["ln_g_c"][j].partition_broadcast(128), [], [cwr], dd)
                b.dma(SP, lnb[:], I["ln_b_c"][j].partition_broadcast(128), [], [cwr], dd)
                wv = I["w_in_c"][j].rearrange("(kc p) n -> p kc n", p=128)
                ui = [0]

                def front(g):
                    k = g % 2
                    t0 = g * NTC
                    norm_group(t0, NTC, l, 0, hT[k], hTr[k], v_sb[k], vr[k], ss, ssr)
                    yield
                    for cb in range(4):
                        s = ui[0] % 2
                        ui[0] += 1
                        b.dma(POOL, win[s][:], wv[:, :, cb * 512:(cb + 1) * 512], [], [winr[s]], wind[s])
                        for tt in range(NTC):
                            bk = b.bank()
                            for kc in range(8):
                                b.op(PE, lambda e: e.matmul(ps[bk][:, :], lhsT=hT[k][:, kc, tt * 128:(tt + 1) * 128],
                                                            rhs=win[s][:, kc, :], start=(kc == 0), stop=(kc == 7)),
                                     [hTr[k], winr[s]], [psr[bk]])
                            if cb < 2:
                                b.op(ACT, lambda e: e.activation(out=u_sb[k][:, tt, cb * 512:(cb + 1) * 512], in_=ps[bk][:, :],
                                                                 func=AF.Gelu_apprx_tanh), [psr[bk]], [ur[k][tt]])
                            else:
                                b.op(ACT, lambda e: e.activation(out=v_sb[k][:, tt, (cb - 2) * 512:(cb - 1) * 512],
                                                                 in_=ps[bk][:, :], func=AF.Gelu_apprx_tanh),
                                     [psr[bk]], [vr[k][tt]])
                            yield

                def back(g):
                    k = g % 2
                    t0 = g * NTC
                    v = v_sb[k]
                    a_sb, ar = v_sb[k], vr[k]
                    for tt in range(NTC):
                        for hh in range(2):
                            b.op(DVE, lambda e: e.bn_stats(out=st[:, hh, :], in_=v[:, tt, hh * 512:(hh + 1) * 512]),
                                 [vr[k][tt]], [str_])
                        b.op(DVE, lambda e: e.bn_aggr(out=mv[:], in_=st[:].rearrange("p a s -> p (a s)")), [str_], [str_])
                        b.op(DVE, lambda e: e.tensor_scalar(out=mv[:, 1:2], in0=mv[:, 1:2], scalar1=EPS, scalar2=None,
                                                            op0=ALU.add), [str_], [str_])
                        b.op(ACT, lambda e: e.activation(out=mv[:, 1:2], in_=mv[:, 1:2], func=AF.Sqrt), [str_], [str_])
                        b.op(DVE, lambda e: e.reciprocal(out=mv[:, 1:2], in_=mv[:, 1:2]), [str_], [str_])
                        b.op(DVE, lambda e: e.tensor_scalar(out=v[:, tt, :], in0=v[:, tt, :], scalar1=mv[:, 0:1],
                                                            scalar2=mv[:, 1:2], op0=ALU.subtract, op1=ALU.mult),
                             [vr[k][tt], str_], [vr[k][tt]])
                        b.op(DVE, lambda e: e.tensor_tensor(out=v[:, tt, :], in0=v[:, tt, :], in1=lng[:], op=ALU.mult),
                             [vr[k][tt], cwr], [vr[k][tt]])
                        b.op(DVE, lambda e: e.tensor_tensor(out=vln[:, tt, :], in0=v[:, tt, :], in1=lnb[:], op=ALU.add),
                             [vr[k][tt], cwr], [vlr[tt]])
                        yield
                        bk2 = [b.bank(), b.bank()]
                        for gg in range(8):
                            bk = bk2[gg // 4]
                            b.op(PE, lambda e: e.matmul(ps[bk][:, (gg % 4) * 128:(gg % 4 + 1) * 128], lhsT=wst[:, gg, :],
                                                        rhs=vln[:, tt, gg * 128:(gg + 1) * 128], start=True, stop=True),
                                 [cwr, vlr[tt]], [psr[bk]])
                        yield
                        for gg in range(8):
                            bk = bk2[gg // 4]
                            b.op(DVE, lambda e: e.scalar_tensor_tensor(
                                out=a_sb[:, tt, gg * 128:(gg + 1) * 128], in0=ps[bk][:, (gg % 4) * 128:(gg % 4 + 1) * 128],
                                scalar=bst[:, gg:gg + 1], in1=u_sb[k][:, tt, gg * 128:(gg + 1) * 128],
                                op0=ALU.add, op1=ALU.mult), [psr[bk], cwr, ur[k][tt]], [ar[tt]])
                        yield
                    for kc in range(8):
                        bk = b.bank()
                        for tt in range(NTC):
                            b.op(PE, lambda e: e.transpose(out=ps[bk][:, tt * 128:(tt + 1) * 128],
                                                           in_=a_sb[:, tt, kc * 128:(kc + 1) * 128], identity=ident),
                                 [ar[tt], cr], [psr[bk]])
                        b.op(ACT, lambda e: e.copy(out=aT[:, kc, :], in_=ps[bk][:, :W]), [psr[bk]], [aTr[kc]])
                        if kc % 2 == 1:
                            yield
                    for tt in range(NTC):
                        for half in range(2):
                            bk = b.bank()
                            for kc in range(8):
                                b.op(PE, lambda e: e.matmul(ps[bk][:, :], lhsT=aT[:, kc, tt * 128:(tt + 1) * 128],
                                                            rhs=wout[:, kc, half * 512:(half + 1) * 512],
                                                            start=(kc == 0), stop=(kc == 7)), [aTr[kc], woutr], [psr[bk]])
                            resid_update(t0 + tt, half, bk, 0, tmp[half], tmpr[half])
                            yield

                def interleave(*gens):
                    gens = [g_ for g_ in gens if g_ is not None]
                    while gens:
                        for g_ in list(gens):
                            try:
                                next(g_)
                            except StopIteration:
                                gens.remove(g_)

                interleave(front(0))
                for g in range(NGC):
                    interleave(back(g), front(g + 1) if g + 1 < NGC else None)

        def final_phase():
            with b.phase():
                fg = b.sb("fg", [128, D], F32)
                fgr = Res()
                junk = b.sb("junk", [128, D], F32)
                junkr = Res()
                ss = b.sb("ssf", [128, NT], F32)
                ssr = Res()
                yo = [b.sb("yo", [128, D], F32) for _ in range(3)]
                yor = RL(3)
                yd = [b.ds() for _ in range(3)]
                b.dma(SP, fg[:], I["final_norm_g"].partition_broadcast(128), [], [fgr], b.ds())
                for t in range(NT):
                    b.op(ACT, lambda e: e.activation(out=junk[:], in_=x[:, t, :], func=AF.Square,
                                                     accum_out=ss[:, t:t + 1]), [xr[t]], [junkr, ssr])
                rstd_from_ss(ss, NT, 1.0 / D, ssr)
                yv = O["y"].rearrange("(t p) d -> p t d", p=128)
                for t in range(NT):
                    k = t % 3
                    b.op(DVE, lambda e: e.scalar_tensor_tensor(out=yo[k][:], in0=x[:, t, :], scalar=ss[:, t:t + 1],
                                                               in1=fg[:], op0=ALU.mult, op1=ALU.mult),
                         [xr[t], ssr, fgr], [yor[k]])
                    b.dma(SP, yv[:, t, :], yo[k][:], [yor[k]], [], yd[k])

        def ab_phase(l):
            j = l // 2
            ab_layer(b, I, O, j, l, x, xr, consts, cr, ident, norm_group, resid_update,
                     dict(cc1_in=cc1_in, cc1_out=cc1_out, cc2_in=cc2_in, cc2_out=cc2_out, qscr=qscr, rscr=rscr,
                          oscr=oscr, gscr2=gscr2, ascr=ascr, mscr=mscr))

        for l in range(int(os.environ.get("NLAYERS", DEPTH))):
            load_gates(l, 0)
            if l % 2 == 0:
                ab_phase(l)
            else:
                c_phase(l)
            load_gates(l, 1)
            ffn_phase(l)
        final_phase()
        b.barrier()
    return nc


def ab_layer(b, I, O, j, l, x, xr, consts, cr, ident, norm_group, resid_update, S):
    PE, ACT, DVE, POOL, SP = b.PE, b.ACT, b.DVE, b.POOL, b.SP
    ps, psr = b.ps, b.psr
    cc1_in, cc1_out, cc2_in, cc2_out = S["cc1_in"], S["cc1_out"], S["cc2_in"], S["cc2_out"]
    qscr, rscr, oscr, gscr2, ascr, mscr = S["qscr"], S["rscr"], S["oscr"], S["gscr2"], S["ascr"], S["mscr"]
    mscr_r = RL(8)
    RS = 128 ** -0.5
    STAGE = float(os.environ.get("ABDBG", "9"))
    if STAGE < 1:
        return
    cc1i_r, cc1o_r, cc2i_r, cc2o_r = Res(), Res(), Res(), Res()
    qscr_r, rscr_r, oscr_r, gscr2_r, ascr_r = Res(), RL(16), RL(16), RL(16), RL(16)

    def copy_alt(k, out, in_, reads, writes):
        if k % 2 == 0:
            b.op(ACT, lambda e: e.copy(out=out, in_=in_), reads, writes)
        else:
            b.op(DVE, lambda e: e.tensor_copy(out=out, in_=in_), reads, writes)

    with b.phase():
        W = 256
        hT = b.sb("hT", [128, 8, W], BF16); hTr = Res()
        xs4 = b.sb("xs4", [128, 2, D], F32); xs4r = RL(2)
        mix, mixr = xs4, xs4r
        ss = b.sb("ss", [128, 8], F32); ssr = Res()
        win = [b.sb("winab", [128, 8, 512], BF16) for _ in range(2)]; winr = RL(2); wind = [b.dsw() for _ in range(2)]
        wuq = b.sb("wuq", [128, 2, 8, 96], BF16)
        wkn = b.sb("wkn", [128, 8, 96], BF16)
        wv = b.sb("wv", [128, 8, 64], BF16)
        qng = b.sb("qng", [128, 2], F32)
        kvg = b.sb("kvg", [128, 128], F32)
        rm = b.sb("rm", [32, 32], BF16)
        rhi = b.sb("rhi", [32, W], BF16); rlo = b.sb("rlo", [32, W], BF16); rhr = Res()
        wr = Res()
        ldb = b.sb("ldb", [128, 16], F32)
        DT = b.sb("DT", [128, 2, 4, 128], BF16)
        QD = b.sb("QD", [128, 2, 4, 128], F32)
        KD = b.sb("KD", [128, 2, 4], F32)
        CDc = b.sb("CD", [128, 2, 4], F32)
        etmp = b.sb("etmp", [128, 128], F32)
        dr = Res()
        cqn = b.sb("cqn", [128, 256], F32); ckvn = b.sb("ckvn", [128, 2, 128], F32); krs = b.sb("krs", [128, 2, 32], F32)
        pr = RL(2)
        cqnT = b.sb("cqnT", [128, 2, W], BF16); ckvnT = b.sb("ckvnT", [128, W], BF16)
        krT = b.sb("krT", [32, W], BF16); krTf = b.sb("krTf", [32, W], F32)
        tr = Res()
        cs2 = b.sb("cs2", [32, 2, 2, W], F32); csr = Res(); csd = b.ds()
        wuqR = b.sb("wuqR", [128, 2, 8, 32], BF16)
        qT = b.sb("qT", [96, 8, W], BF16); qTr = Res()
        qrr = Res()
        rt1 = b.sb("rt1", [32, 2 * W], F32); rt2 = b.sb("rt2", [32, 2 * W], F32); rtr = Res()
        KT = b.sb("KT", [96, 8, W], BF16); KTr = Res()
        V = b.sb("V", [128, 2, 8, 65], BF16); Vr = Res()
        PT = [b.sb("PT", [128, W], BF16) for _ in range(2)]; PTr = RL(2)
        rc = b.sb("rc", [128, 2], F32); rcr = Res()
        rop = b.sb("rop", [128, 2, 5, 4, 128], BF16); ropr = RL(2)
        qdl = b.sb("qdl", [128, 2, 4, 128], BF16); kdl = b.sb("kdl", [128, 2, 4, 128], BF16); rlr = RL(2)
        sgt = b.sb("sgt", [128, 2, 512], BF16); sgr = RL(2)
        Sst = [b.sb("Sst", [128, 4, 128], F32) for _ in range(2)]; Sbf = [b.sb("Sbf", [128, 4, 128], BF16) for _ in range(2)]
        Sr = RL(2)
        AD = b.sb("AD", [128, 4, 128], BF16); ADr = Res()
        ol = b.sb("ol", [128, 2, 512], F32); olr = RL(2)
        osum = b.sb("osum", [128, 512], F32); osr = Res()
        dA = b.ds(); dB = b.ds(); dC = b.ds(); dD = b.ds(); dE = b.ds(); dF = b.ds(); dG = b.ds(); dH = b.ds(); dS = b.dsw()

        wuqv = I["w_uq"][j].rearrange("(kc p) (h c) -> p kc h c", p=128, c=96)
        for kcq in range(2):
            b.dma(POOL, wuq[:, kcq, :, 0:32], wuqv[:, kcq, :, 64:96], [], [wr], dS)
            b.dma(POOL, wuq[:, kcq, :, 32:96], wuqv[:, kcq, :, 0:64], [], [wr], dS)
        wukv = I["w_ukv"][j].rearrange("p (h c) -> p h c", c=128)
        b.op(DVE, lambda e: e.memset(wkn[:], 0.0), [], [wr])
        b.dma(POOL, wkn[:, :, 32:96], wukv[:, :, 0:64], [], [wr], dS)
        b.dma(POOL, wv[:], wukv[:, :, 64:128], [], [wr], dS)
        for kcq in range(2):
            for o_, i_, sgn in ((0, 8, -1.0), (8, 0, 1.0), (16, 24, -1.0), (24, 16, 1.0)):
                b.op(DVE, lambda e: e.tensor_scalar(out=wuqR[:, kcq, :, o_:o_ + 8], in0=wuq[:, kcq, :, i_:i_ + 8], scalar1=sgn,
                                                    scalar2=None, op0=ALU.mult), [wr], [wr])
        b.dma(SP, qng[:], I["q_norm_gT"][j], [], [wr], dB)
        b.dma(SP, kvg[:], I["kv_norm_g"][j].partition_broadcast(128), [], [wr], dB)
        b.dma(POOL, rm[:], I["rmT"][:, :], [], [wr], dS)
        b.dma(SP, ldb[:], I["ldec"][0].partition_broadcast(128), [], [dr], dB)
        b.op(ACT, lambda e: e.activation(out=ldb[:], in_=ldb[:], func=AF.Exp), [dr], [dr])
        b.op(DVE, lambda e: e.tensor_scalar(out=ldb[:], in0=ldb[:], scalar1=-1.0, scalar2=None, op0=ALU.mult), [dr], [dr])
        for d_ in range(2):
            for h in range(4):
                lg = ldb[:, j * 8 + d_ * 4 + h:j * 8 + d_ * 4 + h + 1]
                b.op(ACT, lambda e: e.activation(out=etmp[:], in_=consts[:, 1 + 2 * d_, :], func=AF.Exp, scale=lg), [dr, cr], [dr])
                b.op(DVE, lambda e: e.scalar_tensor_tensor(out=DT[:, d_, h, :], in0=etmp[:], scalar=RS, in1=consts[:, 2 + 2 * d_, :],
                                                           op0=ALU.mult, op1=ALU.mult), [dr, cr], [dr])
                b.op(ACT, lambda e: e.activation(out=QD[:, d_, h, :], in_=consts[:, 6 + d_, :], func=AF.Exp, scale=lg), [dr, cr], [dr])
                b.op(ACT, lambda e: e.activation(out=KD[:, d_, h:h + 1], in_=consts[:, 5, 1 + 2 * d_:2 + 2 * d_], func=AF.Exp, scale=lg),
                     [dr, cr], [dr])
                b.op(ACT, lambda e: e.activation(out=CDc[:, d_, h:h + 1], in_=consts[:, 5, 4:5], func=AF.Exp, scale=lg), [dr, cr], [dr])
        b.op(DVE, lambda e: e.tensor_scalar(out=KD[:], in0=KD[:], scalar1=RS, scalar2=None, op0=ALU.mult), [dr], [dr])
        b.op(DVE, lambda e: e.memset(V[:, :, :, 64:65], 1.0), [], [Vr])

        winv = I["w_in_ab"][j].rearrange("(kc p) n -> p kc n", p=128)
        blocks = [(0, 416), (416, 512), (928, 512), (1440, 512), (1952, 512)]
        ui = [0]

        def load_blk(bi):
            s_ = ui[0] % 2
            ui[0] += 1
            c0, n = blocks[bi]
            b.dma(POOL, win[s_][:, :, 0:n], winv[:, :, c0:c0 + n], [], [winr[s_]], wind[s_])
            return s_

        def ret_step(d_, si, qT_, qdT_, kT_, kd_, v_, rd):
            ba = b.bank()
            for h in range(4):
                b.op(PE, lambda e: e.matmul(ps[ba][:, h * 128:(h + 1) * 128], lhsT=kT_[:, h, :], rhs=qT_[:, h, :], start=True, stop=True),
                     rd, [psr[ba]])
            b.op(DVE, lambda e: e.tensor_tensor(out=AD[:].rearrange("p h i -> p (h i)"), in0=ps[ba][:, :],
                                                in1=DT[:, d_].rearrange("p h i -> p (h i)"), op=ALU.mult), [psr[ba], dr], [ADr])
            bo = b.bank()
            for h in range(4):
                b.op(PE, lambda e: e.matmul(ps[bo][:, h * 128:(h + 1) * 128], lhsT=AD[:, h, :], rhs=v_[:, h, :], start=True, stop=False),
                     rd + [ADr], [psr[bo]])
                b.op(PE, lambda e: e.matmul(ps[bo][:, h * 128:(h + 1) * 128], lhsT=qdT_[:, h, :], rhs=Sbf[si][:, h, :], start=False, stop=True),
                     rd + [Sr[si]], [psr[bo]])
            bs_ = b.bank()
            for h in range(4):
                b.op(PE, lambda e: e.matmul(ps[bs_][:, h * 128:(h + 1) * 128], lhsT=kd_[:, h, :], rhs=v_[:, h, :], start=True, stop=True),
                     rd, [psr[bs_]])
            for h in range(4):
                b.op(DVE, lambda e: e.scalar_tensor_tensor(out=Sst[si][:, h, :], in0=Sst[si][:, h, :], scalar=CDc[:, d_, h:h + 1],
                                                           in1=ps[bs_][:, h * 128:(h + 1) * 128], op0=ALU.mult, op1=ALU.add),
                     [psr[bs_], dr, Sr[si]], [Sr[si]])
            b.op(ACT, lambda e: e.copy(out=Sbf[si][:], in_=Sst[si][:]), [Sr[si]], [Sr[si]])
            return bo

        def merge_tile(tt, bo_trail, o_lead_ap, o_lead_res, sg_ap, sg_res, mix_t, mix_res):
            b.op(DVE, lambda e: e.tensor_tensor(out=osum[:], in0=ps[bo_trail][:, :], in1=o_lead_ap, op=ALU.add),
                 [psr[bo_trail], o_lead_res], [osr])
            for h in range(4):
                b.op(ACT, lambda e: e.activation(out=etmp[:], in_=osum[:, h * 128:(h + 1) * 128], func=AF.Square,
                                                 accum_out=ss[:, 4 + h:5 + h]), [osr], [dr, ssr])
            b.op(DVE, lambda e: e.tensor_scalar(out=ss[:, 4:8], in0=ss[:, 4:8], scalar1=1.0 / 128, scalar2=EPS, op0=ALU.mult, op1=ALU.add),
                 [ssr], [ssr])
            b.op(ACT, lambda e: e.activation(out=ss[:, 4:8], in_=ss[:, 4:8], func=AF.Sqrt), [ssr], [ssr])
            b.op(DVE, lambda e: e.reciprocal(out=ss[:, 4:8], in_=ss[:, 4:8]), [ssr], [ssr])
            for h in range(4):
                b.op(DVE, lambda e: e.scalar_tensor_tensor(out=mix_t[:, 512 + h * 128:512 + (h + 1) * 128], in0=osum[:, h * 128:(h + 1) * 128],
                                                           scalar=ss[:, 4 + h:5 + h], in1=sg_ap[:, h * 128:(h + 1) * 128],
                                                           op0=ALU.mult, op1=ALU.mult), [osr, ssr, sg_res], [mix_res])

        S["ret_step_defs"] = None
        b.dma(SP, Sst[1][:], I["s0_lead"][j].rearrange("h d e -> d h e"), [], [Sr[1]], dC)
        b.op(ACT, lambda e: e.copy(out=Sbf[1][:], in_=Sst[1][:]), [Sr[1]], [Sr[1]])

        for g in range(12):
            t0 = 2 * g
            prompt = g < 4
            norm_group(t0, 2, l, 0, hT, hTr, xs4, xs4r, ss, ssr)
            s_ = load_blk(0)
            bT = b.bank()
            bC = b.bank()
            for tt in range(2):
                t = t0 + tt
                bk = b.bank()
                for kc in range(8):
                    b.op(PE, lambda e: e.matmul(ps[bk][:, 0:416], lhsT=hT[:, kc, tt * 128:(tt + 1) * 128], rhs=win[s_][:, kc, 0:416],
                                                start=(kc == 0), stop=(kc == 7)), [hTr, winr[s_]], [psr[bk]])
                b.op(ACT, lambda e: e.activation(out=cqn[:], in_=ps[bk][:, 0:256], func=AF.Square, accum_out=ss[:, 0:1]), [psr[bk]], [pr[0], ssr])
                b.op(ACT, lambda e: e.activation(out=ckvn[:, tt, :], in_=ps[bk][:, 256:384], func=AF.Square, accum_out=ss[:, 1:2]),
                     [psr[bk]], [pr[1], ssr])
                b.op(DVE, lambda e: e.tensor_scalar(out=ss[:, 0:1], in0=ss[:, 0:1], scalar1=1.0 / 256, scalar2=EPS, op0=ALU.mult, op1=ALU.add),
                     [ssr], [ssr])
                b.op(DVE, lambda e: e.tensor_scalar(out=ss[:, 1:2], in0=ss[:, 1:2], scalar1=1.0 / 128, scalar2=EPS, op0=ALU.mult, op1=ALU.add),
                     [ssr], [ssr])
                b.op(ACT, lambda e: e.activation(out=ss[:, 0:2], in_=ss[:, 0:2], func=AF.Sqrt), [ssr], [ssr])
                b.op(DVE, lambda e: e.reciprocal(out=ss[:, 0:2], in_=ss[:, 0:2]), [ssr], [ssr])
                b.op(DVE, lambda e: e.tensor_scalar(out=cqn[:], in0=ps[bk][:, 0:256], scalar1=ss[:, 0:1], scalar2=None, op0=ALU.mult),
                     [psr[bk], ssr], [pr[0]])
                b.op(DVE, lambda e: e.scalar_tensor_tensor(out=ckvn[:, tt, :], in0=ps[bk][:, 256:384], scalar=ss[:, 1:2], in1=kvg[:],
                                                           op0=ALU.mult, op1=ALU.mult), [psr[bk], ssr, wr], [pr[1]])
                b.op(ACT, lambda e: e.copy(out=krs[:, tt, :], in_=ps[bk][:, 384:416]), [psr[bk]], [pr[1]])
                for kcq in range(2):
                    b.op(PE, lambda e: e.transpose(out=ps[bT][:, kcq * W + tt * 128:kcq * W + (tt + 1) * 128],
                                                   in_=cqn[:, kcq * 128:(kcq + 1) * 128], identity=ident), [pr[0], cr], [psr[bT]])
                b.op(PE, lambda e: e.transpose(out=ps[bC][:, tt * 128:(tt + 1) * 128], in_=ckvn[:, tt, :], identity=ident),
                     [pr[1], cr], [psr[bC]])
                b.op(PE, lambda e: e.transpose(out=ps[bC][0:32, W + tt * 128:W + (tt + 1) * 128], in_=krs[:, tt, :], identity=ident),
                     [pr[1], cr], [psr[bC]])
            if prompt:
                b.dma(SP, O["o_ckv"][j, t0 * 128:(t0 + 2) * 128, :].rearrange("(t p) c -> p t c", p=128), ckvn[:], [pr[1]], [], dD)
                b.dma(SP, O["o_kr"][j, t0 * 128:(t0 + 2) * 128, :].rearrange("(t p) c -> p t c", p=128), krs[:], [pr[1]], [], dE)
            for kcq in range(2):
                b.op(ACT, lambda e: e.activation(out=cqnT[:, kcq, :], in_=ps[bT][:, kcq * W:(kcq + 1) * W], func=AF.Identity,
                                                 scale=qng[:, kcq:kcq + 1]), [psr[bT], wr], [tr])
            b.op(DVE, lambda e: e.tensor_copy(out=ckvnT[:], in_=ps[bC][:, 0:W]), [psr[bC]], [tr])
            b.op(DVE, lambda e: e.tensor_copy(out=krTf[:], in_=ps[bC][0:32, W:2 * W]), [psr[bC]], [tr])
            if STAGE < 2:
                continue
            if prompt:
                b.op(ACT, lambda e: e.copy(out=krT[:], in_=ps[bC][0:32, W:2 * W]), [psr[bC]], [tr])
            else:
                tok0 = (t0 - 8) * 128
                for rep_ in range(2):
                    b.dma(SP, cs2[:, :, rep_, :], I["ropecs"][:, :, tok0:tok0 + W].rearrange("a p w -> p a w"), [], [csr], csd)
                bk = b.bank()
                b.op(ACT, lambda e: e.copy(out=rhi[:], in_=krTf[:]), [tr], [rhr])
                b.op(DVE, lambda e: e.tensor_tensor(out=rlo[:], in0=krTf[:], in1=rhi[:], op=ALU.subtract), [tr, rhr], [rhr])
                b.op(PE, lambda e: e.matmul(ps[bk][0:32, 0:W], lhsT=rm[:], rhs=rhi[:], start=True, stop=False), [wr, rhr], [psr[bk]])
                b.op(PE, lambda e: e.matmul(ps[bk][0:32, 0:W], lhsT=rm[:], rhs=rlo[:], start=False, stop=True), [wr, rhr], [psr[bk]])
                b.op(DVE, lambda e: e.tensor_tensor(out=rt1[:, 0:W], in0=ps[bk][0:32, 0:W], in1=cs2[:, 1, 0, :], op=ALU.mult), [psr[bk], csr], [rtr])
                b.op(DVE, lambda e: e.tensor_tensor(out=rt2[:, 0:W], in0=krTf[:], in1=cs2[:, 0, 0, :], op=ALU.mult), [tr, csr], [rtr])
                b.op(DVE, lambda e: e.tensor_tensor(out=krT[:], in0=rt1[:, 0:W], in1=rt2[:, 0:W], op=ALU.add), [rtr], [tr])
                b.dma(SP, cc1_in[0:128, tok0:tok0 + W], ckvnT[:], [tr], [cc1i_r], dD)
                b.dma(SP, cc1_in[128:160, tok0:tok0 + W], krT[:], [tr], [cc1i_r], dE)
            if STAGE < 2.2:
                continue
            for hp in range(4):
                bk = b.bank()
                for hh in range(2):
                    h = 2 * hp + hh
                    for kcq in range(2):
                        b.op(PE, lambda e: e.matmul(ps[bk][0:96, hh * W:(hh + 1) * W], lhsT=wuq[:, kcq, h, :], rhs=cqnT[:, kcq, :],
                                                    start=(kcq == 0), stop=(kcq == 1)), [wr, tr], [psr[bk]])
                copy_alt(hp, qT[:, 2 * hp:2 * hp + 2, :], ps[bk][0:96, :].rearrange("p (h w) -> p h w", w=W), [psr[bk]], [qTr])
                if not prompt and STAGE >= 2.4:
                    bk2 = b.bank()
                    for hh in range(2):
                        h = 2 * hp + hh
                        for kcq in range(2):
                            b.op(PE, lambda e: e.matmul(ps[bk2][0:32, hh * W:(hh + 1) * W], lhsT=wuqR[:, kcq, h, :], rhs=cqnT[:, kcq, :],
                                                        start=(kcq == 0), stop=(kcq == 1)), [wr, tr], [psr[bk2]])
                    b.op(DVE, lambda e: e.tensor_tensor(out=rt1[:], in0=ps[bk2][0:32, :], in1=cs2[:, 1].rearrange("p r w -> p (r w)"),
                                                        op=ALU.mult), [psr[bk2], csr], [rtr])
                    b.op(DVE, lambda e: e.tensor_tensor(out=rt2[:], in0=ps[bk][0:32, :], in1=cs2[:, 0].rearrange("p r w -> p (r w)"),
                                                        op=ALU.mult), [psr[bk], csr], [rtr])
                    b.op(DVE, lambda e: e.tensor_tensor(out=KT[0:32, 2 * hp:2 * hp + 2, :].rearrange("p h w -> p (h w)"), in0=rt1[:], in1=rt2[:],
                                                        op=ALU.add), [rtr], [KTr])
            if not prompt and STAGE >= 2.6:
                b.dma(SP, qscr[:, 0, :, tok0:tok0 + W].rearrange("h r w -> r h w"), qT[:], [qTr], [qscr_r], dF)
                b.dma(SP, qscr[:, 1, 32:96, tok0:tok0 + W].rearrange("h r w -> r h w"), qT[32:96, :, :], [qTr], [qscr_r], dG)
                b.dma(SP, qscr[:, 1, 0:32, tok0:tok0 + W].rearrange("h r w -> r h w"), KT[0:32, :, :], [KTr], [qscr_r], dA)
            if STAGE < 3:
                continue
            s_ = load_blk(1)
            for hp in range(2):
                bk = b.bank()
                for hh in range(2):
                    h = 2 * hp + hh
                    for kc in range(8):
                        b.op(PE, lambda e: e.matmul(ps[bk][:, hh * W:(hh + 1) * W], lhsT=win[s_][:, kc, h * 128:(h + 1) * 128], rhs=hT[:, kc, :],
                                                    start=(kc == 0), stop=(kc == 7)), [winr[s_], hTr], [psr[bk]])
                for hh in range(2):
                    h = 2 * hp + hh
                    for tt in range(2):
                        src = ps[bk][:, hh * W + tt * 128:hh * W + (tt + 1) * 128]
                        b.op(ACT, lambda e: e.copy(out=rop[:, tt, 0, h, :], in_=src), [psr[bk]], [ropr[tt]])
                        b.op(DVE, lambda e: e.tensor_tensor(out=rop[:, tt, 1, h, :], in0=src, in1=QD[:, 1, h, :], op=ALU.mult),
                             [psr[bk], dr], [ropr[tt]])
                        b.op(DVE, lambda e: e.tensor_tensor(out=qdl[:, tt, h, :], in0=src, in1=QD[:, 0, h, :], op=ALU.mult),
                             [psr[bk], dr], [rlr[tt]])
            s_ = load_blk(2)
            for hp in range(2):
                bk = b.bank()
                for hh in range(2):
                    h = 2 * hp + hh
                    for kc in range(8):
                        b.op(PE, lambda e: e.matmul(ps[bk][:, hh * W:(hh + 1) * W], lhsT=win[s_][:, kc, h * 128:(h + 1) * 128], rhs=hT[:, kc, :],
                                                    start=(kc == 0), stop=(kc == 7)), [winr[s_], hTr], [psr[bk]])
                for tt in range(2):
                    copy_alt(tt, rop[:, tt, 2, 2 * hp:2 * hp + 2, :],
                             ps[bk][:, :].rearrange("p (h t i) -> p h t i", h=2, t=2)[:, :, tt, :], [psr[bk]], [ropr[tt]])
            for tt in range(2):
                bk = b.bank()
                for kc in range(8):
                    b.op(PE, lambda e: e.matmul(ps[bk][:, :], lhsT=hT[:, kc, tt * 128:(tt + 1) * 128], rhs=win[s_][:, kc, :],
                                                start=(kc == 0), stop=(kc == 7)), [winr[s_], hTr], [psr[bk]])
                for h in range(4):
                    b.op(ACT, lambda e: e.activation(out=rop[:, tt, 3, h, :], in_=ps[bk][:, h * 128:(h + 1) * 128], func=AF.Identity,
                                                     scale=KD[:, 1, h:h + 1]), [psr[bk], dr], [ropr[tt]])
                    b.op(ACT, lambda e: e.activation(out=kdl[:, tt, h, :], in_=ps[bk][:, h * 128:(h + 1) * 128], func=AF.Identity,
                                                     scale=KD[:, 0, h:h + 1]), [psr[bk], dr], [rlr[tt]])
            s_ = load_blk(3)
            for tt in range(2):
                bk = b.bank()
                for kc in range(8):
                    b.op(PE, lambda e: e.matmul(ps[bk][:, :], lhsT=hT[:, kc, tt * 128:(tt + 1) * 128], rhs=win[s_][:, kc, :],
                                                start=(kc == 0), stop=(kc == 7)), [winr[s_], hTr], [psr[bk]])
                copy_alt(tt, rop[:, tt, 4, :, :].rearrange("p h e -> p (h e)"), ps[bk][:, :], [psr[bk]], [ropr[tt]])
            s_ = load_blk(4)
            for tt in range(2):
                bk = b.bank()
                for kc in range(8):
                    b.op(PE, lambda e: e.matmul(ps[bk][:, :], lhsT=hT[:, kc, tt * 128:(tt + 1) * 128], rhs=win[s_][:, kc, :],
                                                start=(kc == 0), stop=(kc == 7)), [winr[s_], hTr], [psr[bk]])
                b.op(ACT, lambda e: e.activation(out=sgt[:, tt, :], in_=ps[bk][:, :], func=AF.Silu), [psr[bk]], [sgr[tt]])

            if STAGE < 4:
                continue
            if prompt:
                seq = g
                for hp in range(4):
                    bk = b.bank()
                    for hh in range(2):
                        h = 2 * hp + hh
                        b.op(PE, lambda e: e.matmul(ps[bk][0:96, hh * W:(hh + 1) * W], lhsT=wkn[:, h, :], rhs=ckvnT[:], start=True, stop=True),
                             [wr, tr], [psr[bk]])
                    for r0 in (32, 64):
                        copy_alt(hp + r0 // 32, KT[r0:r0 + 32, 2 * hp:2 * hp + 2, :], ps[bk][r0:r0 + 32, :].rearrange("p (h w) -> p h w", w=W),
                                 [psr[bk]], [KTr])
                for h in range(8):
                    copy_alt(h, KT[0:32, h, :], krT[:], [tr], [KTr])
                for kt in range(2):
                    bk = b.bank()
                    b.op(PE, lambda e: e.matmul(ps[bk][:, :], lhsT=ckvnT[:, kt * 128:(kt + 1) * 128], rhs=wv[:].rearrange("p h c -> p (h c)"),
                                                start=True, stop=True), [wr, tr], [psr[bk]])
                    copy_alt(kt, V[:, kt, :, 0:64], ps[bk][:, :].rearrange("p (h c) -> p h c", c=64), [psr[bk]], [Vr])
                pi = 0
                for h in range(8):
                    bo = b.bank()
                    for kt in range(2):
                        bk = b.bank()
                        b.op(PE, lambda e: e.matmul(ps[bk][:, 0:W], lhsT=KT[:, h, kt * 128:(kt + 1) * 128], rhs=qT[:, h, :], start=True, stop=True),
                             [KTr, qTr], [psr[bk]])
                        k = pi % 2
                        pi += 1
                        b.op(ACT, lambda e: e.activation(out=PT[k][:], in_=ps[bk][:, 0:W], func=AF.Exp, scale=SCALE), [psr[bk]], [PTr[k]])
                        for qi in range(2):
                            b.op(PE, lambda e: e.matmul(ps[bo][:, qi * 65:(qi + 1) * 65], lhsT=PT[k][:, qi * 128:(qi + 1) * 128], rhs=V[:, kt, h, :],
                                                        start=(kt == 0 and qi == 0), stop=(kt == 1), skip_group_check=True),
                                 [PTr[k], Vr], [psr[bo]])
                    b.op(DVE, lambda e: e.reciprocal(out=rc[:], in_=ps[bo][:, 0:130].rearrange("p (q c) -> p q c", c=65)[:, :, 64]),
                         [psr[bo]], [rcr])
                    for qi in range(2):
                        b.op(ACT, lambda e: e.activation(out=mix[:, qi, h * 64:(h + 1) * 64], in_=ps[bo][:, qi * 65:qi * 65 + 64], func=AF.Identity,
                                                         scale=rc[:, qi:qi + 1]), [psr[bo], rcr], [mixr[qi]])
                if STAGE < 5:
                    continue
                b.op(DVE, lambda e: e.memset(Sst[0][:], 0.0), [], [Sr[0]])
                b.op(DVE, lambda e: e.memset(Sbf[0][:], 0.0), [], [Sr[0]])
                for tt in range(2):
                    bo = ret_step(0, 0, rop[:, tt, 0], qdl[:, tt], rop[:, tt, 2], kdl[:, tt], rop[:, tt, 4], [ropr[tt], rlr[tt]])
                    b.op(ACT, lambda e: e.copy(out=ol[:, tt, :], in_=ps[bo][:, :]), [psr[bo]], [olr[tt]])
                b.dma(SP, O["o_state"][seq, j, 0].rearrange("h d e -> d h e"), Sst[0][:], [Sr[0]], [], dC)
                b.op(DVE, lambda e: e.memset(Sst[0][:], 0.0), [], [Sr[0]])
                b.op(DVE, lambda e: e.memset(Sbf[0][:], 0.0), [], [Sr[0]])
                for tt in (1, 0):
                    bo = ret_step(1, 0, rop[:, tt, 0], rop[:, tt, 1], rop[:, tt, 2], rop[:, tt, 3], rop[:, tt, 4], [ropr[tt]])
                    merge_tile(tt, bo, ol[:, tt, :], olr[tt], sgt[:, tt, :], sgr[tt], mix[:, tt, :], mixr[tt])
                b.dma(SP, O["o_state"][seq, j, 1].rearrange("h d e -> d h e"), Sst[0][:], [Sr[0]], [], dC)
                for tt in range(2):
                    b.dma(SP, mscr[t0 + tt], mix[:, tt, :], [mixr[tt]], [mscr_r[t0 + tt]], dH)
            elif STAGE >= 5:
                for tt in range(2):
                    ti = t0 - 8 + tt
                    bo = ret_step(0, 1, rop[:, tt, 0], qdl[:, tt], rop[:, tt, 2], kdl[:, tt], rop[:, tt, 4], [ropr[tt], rlr[tt]])
                    b.op(ACT, lambda e: e.copy(out=ol[:, tt, :], in_=ps[bo][:, :]), [psr[bo]], [olr[tt]])
                    b.dma(SP, oscr[ti], ol[:, tt, :], [olr[tt]], [oscr_r[ti]], dB)
                    b.dma(SP, rscr[ti], rop[:, tt], [ropr[tt]], [rscr_r[ti]], dC)
                    b.dma(SP, gscr2[ti], sgt[:, tt, :], [sgr[tt]], [gscr2_r[ti]], csd)
        b.dma(SP, cc2_in.ap().rearrange("(h d) e -> d h e", d=128), Sst[1][:], [Sr[1]], [cc2i_r], dC)
        if STAGE >= 6:
            b.allgather(cc1_in, cc1_out, cc1i_r, cc1o_r)
            b.allgather(cc2_in, cc2_out, cc2i_r, cc2o_r)
    if STAGE < 7:
        return

    with b.phase():
        NK = 36
        cT = b.sb("cT", [128, NK * 128], BF16); kR = b.sb("kR", [32, NK * 128], BF16); cTr = Res()
        ctxf = b.sb("ctxf", [128, 4, 160], F32); ctxr = Res()
        wkn = b.sb("wkn", [128, 8, 96], BF16); wv = b.sb("wv", [128, 8, 64], BF16); wr = Res()
        KT = [b.sb("KT", [96, NK * 128], BF16) for _ in range(2)]; KTr = RL(2)
        V = [b.sb("V", [128, NK, 65], BF16) for _ in range(2)]; Vr = RL(2)
        qh = [b.sb("qh", [96, 2, 512], BF16) for _ in range(2)]; qhr = RL(2); qhd = [b.ds() for _ in range(2)]
        PT = [b.sb("PT", [128, 512], BF16) for _ in range(4)]; PTr = RL(4)
        rc = b.sb("rc", [128, 4], F32); rcr = Res()
        ao = [b.sb("ao", [128, 4, 64], F32) for _ in range(2)]; aor = RL(2); aod = [b.ds() for _ in range(2)]
        oT = [b.sb("oT", [65, 512], F32) for _ in range(2)]; oTr = RL(2)
        dA = b.ds(); dB = b.ds(); dS = b.dsw()
        wukv = I["w_ukv"][j].rearrange("p (h c) -> p h c", c=128)
        b.op(DVE, lambda e: e.memset(wkn[:], 0.0), [], [wr])
        b.dma(POOL, wkn[:, :, 32:96], wukv[:, :, 0:64], [], [wr], dS)
        b.dma(POOL, wv[:], wukv[:, :, 64:128], [], [wr], dS)
        b.dma(SP, ctxf[:, :, 0:128], I["ctx_ckv"][j].rearrange("(t p) c -> p t c", p=128), [], [ctxr], dB)
        b.dma(SP, ctxf[:, :, 128:160], I["ctx_kr"][j].rearrange("(t p) c -> p t c", p=128), [], [ctxr], dB)
        bk = b.bank(); bk2 = b.bank()
        for kt in range(4):
            b.op(PE, lambda e: e.transpose(out=ps[bk][:, kt * 128:(kt + 1) * 128], in_=ctxf[:, kt, 0:128], identity=ident), [ctxr, cr], [psr[bk]])
            b.op(PE, lambda e: e.transpose(out=ps[bk2][0:32, kt * 128:(kt + 1) * 128], in_=ctxf[:, kt, 128:160], identity=ident),
                 [ctxr, cr], [psr[bk2]])
        b.op(ACT, lambda e: e.copy(out=cT[:, 0:512], in_=ps[bk][:, :]), [psr[bk]], [cTr])
        b.op(DVE, lambda e: e.tensor_copy(out=kR[:, 0:512], in_=ps[bk2][0:32, :]), [psr[bk2]], [cTr])
        for r_ in range(2):
            b.dma(SP, cT[:, 512 + r_ * 2048:512 + (r_ + 1) * 2048], cc1_out[r_ * 160:r_ * 160 + 128, :], [cc1o_r], [cTr], dA)
            b.dma(SP, kR[:, 512 + r_ * 2048:512 + (r_ + 1) * 2048], cc1_out[r_ * 160 + 128:r_ * 160 + 160, :], [cc1o_r], [cTr], dB)
        for s_ in range(2):
            b.op(DVE, lambda e: e.memset(V[s_][:, :, 64:65], 1.0), [], [Vr[s_]])
        sbanks = [0, 1, 2, 3, 4, 5]
        sbi = [0]

        def sbank():
            i = sbanks[sbi[0] % 6]
            sbi[0] += 1
            return i
        obanks = [6, 7]
        oi = 0
        pi = 0
        qi_ = 0
        def build_kv(h):
            s_ = h % 2
            for kb in range(9):
                bk = sbank()
                b.op(PE, lambda e: e.matmul(ps[bk][0:96, :], lhsT=wkn[:, h, :], rhs=cT[:, kb * 512:(kb + 1) * 512], start=True, stop=True),
                     [wr, cTr], [psr[bk]])
                for r0 in (32, 64):
                    b.op(DVE, lambda e: e.tensor_copy(out=KT[s_][r0:r0 + 32, kb * 512:(kb + 1) * 512], in_=ps[bk][r0:r0 + 32, :]),
                         [psr[bk]], [KTr[s_]])
            b.op(DVE, lambda e: e.tensor_copy(out=KT[s_][0:32, :], in_=kR[:]), [cTr], [KTr[s_]])
            for k8 in range(5):
                bk = sbank()
                nkt = min(8, NK - 8 * k8)
                for q in range(nkt):
                    kt = 8 * k8 + q
                    b.op(PE, lambda e: e.matmul(ps[bk][:, q * 64:(q + 1) * 64], lhsT=cT[:, kt * 128:(kt + 1) * 128], rhs=wv[:, h, :],
                                                start=True, stop=True), [wr, cTr], [psr[bk]])
                b.op(DVE, lambda e: e.tensor_copy(out=V[s_][:, 8 * k8:8 * k8 + nkt, 0:64],
                                                  in_=ps[bk][:, 0:nkt * 64].rearrange("p (q c) -> p q c", c=64)), [psr[bk]], [Vr[s_]])

        pending = []

        def make_epilogue(bo_, a_, h_, qb_):
            def run():
                b.op(DVE, lambda e: e.tensor_copy(out=oT[a_][:], in_=ps[bo_][0:65, :]), [psr[bo_]], [oTr[a_]])
                bt = sbank()
                for qi in range(4):
                    b.op(PE, lambda e: e.transpose(out=ps[bt][:, qi * 65:(qi + 1) * 65], in_=oT[a_][:, qi * 128:(qi + 1) * 128],
                                                   identity=ident[0:65, 0:65]), [oTr[a_], cr], [psr[bt]])
                b.op(DVE, lambda e: e.reciprocal(out=rc[:], in_=ps[bt][:, 0:260].rearrange("p (q c) -> p q c", c=65)[:, :, 64]),
                     [psr[bt]], [rcr])
                for qi in range(4):
                    b.op(DVE, lambda e: e.tensor_scalar(out=ao[a_][:, qi, :], in0=ps[bt][:, qi * 65:qi * 65 + 64], scalar1=rc[:, qi:qi + 1],
                                                        scalar2=None, op0=ALU.mult), [psr[bt], rcr], [aor[a_]])
                b.dma(SP, ascr[qb_ * 4:qb_ * 4 + 4, :, h_ * 64:(h_ + 1) * 64].rearrange("t p c -> p t c"), ao[a_][:], [aor[a_]],
                      ascr_r[qb_ * 4:qb_ * 4 + 4], aod[a_])
            return run

        build_kv(0)
        for h in range(8):
            s_ = h % 2
            for qb in range(4):
                qs = qi_ % 2
                qi_ += 1
                b.dma(SP, qh[qs][:], qscr[h, :, :, qb * 512:(qb + 1) * 512].rearrange("v r w -> r v w"), [qscr_r], [qhr[qs]], qhd[qs])
                bo = obanks[oi % 2]
                oi += 1
                if qb == 1 and h + 1 < 8:
                    build_kv(h + 1)
                pk = [None] * NK
                SKEW = 2
                for kt in range(NK + SKEW):
                    if kt < NK:
                        ver = 0 if kt < 4 else 1
                        bk = sbank()
                        b.op(PE, lambda e: e.matmul(ps[bk][:, :], lhsT=KT[s_][:, kt * 128:(kt + 1) * 128], rhs=qh[qs][:, ver, :],
                                                    start=True, stop=True), [KTr[s_], qhr[qs]], [psr[bk]])
                        k = pi % 4
                        pi += 1
                        pk[kt] = k
                        b.op(ACT, lambda e: e.activation(out=PT[k][:], in_=ps[bk][:, :], func=AF.Exp, scale=SCALE), [psr[bk]], [PTr[k]])
                    if kt == SKEW and pending:
                        pending.pop(0)()
                    if kt >= SKEW:
                        kp = kt - SKEW
                        k = pk[kp]
                        b.op(PE, lambda e: e.matmul(ps[bo][0:65, :], lhsT=V[s_][:, kp, :], rhs=PT[k][:, :],
                                                    start=(kp == 0), stop=(kp == NK - 1)), [PTr[k], Vr[s_]], [psr[bo]])
                pending.append(make_epilogue(bo, (oi - 1) % 2, h, qb))
        for ep in pending:
            ep()
        pending.clear()

    if STAGE < 8:
        return
    with b.phase():
        ss = b.sb("ss", [128, 8], F32); ssr = Res()
        ldb = b.sb("ldb", [128, 16], F32)
        DT = b.sb("DT", [128, 2, 4, 128], BF16)
        CDc = b.sb("CD", [128, 2, 4], F32)
        etmp = b.sb("etmp", [128, 128], F32)
        dr = Res()
        wo = b.sb("wo", [128, 8, D], BF16); wor = Res()
        rop = [b.sb("rop", [128, 5, 4, 128], BF16) for _ in range(2)]; ropr = RL(2); ropd = [b.ds() for _ in range(2)]
        ol = [b.sb("ol", [128, 512], F32) for _ in range(2)]; olr = RL(2); old = [b.ds() for _ in range(2)]
        sgt = [b.sb("sgt", [128, 512], BF16) for _ in range(2)]; sgr = RL(2); sgd = [b.ds() for _ in range(2)]
        mix = [b.sb("mix", [128, D], F32) for _ in range(2)]; mixr = RL(2); mixd = [b.ds() for _ in range(2)]
        Sst = b.sb("Sst", [128, 4, 128], F32); Sbf = b.sb("Sbf", [128, 4, 128], BF16); Sr = Res()
        R0 = b.sb("R0", [128, 2, 4, 128], F32); R0r = Res()
        pw = b.sb("pw", [128, 2], F32)
        AD = b.sb("AD", [128, 4, 128], BF16); ADr = Res()
        osum = b.sb("osum", [128, 512], F32); osr = Res()
        mixT = b.sb("mixT", [128, 8, 128], BF16); mixTr = RL(8)
        tmp = [b.sb("tmp", [128, 512], F32) for _ in range(2)]; tmpr = RL(2)
        dA = b.ds(); dB = b.ds()
        b.dma(POOL, wo[:], I["w_o_ab"][j].rearrange("(kc p) n -> p kc n", p=128), [], [wor], b.dsw())
        b.dma(SP, ldb[:], I["ldec"][0].partition_broadcast(128), [], [dr], dA)
        b.dma(SP, pw[:], I["pw"][:, :], [], [dr], dA)
        b.op(ACT, lambda e: e.activation(out=ldb[:], in_=ldb[:], func=AF.Exp), [dr], [dr])
        b.op(DVE, lambda e: e.tensor_scalar(out=ldb[:], in0=ldb[:], scalar1=-1.0, scalar2=None, op0=ALU.mult), [dr], [dr])
        for h in range(4):
            lg = ldb[:, j * 8 + 4 + h:j * 8 + 4 + h + 1]
            b.op(ACT, lambda e: e.activation(out=etmp[:], in_=consts[:, 3, :], func=AF.Exp, scale=lg), [dr, cr], [dr])
            b.op(DVE, lambda e: e.scalar_tensor_tensor(out=DT[:, 1, h, :], in0=etmp[:], scalar=RS, in1=consts[:, 4, :],
                                                       op0=ALU.mult, op1=ALU.mult), [dr, cr], [dr])
            b.op(ACT, lambda e: e.activation(out=CDc[:, 1, h:h + 1], in_=consts[:, 5, 4:5], func=AF.Exp, scale=lg), [dr, cr], [dr])
        for r_ in range(2):
            b.dma(SP, R0[:, r_], cc2_out[r_ * 512:(r_ + 1) * 512, :].rearrange("(h d) e -> d h e", d=128), [cc2o_r], [R0r], dB)
        b.op(DVE, lambda e: e.tensor_scalar(out=Sst[:], in0=R0[:, 0], scalar1=pw[:, 0:1], scalar2=None, op0=ALU.mult), [R0r, dr], [Sr])
        b.op(DVE, lambda e: e.scalar_tensor_tensor(out=Sst[:], in0=R0[:, 1], scalar=pw[:, 1:2], in1=Sst[:], op0=ALU.mult, op1=ALU.add),
             [R0r, dr, Sr], [Sr])
        b.op(ACT, lambda e: e.copy(out=Sbf[:], in_=Sst[:]), [Sr], [Sr])
        def proj_tile(t, k):
            for kc in range(8):
                bk = b.bank()
                b.op(PE, lambda e: e.transpose(out=ps[bk][:, 0:128], in_=mix[k][:, kc * 128:(kc + 1) * 128], identity=ident),
                     [mixr[k], cr], [psr[bk]])
                copy_alt(kc, mixT[:, kc, :], ps[bk][:, 0:128], [psr[bk]], [mixTr[kc]])
            for half in range(2):
                bk = b.bank()
                for kc in range(8):
                    b.op(PE, lambda e: e.matmul(ps[bk][:, :], lhsT=mixT[:, kc, :], rhs=wo[:, kc, half * 512:(half + 1) * 512],
                                                start=(kc == 0), stop=(kc == 7)), [mixTr[kc], wor], [psr[bk]])
                resid_update(t, half, bk, 0, tmp[half], tmpr[half])

        for t in range(8):
            k = t % 2
            b.dma(SP, mix[k][:], mscr[t], [mscr_r[t]], [mixr[k]], mixd[k])
            proj_tile(t, k)
        for n_, ti in enumerate(range(15, -1, -1)):
            k = n_ % 2
            t = 8 + ti
            b.dma(SP, rop[k][:], rscr[ti], [rscr_r[ti]], [ropr[k]], ropd[k])
            b.dma(SP, ol[k][:], oscr[ti], [oscr_r[ti]], [olr[k]], old[k])
            b.dma(SP, sgt[k][:], gscr2[ti], [gscr2_r[ti]], [sgr[k]], sgd[k])
            b.dma(SP, mix[k][:, 0:512], ascr[ti], [ascr_r[ti]], [mixr[k]], mixd[k])
            rd = [ropr[k]]
            ba = b.bank()
            for h in range(4):
                b.op(PE, lambda e: e.matmul(ps[ba][:, h * 128:(h + 1) * 128], lhsT=rop[k][:, 2, h, :], rhs=rop[k][:, 0, h, :], start=True, stop=True),
                     rd, [psr[ba]])
            b.op(DVE, lambda e: e.tensor_tensor(out=AD[:].rearrange("p h i -> p (h i)"), in0=ps[ba][:, :],
                                                in1=DT[:, 1].rearrange("p h i -> p (h i)"), op=ALU.mult), [psr[ba], dr], [ADr])
            bo = b.bank()
            for h in range(4):
                b.op(PE, lambda e: e.matmul(ps[bo][:, h * 128:(h + 1) * 128], lhsT=AD[:, h, :], rhs=rop[k][:, 4, h, :], start=True, stop=False),
                     rd + [ADr], [psr[bo]])
                b.op(PE, lambda e: e.matmul(ps[bo][:, h * 128:(h + 1) * 128], lhsT=rop[k][:, 1, h, :], rhs=Sbf[:, h, :], start=False, stop=True),
                     rd + [Sr], [psr[bo]])
            bs_ = b.bank()
            for h in range(4):
                b.op(PE, lambda e: e.matmul(ps[bs_][:, h * 128:(h + 1) * 128], lhsT=rop[k][:, 3, h, :], rhs=rop[k][:, 4, h, :], start=True, stop=True),
                     rd, [psr[bs_]])
            for h in range(4):
                b.op(DVE, lambda e: e.scalar_tensor_tensor(out=Sst[:, h, :], in0=Sst[:, h, :], scalar=CDc[:, 1, h:h + 1],
                                                           in1=ps[bs_][:, h * 128:(h + 1) * 128], op0=ALU.mult, op1=ALU.add),
                     [psr[bs_], dr, Sr], [Sr])
            b.op(ACT, lambda e: e.copy(out=Sbf[:], in_=Sst[:]), [Sr], [Sr])
            b.op(DVE, lambda e: e.tensor_tensor(out=osum[:], in0=ps[bo][:, :], in1=ol[k][:], op=ALU.add), [psr[bo], olr[k]], [osr])
            for h in range(4):
                b.op(ACT, lambda e: e.activation(out=etmp[:], in_=osum[:, h * 128:(h + 1) * 128], func=AF.Square,
                                                 accum_out=ss[:, 4 + h:5 + h]), [osr], [dr, ssr])
            b.op(DVE, lambda e: e.tensor_scalar(out=ss[:, 4:8], in0=ss[:, 4:8], scalar1=1.0 / 128, scalar2=EPS, op0=ALU.mult, op1=ALU.add),
                 [ssr], [ssr])
            b.op(ACT, lambda e: e.activation(out=ss[:, 4:8], in_=ss[:, 4:8], func=AF.Sqrt), [ssr], [ssr])
            b.op(DVE, lambda e: e.reciprocal(out=ss[:, 4:8], in_=ss[:, 4:8]), [ssr], [ssr])
            for h in range(4):
                b.op(DVE, lambda e: e.scalar_tensor_tensor(out=mix[k][:, 512 + h * 128:512 + (h + 1) * 128], in0=osum[:, h * 128:(h + 1) * 128],
                                                           scalar=ss[:, 4 + h:5 + h], in1=sgt[k][:, h * 128:(h + 1) * 128],
                                                           op0=ALU.mult, op1=ALU.mult), [osr, ssr, sgr[k]], [mixr[k]])
            proj_tile(t, k)


_NC = None


def _rope_tables(pos):
    half = 16
    inv = 1.0 / np.power(np.float32(10000.0), np.arange(0, half, 2, dtype=np.float32) / np.float32(half))
    row = (pos // 64).astype(np.float32)
    col = (pos % 64).astype(np.float32)
    ang = np.stack([row[:, None] * inv, col[:, None] * inv], axis=1)
    ang = np.stack([ang, ang], axis=2).reshape(len(pos), 32)
    return np.cos(ang).astype(np.float32), np.sin(ang).astype(np.float32)


def _consts():
    c = np.zeros((128, 8, 128), np.float32)
    p = np.arange(128, dtype=np.float32)[:, None]
    f = np.arange(128, dtype=np.float32)[None, :]
    c[:, 0] = (p == f)
    c[:, 1] = np.maximum(f - p, 0)
    c[:, 2] = (f >= p)
    c[:, 3] = np.maximum(p - f, 0)
    c[:, 4] = (p >= f)
    c[:, 5, 0] = p[:, 0] + 1.0
    c[:, 5, 1] = 127.0 - p[:, 0]
    c[:, 5, 2] = 128.0 - p[:, 0]
    c[:, 5, 3] = p[:, 0]
    c[:, 5, 4] = 128.0
    c[:, 5, 5] = 1.0
    c[:, 6] = f + 1.0
    c[:, 7] = 128.0 - f
    return c


def _rmT():
    rm = np.zeros((32, 32), np.float32)
    for a in range(2):
        o = 16 * a
        for i in range(8):
            rm[o + 8 + i, o + i] = -1.0
            rm[o + i, o + 8 + i] = 1.0
    return rm


def kernel(x_prompt, x_sample, cache_mla_ckv, cache_mla_krope, state_ret, c, c_ctx,
           w_ada, b_ada, norm_g, w_in_ab, q_norm_g, w_uq, kv_norm_g, w_ukv, ret_log_decay, w_o_ab,
           w_in_c, ln_g_c, ln_b_c, w_s_c, b_s_c, w_out_c, w_ffn_gate, w_ffn_up, w_ffn_down, final_norm_g):
    global _NC
    f = lambda a: np.ascontiguousarray(np.asarray(a, dtype=np.float32))
    x_prompt, x_sample = f(x_prompt), f(x_sample)
    shared = {
        "w_ada": f(w_ada), "b_ada": f(b_ada),
        "b_adaT": f(np.asarray(b_ada).reshape(4, 48, 128).transpose(0, 2, 1)),
        "norm_gT": f(np.asarray(norm_g).reshape(4, 2, 8, 128).transpose(0, 1, 3, 2)),
        "w_in_ab": f(w_in_ab), "q_norm_gT": f(np.asarray(q_norm_g).reshape(2, 2, 128).transpose(0, 2, 1)),
        "w_uq": f(w_uq), "kv_norm_g": f(kv_norm_g), "w_ukv": f(w_ukv), "w_o_ab": f(w_o_ab),
        "w_in_c": f(w_in_c), "ln_g_c": f(ln_g_c), "ln_b_c": f(ln_b_c), "w_out_c": f(w_out_c),
        "w_ffn_gate": f(w_ffn_gate), "w_ffn_up": f(w_ffn_up), "w_ffn_down": f(w_ffn_down),
        "final_norm_g": f(final_norm_g), "consts": _consts(), "rmT": _rmT(),
    }
    ws = np.asarray(w_s_c, dtype=np.float32)
    bs = np.asarray(b_s_c, dtype=np.float32)
    wsT_nat = f(ws.transpose(0, 3, 1, 2))
    wsT_rev = f(ws[:, :, ::-1, ::-1].transpose(0, 3, 1, 2))
    bsT_nat = f(bs.transpose(0, 2, 1))
    bsT_rev = f(bs[:, :, ::-1].transpose(0, 2, 1))
    in_maps = []
    for r in range(8):
        p, par = r // 2, r % 2
        xp = x_prompt[4 * r:4 * r + 4]
        xs = x_sample[p, par * 2048:(par + 1) * 2048]
        pos = np.arange(par * 2048, (par + 1) * 2048)
        if par:
            xp = xp[:, ::-1]
            xs = xs[::-1]
            pos = pos[::-1]
        cos, sin = _rope_tables(pos)
        cond2 = np.stack([np.asarray(c_ctx, np.float32), np.asarray(c, np.float32)[p]])
        pw = np.zeros((128, 2), np.float32)
        pw[:, 1 - par] = 1.0
        ld = np.asarray(ret_log_decay, np.float32)[:, [par, 1 - par], :]
        m = dict(shared)
        m.update({
            "x_in": f(np.concatenate([xp.reshape(1024, D), xs], axis=0)),
            "cond2T": f(cond2.reshape(2, 8, 128).transpose(2, 1, 0)),
            "ctx_ckv": f(np.asarray(cache_mla_ckv)[p]), "ctx_kr": f(np.asarray(cache_mla_krope)[p]),
            "s0_lead": f(np.asarray(state_ret)[p, :, par]),
            "ldec": f(ld.reshape(1, 16)),
            "ropecs": f(np.stack([cos.T, sin.T])),
            "pw": pw,
            "wsT": wsT_rev if par else wsT_nat, "b_sT": bsT_rev if par else bsT_nat,
        })
        in_maps.append(m)
    if _NC is None:
        _NC = build_program()
    res = run_bass_kernel_spmd(_NC, in_maps, core_ids=list(range(8)))
    y_prompt = np.zeros((32, 256, D), np.float32)
    y_sample = np.zeros((4, 4096, D), np.float32)
    o_ckv = np.zeros((32, 2, 256, 128), np.float32)
    o_kr = np.zeros((32, 2, 256, 32), np.float32)
    o_st = np.zeros((32, 2, 2, 4, 128, 128), np.float32)
    for r in range(8):
        p, par = r // 2, r % 2
        o = res.results[r]
        yp = o["y"][:1024].reshape(4, 256, D)
        ys = o["y"][1024:]
        ck = o["o_ckv"].reshape(2, 4, 256, 128).transpose(1, 0, 2, 3)
        kr = o["o_kr"].reshape(2, 4, 256, 32).transpose(1, 0, 2, 3)
        stt = o["o_state"]
        if par:
            yp, ys, ck, kr = yp[:, ::-1], ys[::-1], ck[:, :, ::-1], kr[:, :, ::-1]
            stt = stt[:, :, ::-1]
        o_st[4 * r:4 * r + 4] = stt
        y_prompt[4 * r:4 * r + 4] = yp
        y_sample[p, par * 2048:(par + 1) * 2048] = ys
        o_ckv[4 * r:4 * r + 4] = ck
        o_kr[4 * r:4 * r + 4] = kr
    return (y_prompt, y_sample, o_ckv, o_kr, o_st)
```

```python
import contextlib
import os
import numpy as np
import concourse.bass as bass
import concourse.mybir as mybir
from concourse.bass_utils import run_bass_kernel_spmd

F32 = mybir.dt.float32
BF16 = mybir.dt.bfloat16
AF = mybir.ActivationFunctionType
ALU = mybir.AluOpType
AX = mybir.AxisListType

D = 1024
DEPTH = 4
DFF = 2816
NF = 22
NT = 24
NG = 6
EPS = 1e-6
ABIN = 2464
SCALE = 96 ** -0.5
WINDOW = 3


class Res:
    __slots__ = ("w", "r", "excl")

    def __init__(self, excl=False):
        self.w = None
        self.r = {}
        self.excl = excl


def RL(n):
    return [Res() for _ in range(n)]


class Eng:
    def __init__(self, key, eng, sems):
        self.key = key
        self.eng = eng
        self.sems = sems
        self.sem = sems[0]
        self.count = 0
        self.waited = {}


class DSem:
    def __init__(self, key, sem):
        self.key = key
        self.sem = sem
        self.total = 0


class Builder:
    def __init__(self, nc, es):
        self.nc = nc
        self.es = es
        mk = lambda n: es.enter_context(nc.semaphore(n))
        self.PE = Eng("pe", nc.tensor, [mk("s_pe0"), mk("s_pe1"), mk("s_pe2")])
        self.ACT = Eng("act", nc.scalar, [mk("s_act0"), mk("s_act1"), mk("s_act2")])
        self.DVE = Eng("dve", nc.vector, [mk("s_dve0"), mk("s_dve1"), mk("s_dve2")])
        self.POOL = Eng("pool", nc.gpsimd, [mk("s_pool0"), mk("s_pool1"), mk("s_pool2")])
        self.SP = Eng("sp", nc.sync, [mk("s_sp0"), mk("s_sp1"), mk("s_sp2")])
        self.engs = [self.PE, self.ACT, self.DVE, self.POOL, self.SP]
        self.ekeys = {E.key for E in self.engs}
        self.epoch = 0
        self.dsems = [DSem(f"d{i}", mk(f"s_d{i}")) for i in range(72)]
        self.dfree_hw = self.dsems[:44]
        self.dfree_sw = self.dsems[44:]
        self.cc_sem = mk("s_cc")
        self.cc_count = 0
        self.ps = [es.enter_context(nc.psum_tensor(f"ps{i}", [128, 512], F32)) for i in range(8)]
        self.psr = [Res(excl=True) for _ in range(8)]
        self.bank_i = 0
        self.uid = 0
        self.phase_es = None
        self.phase_ds = []

    def sb(self, name, shape, dtype, persistent=False):
        self.uid += 1
        stack = self.es if (persistent or self.phase_es is None) else self.phase_es
        return stack.enter_context(self.nc.sbuf_tensor(f"{name}_{self.uid}", shape, dtype))

    def ds(self, persistent=False, sw=False):
        pool = self.dfree_sw if sw else self.dfree_hw
        d = pool.pop()
        if not persistent and self.phase_es is not None:
            self.phase_ds.append((d, pool))
        return d

    def dsw(self):
        return self.ds(sw=True)

    def bank(self):
        i = self.bank_i
        self.bank_i = (i + 1) % 8
        return i

    @contextlib.contextmanager
    def phase(self):
        assert self.phase_es is None
        with contextlib.ExitStack() as pes:
            self.phase_es = pes
            self.phase_ds = []
            yield
            self.barrier()
            for d, pool in self.phase_ds:
                pool.append(d)
            self.phase_ds = []
            self.phase_es = None

    def _wait(self, E, deps):
        for key, (sem, val) in deps.items():
            if key == E.key:
                if E is self.PE or E is self.SP:
                    continue
                if val <= E.count - WINDOW:
                    continue
            if E.waited.get(key, 0) >= val:
                continue
            E.eng.wait_ge(sem, val)
            E.waited[key] = val

    def _collect(self, reads, writes):
        deps = {}

        def add(rec):
            if rec is None:
                return
            key, sem, val, ep = rec
            if ep is not None and ep < self.epoch:
                return
            if key not in deps or deps[key][1] < val:
                deps[key] = (sem, val)

        for r in reads:
            add(r.w)
            if r.excl:
                for rec in r.r.values():
                    add(rec)
        for w in writes:
            add(w.w)
            for rec in w.r.values():
                add(rec)
        return deps

    @staticmethod
    def _commit(rec, reads, writes):
        for w in writes:
            w.w = rec
            w.r = {}
        for r in reads:
            r.r[rec[0]] = rec

    def op(self, E, fn, reads=(), writes=()):
        self._wait(E, self._collect(reads, writes))
        ins = fn(E.eng)
        E.count += 1
        ins.then_inc(E.sem, 1)
        self._commit((E.key, E.sem, E.count, self.epoch), reads, writes)

    def dma(self, Q, out, in_, reads, writes, d):
        assert (Q is self.POOL) == (d in self.dsems[44:]), "SW/HW DMA semaphore pools must not mix"
        deps = self._collect(reads, writes)
        if d.total:
            deps[d.key] = (d.sem, d.total)
        self._wait(Q, deps)
        ins = Q.eng.dma_start(out=out, in_=in_)
        d.total += 16
        ins.then_inc(d.sem, 16)
        self._commit((d.key, d.sem, d.total, None), reads, writes)

    def barrier(self):
        for E in self.engs:
            deps = {}
            for o in self.engs:
                if o is not E and o.count:
                    deps[o.key] = (o.sem, o.count)
            for d in self.dsems:
                if d.total:
                    deps[d.key] = (d.sem, d.total)
            if self.cc_count:
                deps["cc"] = (self.cc_sem, self.cc_count)
            self._wait(E, deps)
        self.epoch += 1
        for E in self.engs:
            old = E.sem
            E.sem = E.sems[self.epoch % 3]
            E.count = 0
            for k in self.ekeys:
                E.waited.pop(k, None)
        for E in self.engs:
            E.eng.sem_clear(E.sems[(self.epoch + 1) % 3])

    def allgather(self, in_t, out_t, in_res, out_res):
        P = self.POOL
        self._wait(P, self._collect([in_res], [out_res]))
        ins = P.eng.collective_compute(
            "AllGather", ALU.bypass, replica_groups=[[0, 1], [2, 3], [4, 5], [6, 7]],
            ins=[in_t.ap().opt()], outs=[out_t.ap().opt()])
        self.cc_count += 1
        ins.then_inc(self.cc_sem, 1)
        self._commit(("cc", self.cc_sem, self.cc_count, None), [in_res], [out_res])


def build_program():
    nc = bass.Bass("TRN2", target_bir_lowering=False)

    def din(name, shape):
        return nc.dram_tensor(name, list(shape), F32, kind="ExternalInput").ap()

    def dout(name, shape):
        return nc.dram_tensor(name, list(shape), F32, kind="ExternalOutput").ap()

    I = {}
    for name, shape in [
        ("x_in", (3072, D)), ("cond2T", (128, 8, 2)), ("ctx_ckv", (2, 512, 128)), ("ctx_kr", (2, 512, 32)),
        ("s0_lead", (2, 4, 128, 128)), ("ldec", (1, 16)), ("ropecs", (2, 32, 2048)), ("pw", (128, 2)),
        ("w_ada", (4, D, 6 * D)), ("b_ada", (4, 6 * D)), ("b_adaT", (4, 128, 48)), ("norm_gT", (4, 2, 128, 8)),
        ("w_in_ab", (2, D, ABIN)), ("q_norm_gT", (2, 128, 2)), ("w_uq", (2, 256, 768)), ("kv_norm_g", (2, 128)),
        ("w_ukv", (2, 128, 1024)), ("w_o_ab", (2, D, D)), ("w_in_c", (2, D, 2 * D)), ("ln_g_c", (2, D)),
        ("ln_b_c", (2, D)), ("wsT", (2, 128, 8, 128)), ("b_sT", (2, 128, 8)), ("w_out_c", (2, D, D)),
        ("w_ffn_gate", (4, D, DFF)), ("w_ffn_up", (4, D, DFF)), ("w_ffn_down", (4, DFF, D)),
        ("final_norm_g", (D,)), ("consts", (128, 8, 128)), ("rmT", (32, 32)),
    ]:
        I[name] = din(name, shape)
    O = {
        "y": dout("y", (3072, D)), "o_ckv": dout("o_ckv", (2, 1024, 128)), "o_kr": dout("o_kr", (2, 1024, 32)),
        "o_state": dout("o_state", (4, 2, 2, 4, 128, 128)),
    }
    gscr = nc.dram_tensor("gscr", [4, 2, 2, D], F32)
    gscr_r = Res()
    cc1_in = nc.dram_tensor("cc1_in", [160, 2048], BF16)
    cc1_out = nc.dram_tensor("cc1_out", [320, 2048], BF16)
    cc2_in = nc.dram_tensor("cc2_in", [512, 128], F32)
    cc2_out = nc.dram_tensor("cc2_out", [1024, 128], F32)
    qscr = nc.dram_tensor("qscr", [8, 2, 96, 2048], BF16)
    rscr = nc.dram_tensor("rscr", [16, 128, 5, 4, 128], BF16)
    oscr = nc.dram_tensor("oscr", [16, 128, 512], F32)
    gscr2 = nc.dram_tensor("gscr2", [16, 128, 512], BF16)
    ascr = nc.dram_tensor("ascr", [16, 128, 512], F32)
    mscr = nc.dram_tensor("mscr", [8, 128, D], F32)
    wc_gu = nc.dram_tensor("wc_gu", [11, 2, 128, 8 * 256], BF16)
    wc_d = nc.dram_tensor("wc_d", [2, 11, 128, 2 * 512], BF16)
    wc_gu_r = [[Res(), Res()] for _ in range(11)]
    wc_d_r = [[Res() for _ in range(11)] for _ in range(2)]

    with contextlib.ExitStack() as es:
        b = Builder(nc, es)
        PE, ACT, DVE, POOL, SP = b.PE, b.ACT, b.DVE, b.POOL, b.SP
        ps, psr = b.ps, b.psr

        x = b.sb("x", [128, NT, D], F32, True)
        xr = RL(NT)
        consts = b.sb("consts", [128, 8, 128], F32, True)
        cr = Res()
        modc = b.sb("modc", [128, 4, 2, 4, 8], F32, True)
        modr = RL(4)
        gb = b.sb("gb", [128, 2, D], F32, True)
        gbr = Res()
        ident = consts[:, 0, :]

        d0 = b.ds(True)
        b.dma(SP, consts[:], I["consts"][:, :, :], [], [cr], d0)
        xin = I["x_in"].rearrange("(t p) d -> p t d", p=128)
        xds = [b.ds(True) for _ in range(NG)]
        for g in range(NG):
            b.dma(SP, x[:, 4 * g:4 * g + 4, :], xin[:, 4 * g:4 * g + 4, :], [], xr[4 * g:4 * g + 4], xds[g])

        def cond_of(t):
            return 0 if t < 8 else 1

        def rstd_from_ss(ss, n, inv_n, rs_res):
            b.op(DVE, lambda e: e.tensor_scalar(out=ss[:, :n], in0=ss[:, :n], scalar1=inv_n, scalar2=EPS,
                                               op0=ALU.mult, op1=ALU.add), [rs_res], [rs_res])
            b.op(ACT, lambda e: e.activation(out=ss[:, :n], in_=ss[:, :n], func=AF.Sqrt), [rs_res], [rs_res])
            b.op(DVE, lambda e: e.reciprocal(out=ss[:, :n], in_=ss[:, :n]), [rs_res], [rs_res])

        def norm_group(t0, n, l, which, hT, hTres, xs4, xs4r, ss, ssr):
            c = cond_of(t0)
            for tt in range(n):
                t = t0 + tt
                b.op(ACT, lambda e: e.activation(out=xs4[:, tt, :], in_=x[:, t, :], func=AF.Square,
                                                 accum_out=ss[:, tt:tt + 1]), [xr[t]], [xs4r[tt], ssr])
            rstd_from_ss(ss, n, 1.0 / D, ssr)
            for tt in range(n):
                t = t0 + tt
                b.op(DVE, lambda e: e.tensor_scalar(out=xs4[:, tt, :], in0=x[:, t, :], scalar1=ss[:, tt:tt + 1],
                                                    scalar2=None, op0=ALU.mult), [xr[t], ssr], [xs4r[tt]])
            for kc in range(8):
                bk = b.bank()
                for tt in range(n):
                    b.op(PE, lambda e: e.transpose(out=ps[bk][:, tt * 128:(tt + 1) * 128],
                                                   in_=xs4[:, tt, kc * 128:(kc + 1) * 128], identity=ident),
                         [xs4r[tt], cr], [psr[bk]])
                b.op(ACT, lambda e: e.activation(out=hT[:, kc, :n * 128], in_=ps[bk][:, :n * 128], func=AF.Identity,
                                                 scale=modc[:, l, c, 2 * which + 1, kc:kc + 1],
                                                 bias=modc[:, l, c, 2 * which, kc:kc + 1]),
                     [psr[bk], modr[l]], [hTres])

        def resid_update(t, half, bk, gate, tmp, tmpr):
            c = cond_of(t)
            sl = slice(half * 512, (half + 1) * 512)
            b.op(DVE, lambda e: e.tensor_tensor(out=tmp[:], in0=ps[bk][:, :], in1=gb[:, c, sl], op=ALU.mult),
                 [psr[bk], gbr], [tmpr])
            b.op(DVE, lambda e: e.tensor_tensor(out=x[:, t, sl], in0=x[:, t, sl], in1=tmp[:], op=ALU.add),
                 [tmpr, xr[t]], [xr[t]])

        with b.phase():
            scT = b.sb("scT", [128, 8, 2], BF16)
            scTf = b.sb("scTf", [128, 8, 2], F32)
            scr = Res()
            wsl = [b.sb("wada", [128, 8, 512], BF16) for _ in range(3)]
            wslr = RL(3)
            wsd = [b.dsw() for _ in range(3)]
            badT = b.sb("badT", [128, 4, 48], F32)
            ngT = b.sb("ngT", [128, 4, 2, 8], F32)
            brow = b.sb("brow", [2, 4, 2, D], F32)
            grow = b.sb("grow", [2, 4, 2, D], F32)
            miscr = Res()
            growr = Res()
            dd = b.ds()
            b.dma(SP, scTf[:], I["cond2T"][:, :, :], [], [scr], dd)
            b.dma(SP, badT[:], I["b_adaT"].rearrange("l p c -> p l c"), [], [miscr], dd)
            b.dma(SP, ngT[:], I["norm_gT"].rearrange("l w p k -> p l w k"), [], [miscr], dd)
            for j in range(2):
                for gi in range(2):
                    b.dma(SP, brow[j:j + 1, :, gi, :], I["b_ada"][None, :, (2 + 3 * gi) * D:(3 + 3 * gi) * D], [], [miscr], dd)
            b.op(ACT, lambda e: e.activation(out=scT[:], in_=scTf[:], func=AF.Silu), [scr], [scr])
            ui = 0
            for l in range(DEPTH):
                wv = I["w_ada"][l].rearrange("(kc p) n -> p kc n", p=128)
                for cb in range(12):
                    s = ui % 3
                    ui += 1
                    b.dma(POOL, wsl[s][:], wv[:, :, cb * 512:(cb + 1) * 512], [], [wslr[s]], wsd[s])
                    v = cb // 2
                    if v in (2, 5):
                        bk = b.bank()
                        for kc in range(8):
                            b.op(PE, lambda e: e.matmul(ps[bk][0:2, :], lhsT=scT[:, kc, :], rhs=wsl[s][:, kc, :],
                                                        start=(kc == 0), stop=(kc == 7)), [scr, wslr[s]], [psr[bk]])
                        gi = 0 if v == 2 else 1
                        hs = (cb % 2) * 512
                        b.op(DVE, lambda e: e.tensor_tensor(out=grow[:, l, gi, hs:hs + 512], in0=ps[bk][0:2, :],
                                                            in1=brow[:, l, gi, hs:hs + 512], op=ALU.add),
                             [psr[bk], miscr], [growr])
                    else:
                        vi = {0: 0, 1: 1, 3: 2, 4: 3}[v]
                        bk = b.bank()
                        for q in range(4):
                            for kc in range(8):
                                b.op(PE, lambda e: e.matmul(ps[bk][:, 2 * q:2 * q + 2],
                                                            lhsT=wsl[s][:, kc, q * 128:(q + 1) * 128],
                                                            rhs=scT[:, kc, :], start=(kc == 0), stop=(kc == 7)),
                                     [scr, wslr[s]], [psr[bk]])
                        k0 = (cb % 2) * 4
                        for c in range(2):
                            b.op(DVE, lambda e: e.tensor_tensor(
                                out=modc[:, l, c, vi, k0:k0 + 4],
                                in0=ps[bk][:, 0:8].rearrange("p (q c) -> p q c", c=2)[:, :, c],
                                in1=badT[:, l, cb * 4:cb * 4 + 4], op=ALU.add), [psr[bk], miscr], [modr[l]])
                for c in range(2):
                    for w in range(2):
                        b.op(DVE, lambda e: e.scalar_tensor_tensor(
                            out=modc[:, l, c, 2 * w + 1, :], in0=modc[:, l, c, 2 * w + 1, :], scalar=1.0,
                            in1=ngT[:, l, w, :], op0=ALU.add, op1=ALU.mult), [modr[l], miscr], [modr[l]])
            for j in range(2):
                b.dma(SP, gscr[:, j, :, :], grow[j:j + 1, :, :, :], [growr], [gscr_r], dd)

        gbd = b.ds(True)

        def load_gates(l, gi):
            for c in range(2):
                b.dma(SP, gb[:, c, :], gscr[l, c, gi, :].partition_broadcast(128), [gscr_r], [gbr], gbd)

        def ffn_phase(l):
            with b.phase():
                hT = [b.sb("hT", [128, 8, 512], BF16) for _ in range(2)]
                hTr = RL(2)
                xs4 = b.sb("xs4", [128, 4, D], F32)
                xs4r = RL(4)
                junk = junkr = None
                ss = b.sb("ss", [128, 4], F32)
                ssr = Res()
                aT = b.sb("aT", [128, NF, 512], BF16)
                aTr = RL(NF)
                wgu = [b.sb("wgu", [128, 2, 8, 256], BF16) for _ in range(3)]
                wgr = [RL(2) for _ in range(3)]
                wgd = [[b.dsw(), b.dsw()] for _ in range(3)]
                wd = [b.sb("wd", [128, 2, 512], BF16) for _ in range(3)]
                wdr = RL(3)
                wdd = [b.dsw() for _ in range(3)]
                wgh = [[b.ds(), b.ds()] for _ in range(3)]
                wdh = [b.ds() for _ in range(3)]
                wgo = [[b.ds(), b.ds()] for _ in range(3)]
                wdo = [b.ds() for _ in range(3)]
                sg = [b.sb("sg", [128, 512], BF16) for _ in range(2)]
                sgr = RL(2)
                tmp = [b.sb("tmp", [128, 512], F32) for _ in range(2)]
                tmpr = RL(2)
                wgv = I["w_ffn_gate"][l].rearrange("(kc p) n -> p kc n", p=128)
                wuv = I["w_ffn_up"][l].rearrange("(kc p) n -> p kc n", p=128)
                wdv = I["w_ffn_down"][l].rearrange("(f p) n -> p f n", p=128)
                norm_group(0, 4, l, 1, hT[0], hTr[0], xs4, xs4r, ss, ssr)
                ug = 0
                ud = 0
                si = 0
                for g in range(NG):
                    h = hT[g % 2]
                    hr = hTr[g % 2]
                    for u in range(11):
                        s = ug % 3
                        ug += 1
                        if g == 0:
                            b.dma(POOL, wgu[s][:, 0], wgv[:, :, u * 256:(u + 1) * 256], [], [wgr[s][0]], wgd[s][0])
                            b.dma(POOL, wgu[s][:, 1], wuv[:, :, u * 256:(u + 1) * 256], [], [wgr[s][1]], wgd[s][1])
                            for w_ in range(2):
                                b.dma(SP, wc_gu[u, w_], wgu[s][:, w_].rearrange("p k n -> p (k n)"), [wgr[s][w_]], [wc_gu_r[u][w_]], wgo[s][w_])
                        else:
                            for w_ in range(2):
                                b.dma(SP, wgu[s][:, w_].rearrange("p k n -> p (k n)"), wc_gu[u, w_], [wc_gu_r[u][w_]], [wgr[s][w_]], wgh[s][w_])
                        for fi in range(2):
                            f = 2 * u + fi
                            bg = b.bank()
                            bu = b.bank()
                            for kc in range(8):
                                b.op(PE, lambda e: e.matmul(ps[bg][:, :], lhsT=wgu[s][:, 0, kc, fi * 128:(fi + 1) * 128],
                                                            rhs=h[:, kc, :], start=(kc == 0), stop=(kc == 7)),
                                     [wgr[s][0], hr], [psr[bg]])
                            for kc in range(8):
                                b.op(PE, lambda e: e.matmul(ps[bu][:, :], lhsT=wgu[s][:, 1, kc, fi * 128:(fi + 1) * 128],
                                                            rhs=h[:, kc, :], start=(kc == 0), stop=(kc == 7)),
                                     [wgr[s][1], hr], [psr[bu]])
                            k = si % 2
                            si += 1
                            b.op(ACT, lambda e: e.activation(out=sg[k][:], in_=ps[bg][:, :], func=AF.Silu),
                                 [psr[bg]], [sgr[k]])
                            b.op(DVE, lambda e: e.tensor_tensor(out=aT[:, f, :], in0=sg[k][:], in1=ps[bu][:, :],
                                                                op=ALU.mult), [sgr[k], psr[bu]], [aTr[f]])
                        if u == 5 and g + 1 < NG:
                            norm_group(4 * (g + 1), 4, l, 1, hT[(g + 1) % 2], hTr[(g + 1) % 2], xs4, xs4r, ss, ssr)
                    for half in range(2):
                        bks = [b.bank() for _ in range(4)]
                        for u in range(11):
                            s = ud % 3
                            ud += 1
                            if g == 0:
                                b.dma(POOL, wd[s][:], wdv[:, 2 * u:2 * u + 2, half * 512:(half + 1) * 512], [], [wdr[s]], wdd[s])
                                b.dma(SP, wc_d[half, u], wd[s][:].rearrange("p f n -> p (f n)"), [wdr[s]], [wc_d_r[half][u]], wdo[s])
                            else:
                                b.dma(SP, wd[s][:].rearrange("p f n -> p (f n)"), wc_d[half, u], [wc_d_r[half][u]], [wdr[s]], wdh[s])
                            for fi in range(2):
                                f = 2 * u + fi
                                for tt in range(4):
                                    b.op(PE, lambda e: e.matmul(ps[bks[tt]][:, :], lhsT=aT[:, f, tt * 128:(tt + 1) * 128],
                                                                rhs=wd[s][:, fi, :], start=(f == 0), stop=(f == NF - 1)),
                                         [aTr[f], wdr[s]], [psr[bks[tt]]])
                        for tt in range(4):
                            resid_update(4 * g + tt, half, bks[tt], 1, tmp[tt % 2], tmpr[tt % 2])

        def c_phase(l):
            j = l // 2
            NTC = 2
            W = NTC * 128
            NGC = NT // NTC
            with b.phase():
                hT = [b.sb("hT", [128, 8, W], BF16) for _ in range(2)]
                hTr = RL(2)
                ss = b.sb("ss", [128, 4], F32)
                ssr = Res()
                win = [b.sb("winc", [128, 8, 512], BF16) for _ in range(2)]
                winr = RL(2)
                wind = [b.dsw() for _ in range(2)]
                wout = b.sb("woutc", [128, 8, D], BF16)
                woutr = Res()
                wst = b.sb("wst", [128, 8, 128], BF16)
                bst = b.sb("bst", [128, 8], F32)
                lng = b.sb("lng", [128, D], F32)
                lnb = b.sb("lnb", [128, D], F32)
                cwr = Res()
                u_sb = [b.sb("u_sb", [128, NTC, D], BF16) for _ in range(2)]
                ur = [RL(NTC) for _ in range(2)]
                v_sb = [b.sb("v_sb", [128, NTC, D], F32) for _ in range(2)]
                vr = [RL(NTC) for _ in range(2)]
                vln = b.sb("vln", [128, NTC, D], BF16)
                vlr = RL(NTC)
                aT = b.sb("aTc", [128, 8, W], BF16)
                aTr = RL(8)
                st = b.sb("bnst", [128, 2, 6], F32)
                mv = b.sb("bnmv", [128, 2], F32)
                str_ = Res()
                tmp = [b.sb("tmp", [128, 512], F32) for _ in range(2)]
                tmpr = RL(2)
                dd = b.ds()
                b.dma(POOL, wout[:], I["w_out_c"][j].rearrange("(kc p) n -> p kc n", p=128), [], [woutr], b.dsw())
                b.dma(POOL, wst[:], I["wsT"][j], [], [cwr], b.dsw())
                b.dma(SP, bst[:], I["b_sT"][j], [], [cwr], dd)
                b.dma(SP, lng[:], I["ln_g_c"][j].partition_broadcast(128), [], [cwr], dd)
                b.dma(SP, lnb[:], I["ln_b_c"][j].partition_broadcast(128), [], [cwr], dd)
                wv = I["w_in_c"][j].rearrange("(kc p) n -> p kc n", p=128)
                ui = [0]

                def front(g):
                    k = g % 2
                    t0 = g * NTC
                    norm_group(t0, NTC, l, 0, hT[k], hTr[k], v_sb[k], vr[k], ss, ssr)
                    yield
                    for cb in range(4):
                        s = ui[0] % 2
                        ui[0] += 1
                        b.dma(POOL, win[s][:], wv[:, :, cb * 512:(cb + 1) * 512], [], [winr[s]], wind[s])
                        for tt in range(NTC):
                            bk = b.bank()
                            for kc in range(8):
                                b.op(PE, lambda e: e.matmul(ps[bk][:, :], lhsT=hT[k][:, kc, tt * 128:(tt + 1) * 128],
                                                            rhs=win[s][:, kc, :], start=(kc == 0), stop=(kc == 7)),
                                     [hTr[k], winr[s]], [psr[bk]])
                            if cb < 2:
                                b.op(ACT, lambda e: e.activation(out=u_sb[k][:, tt, cb * 512:(cb + 1) * 512], in_=ps[bk][:, :],
                                                                 func=AF.Gelu_apprx_tanh), [psr[bk]], [ur[k][tt]])
                            else:
                                b.op(ACT, lambda e: e.activation(out=v_sb[k][:, tt, (cb - 2) * 512:(cb - 1) * 512],
                                                                 in_=ps[bk][:, :], func=AF.Gelu_apprx_tanh),
                                     [psr[bk]], [vr[k][tt]])
                            yield

                def back(g):
                    k = g % 2
                    t0 = g * NTC
                    v = v_sb[k]
                    a_sb, ar = v_sb[k], vr[k]
                    for tt in range(NTC):
                        for hh in range(2):
                            b.op(DVE, lambda e: e.bn_stats(out=st[:, hh, :], in_=v[:, tt, hh * 512:(hh + 1) * 512]),
                                 [vr[k][tt]], [str_])
                        b.op(DVE, lambda e: e.bn_aggr(out=mv[:], in_=st[:].rearrange("p a s -> p (a s)")), [str_], [str_])
                        b.op(DVE, lambda e: e.tensor_scalar(out=mv[:, 1:2], in0=mv[:, 1:2], scalar1=EPS, scalar2=None,
                                                            op0=ALU.add), [str_], [str_])
                        b.op(ACT, lambda e: e.activation(out=mv[:, 1:2], in_=mv[:, 1:2], func=AF.Sqrt), [str_], [str_])
                        b.op(DVE, lambda e: e.reciprocal(out=mv[:, 1:2], in_=mv[:, 1:2]), [str_], [str_])
                        b.op(DVE, lambda e: e.tensor_scalar(out=v[:, tt, :], in0=v[:, tt, :], scalar1=mv[:, 0:1],
                                                            scalar2=mv[:, 1:2], op0=ALU.subtract, op1=ALU.mult),
                             [vr[k][tt], str_], [vr[k][tt]])
                        b.op(DVE, lambda e: e.tensor_tensor(out=v[:, tt, :], in0=v[:, tt, :], in1=lng[:], op=ALU.mult),
                             [vr[k][tt], cwr], [vr[k][tt]])
                        b.op(DVE, lambda e: e.tensor_tensor(out=vln[:, tt, :], in0=v[:, tt, :], in1=lnb[:], op=ALU.add),
                             [vr[k][tt], cwr], [vlr[tt]])
                        yield
                        bk2 = [b.bank(), b.bank()]
                        for gg in range(8):
                            bk = bk2[gg // 4]
                            b.op(PE, lambda e: e.matmul(ps[bk][:, (gg % 4) * 128:(gg % 4 + 1) * 128], lhsT=wst[:, gg, :],
                                                        rhs=vln[:, tt, gg * 128:(gg + 1) * 128], start=True, stop=True),
                                 [cwr, vlr[tt]], [psr[bk]])
                        yield
                        for gg in range(8):
                            bk = bk2[gg // 4]
                            b.op(DVE, lambda e: e.scalar_tensor_tensor(
                                out=a_sb[:, tt, gg * 128:(gg + 1) * 128], in0=ps[bk][:, (gg % 4) * 128:(gg % 4 + 1) * 128],
                                scalar=bst[:, gg:gg + 1], in1=u_sb[k][:, tt, gg * 128:(gg + 1) * 128],
                                op0=ALU.add, op1=ALU.mult), [psr[bk], cwr, ur[k][tt]], [ar[tt]])
                        yield
                    for kc in range(8):
                        bk = b.bank()
                        for tt in range(NTC):
                            b.op(PE, lambda e: e.transpose(out=ps[bk][:, tt * 128:(tt + 1) * 128],
                                                           in_=a_sb[:, tt, kc * 128:(kc + 1) * 128], identity=ident),
                                 [ar[tt], cr], [psr[bk]])
                        b.op(ACT, lambda e: e.copy(out=aT[:, kc, :], in_=ps[bk][:, :W]), [psr[bk]], [aTr[kc]])
                        if kc % 2 == 1:
                            yield
                    for tt in range(NTC):
                        for half in range(2):
                            bk = b.bank()
                            for kc in range(8):
                                b.op(PE, lambda e: e.matmul(ps[bk][:, :], lhsT=aT[:, kc, tt * 128:(tt + 1) * 128],
                                                            rhs=wout[:, kc, half * 512:(half + 1) * 512],
                                                            start=(kc == 0), stop=(kc == 7)), [aTr[kc], woutr], [psr[bk]])
                            resid_update(t0 + tt, half, bk, 0, tmp[half], tmpr[half])
                            yield

                def interleave(*gens):
                    gens = [g_ for g_ in gens if g_ is not None]
                    while gens:
                        for g_ in list(gens):
                            try:
                                next(g_)
                            except StopIteration:
                                gens.remove(g_)

                interleave(front(0))
                for g in range(NGC):
                    interleave(back(g), front(g + 1) if g + 1 < NGC else None)

        def final_phase():
            with b.phase():
                fg = b.sb("fg", [128, D], F32)
                fgr = Res()
                junk = b.sb("junk", [128, D], F32)
                junkr = Res()
                ss = b.sb("ssf", [128, NT], F32)
                ssr = Res()
                yo = [b.sb("yo", [128, D], F32) for _ in range(3)]
                yor = RL(3)
                yd = [b.ds() for _ in range(3)]
                b.dma(SP, fg[:], I["final_norm_g"].partition_broadcast(128), [], [fgr], b.ds())
                for t in range(NT):
                    b.op(ACT, lambda e: e.activation(out=junk[:], in_=x[:, t, :], func=AF.Square,
                                                     accum_out=ss[:, t:t + 1]), [xr[t]], [junkr, ssr])
                rstd_from_ss(ss, NT, 1.0 / D, ssr)
                yv = O["y"].rearrange("(t p) d -> p t d", p=128)
                for t in range(NT):
                    k = t % 3
                    b.op(DVE, lambda e: e.scalar_tensor_tensor(out=yo[k][:], in0=x[:, t, :], scalar=ss[:, t:t + 1],
                                                               in1=fg[:], op0=ALU.mult, op1=ALU.mult),
                         [xr[t], ssr, fgr], [yor[k]])
                    b.dma(SP, yv[:, t, :], yo[k][:], [yor[k]], [], yd[k])

        def ab_phase(l):
            j = l // 2
            ab_layer(b, I, O, j, l, x, xr, consts, cr, ident, norm_group, resid_update,
                     dict(cc1_in=cc1_in, cc1_out=cc1_out, cc2_in=cc2_in, cc2_out=cc2_out, qscr=qscr, rscr=rscr,
                          oscr=oscr, gscr2=gscr2, ascr=ascr, mscr=mscr))

        for l in range(int(os.environ.get("NLAYERS", DEPTH))):
            load_gates(l, 0)
            if l % 2 == 0:
                ab_phase(l)
            else:
                c_phase(l)
            load_gates(l, 1)
            ffn_phase(l)
        final_phase()
        b.barrier()
    return nc


def ab_layer(b, I, O, j, l, x, xr, consts, cr, ident, norm_group, resid_update, S):
    PE, ACT, DVE, POOL, SP = b.PE, b.ACT, b.DVE, b.POOL, b.SP
    ps, psr = b.ps, b.psr
    cc1_in, cc1_out, cc2_in, cc2_out = S["cc1_in"], S["cc1_out"], S["cc2_in"], S["cc2_out"]
    qscr, rscr, oscr, gscr2, ascr, mscr = S["qscr"], S["rscr"], S["oscr"], S["gscr2"], S["ascr"], S["mscr"]
    mscr_r = RL(8)
    RS = 128 ** -0.5
    STAGE = float(os.environ.get("ABDBG", "9"))
    if STAGE < 1:
        return
    cc1i_r, cc1o_r, cc2i_r, cc2o_r = Res(), Res(), Res(), Res()
    qscr_r, rscr_r, oscr_r, gscr2_r, ascr_r = Res(), RL(16), RL(16), RL(16), RL(16)

    def copy_alt(k, out, in_, reads, writes):
        if k % 2 == 0:
            b.op(ACT, lambda e: e.copy(out=out, in_=in_), reads, writes)
        else:
            b.op(DVE, lambda e: e.tensor_copy(out=out, in_=in_), reads, writes)

    with b.phase():
        W = 256
        hT = b.sb("hT", [128, 8, W], BF16); hTr = Res()
        xs4 = b.sb("xs4", [128, 2, D], F32); xs4r = RL(2)
        mix, mixr = xs4, xs4r
        ss = b.sb("ss", [128, 8], F32); ssr = Res()
        win = [b.sb("winab", [128, 8, 512], BF16) for _ in range(2)]; winr = RL(2); wind = [b.dsw() for _ in range(2)]
        wuq = b.sb("wuq", [128, 2, 8, 96], BF16)
        wkn = b.sb("wkn", [128, 8, 96], BF16)
        wv = b.sb("wv", [128, 8, 64], BF16)
        qng = b.sb("qng", [128, 2], F32)
        kvg = b.sb("kvg", [128, 128], F32)
        rm = b.sb("rm", [32, 32], BF16)
        rhi = b.sb("rhi", [32, W], BF16); rlo = b.sb("rlo", [32, W], BF16); rhr = Res()
        wr = Res()
        ldb = b.sb("ldb", [128, 16], F32)
        DT = b.sb("DT", [128, 2, 4, 128], BF16)
        QD = b.sb("QD", [128, 2, 4, 128], F32)
        KD = b.sb("KD", [128, 2, 4], F32)
        CDc = b.sb("CD", [128, 2, 4], F32)
        etmp = b.sb("etmp", [128, 128], F32)
        dr = Res()
        cqn = b.sb("cqn", [128, 256], F32); ckvn = b.sb("ckvn", [128, 2, 128], F32); krs = b.sb("krs", [128, 2, 32], F32)
        pr = RL(2)
        cqnT = b.sb("cqnT", [128, 2, W], BF16); ckvnT = b.sb("ckvnT", [128, W], BF16)
        krT = b.sb("krT", [32, W], BF16); krTf = b.sb("krTf", [32, W], F32)
        tr = Res()
        cs2 = b.sb("cs2", [32, 2, 2, W], F32); csr = Res(); csd = b.ds()
        wuqR = b.sb("wuqR", [128, 2, 8, 32], BF16)
        qT = b.sb("qT", [96, 8, W], BF16); qTr = Res()
        qrr = Res()
        rt1 = b.sb("rt1", [32, 2 * W], F32); rt2 = b.sb("rt2", [32, 2 * W], F32); rtr = Res()
        KT = b.sb("KT", [96, 8, W], BF16); KTr = Res()
        V = b.sb("V", [128, 2, 8, 65], BF16); Vr = Res()
        PT = [b.sb("PT", [128, W], BF16) for _ in range(2)]; PTr = RL(2)
        rc = b.sb("rc", [128, 2], F32); rcr = Res()
        rop = b.sb("rop", [128, 2, 5, 4, 128], BF16); ropr = RL(2)
        qdl = b.sb("qdl", [128, 2, 4, 128], BF16); kdl = b.sb("kdl", [128, 2, 4, 128], BF16); rlr = RL(2)
        sgt = b.sb("sgt", [128, 2, 512], BF16); sgr = RL(2)
        Sst = [b.sb("Sst", [128, 4, 128], F32) for _ in range(2)]; Sbf = [b.sb("Sbf", [128, 4, 128], BF16) for _ in range(2)]
        Sr = RL(2)
        AD = b.sb("AD", [128, 4, 128], BF16); ADr = Res()
        ol = b.sb("ol", [128, 2, 512], F32); olr = RL(2)
        osum = b.sb("osum", [128, 512], F32); osr = Res()
        dA = b.ds(); dB = b.ds(); dC = b.ds(); dD = b.ds(); dE = b.ds(); dF = b.ds(); dG = b.ds(); dH = b.ds(); dS = b.dsw()

        wuqv = I["w_uq"][j].rearrange("(kc p) (h c) -> p kc h c", p=128, c=96)
        for kcq in range(2):
            b.dma(POOL, wuq[:, kcq, :, 0:32], wuqv[:, kcq, :, 64:96], [], [wr], dS)
            b.dma(POOL, wuq[:, kcq, :, 32:96], wuqv[:, kcq, :, 0:64], [], [wr], dS)
        wukv = I["w_ukv"][j].rearrange("p (h c) -> p h c", c=128)
        b.op(DVE, lambda e: e.memset(wkn[:], 0.0), [], [wr])
        b.dma(POOL, wkn[:, :, 32:96], wukv[:, :, 0:64], [], [wr], dS)
        b.dma(POOL, wv[:], wukv[:, :, 64:128], [], [wr], dS)
        for kcq in range(2):
            for o_, i_, sgn in ((0, 8, -1.0), (8, 0, 1.0), (16, 24, -1.0), (24, 16, 1.0)):
                b.op(DVE, lambda e: e.tensor_scalar(out=wuqR[:, kcq, :, o_:o_ + 8], in0=wuq[:, kcq, :, i_:i_ + 8], scalar1=sgn,
                                                    scalar2=None, op0=ALU.mult), [wr], [wr])
        b.dma(SP, qng[:], I["q_norm_gT"][j], [], [wr], dB)
        b.dma(SP, kvg[:], I["kv_norm_g"][j].partition_broadcast(128), [], [wr], dB)
        b.dma(POOL, rm[:], I["rmT"][:, :], [], [wr], dS)
        b.dma(SP, ldb[:], I["ldec"][0].partition_broadcast(128), [], [dr], dB)
        b.op(ACT, lambda e: e.activation(out=ldb[:], in_=ldb[:], func=AF.Exp), [dr], [dr])
        b.op(DVE, lambda e: e.tensor_scalar(out=ldb[:], in0=ldb[:], scalar1=-1.0, scalar2=None, op0=ALU.mult), [dr], [dr])
        for d_ in range(2):
            for h in range(4):
                lg = ldb[:, j * 8 + d_ * 4 + h:j * 8 + d_ * 4 + h + 1]
                b.op(ACT, lambda e: e.activation(out=etmp[:], in_=consts[:, 1 + 2 * d_, :], func=AF.Exp, scale=lg), [dr, cr], [dr])
                b.op(DVE, lambda e: e.scalar_tensor_tensor(out=DT[:, d_, h, :], in0=etmp[:], scalar=RS, in1=consts[:, 2 + 2 * d_, :],
                                                           op0=ALU.mult, op1=ALU.mult), [dr, cr], [dr])
                b.op(ACT, lambda e: e.activation(out=QD[:, d_, h, :], in_=consts[:, 6 + d_, :], func=AF.Exp, scale=lg), [dr, cr], [dr])
                b.op(ACT, lambda e: e.activation(out=KD[:, d_, h:h + 1], in_=consts[:, 5, 1 + 2 * d_:2 + 2 * d_], func=AF.Exp, scale=lg),
                     [dr, cr], [dr])
                b.op(ACT, lambda e: e.activation(out=CDc[:, d_, h:h + 1], in_=consts[:, 5, 4:5], func=AF.Exp, scale=lg), [dr, cr], [dr])
        b.op(DVE, lambda e: e.tensor_scalar(out=KD[:], in0=KD[:], scalar1=RS, scalar2=None, op0=ALU.mult), [dr], [dr])
        b.op(DVE, lambda e: e.memset(V[:, :, :, 64:65], 1.0), [], [Vr])

        winv = I["w_in_ab"][j].rearrange("(kc p) n -> p kc n", p=128)
        blocks = [(0, 416), (416, 512), (928, 512), (1440, 512), (1952, 512)]
        ui = [0]

        def load_blk(bi):
            s_ = ui[0] % 2
            ui[0] += 1
            c0, n = blocks[bi]
            b.dma(POOL, win[s_][:, :, 0:n], winv[:, :, c0:c0 + n], [], [winr[s_]], wind[s_])
            return s_

        def ret_step(d_, si, qT_, qdT_, kT_, kd_, v_, rd):
            ba = b.bank()
            for h in range(4):
                b.op(PE, lambda e: e.matmul(ps[ba][:, h * 128:(h + 1) * 128], lhsT=kT_[:, h, :], rhs=qT_[:, h, :], start=True, stop=True),
                     rd, [psr[ba]])
            b.op(DVE, lambda e: e.tensor_tensor(out=AD[:].rearrange("p h i -> p (h i)"), in0=ps[ba][:, :],
                                                in1=DT[:, d_].rearrange("p h i -> p (h i)"), op=ALU.mult), [psr[ba], dr], [ADr])
            bo = b.bank()
            for h in range(4):
                b.op(PE, lambda e: e.matmul(ps[bo][:, h * 128:(h + 1) * 128], lhsT=AD[:, h, :], rhs=v_[:, h, :], start=True, stop=False),
                     rd + [ADr], [psr[bo]])
                b.op(PE, lambda e: e.matmul(ps[bo][:, h * 128:(h + 1) * 128], lhsT=qdT_[:, h, :], rhs=Sbf[si][:, h, :], start=False, stop=True),
                     rd + [Sr[si]], [psr[bo]])
            bs_ = b.bank()
            for h in range(4):
                b.op(PE, lambda e: e.matmul(ps[bs_][:, h * 128:(h + 1) * 128], lhsT=kd_[:, h, :], rhs=v_[:, h, :], start=True, stop=True),
                     rd, [psr[bs_]])
            for h in range(4):
                b.op(DVE, lambda e: e.scalar_tensor_tensor(out=Sst[si][:, h, :], in0=Sst[si][:, h, :], scalar=CDc[:, d_, h:h + 1],
                                                           in1=ps[bs_][:, h * 128:(h + 1) * 128], op0=ALU.mult, op1=ALU.add),
                     [psr[bs_], dr, Sr[si]], [Sr[si]])
            b.op(ACT, lambda e: e.copy(out=Sbf[si][:], in_=Sst[si][:]), [Sr[si]], [Sr[si]])
            return bo

        def merge_tile(tt, bo_trail, o_lead_ap, o_lead_res, sg_ap, sg_res, mix_t, mix_res):
            b.op(DVE, lambda e: e.tensor_tensor(out=osum[:], in0=ps[bo_trail][:, :], in1=o_lead_ap, op=ALU.add),
                 [psr[bo_trail], o_lead_res], [osr])
            for h in range(4):
                b.op(ACT, lambda e: e.activation(out=etmp[:], in_=osum[:, h * 128:(h + 1) * 128], func=AF.Square,
                                                 accum_out=ss[:, 4 + h:5 + h]), [osr], [dr, ssr])
            b.op(DVE, lambda e: e.tensor_scalar(out=ss[:, 4:8], in0=ss[:, 4:8], scalar1=1.0 / 128, scalar2=EPS, op0=ALU.mult, op1=ALU.add),
                 [ssr], [ssr])
            b.op(ACT, lambda e: e.activation(out=ss[:, 4:8], in_=ss[:, 4:8], func=AF.Sqrt), [ssr], [ssr])
            b.op(DVE, lambda e: e.reciprocal(out=ss[:, 4:8], in_=ss[:, 4:8]), [ssr], [ssr])
            for h in range(4):
                b.op(DVE, lambda e: e.scalar_tensor_tensor(out=mix_t[:, 512 + h * 128:512 + (h + 1) * 128], in0=osum[:, h * 128:(h + 1) * 128],
                                                           scalar=ss[:, 4 + h:5 + h], in1=sg_ap[:, h * 128:(h + 1) * 128],
                                                           op0=ALU.mult, op1=ALU.mult), [osr, ssr, sg_res], [mix_res])

        S["ret_step_defs"] = None
        b.dma(SP, Sst[1][:], I["s0_lead"][j].rearrange("h d e -> d h e"), [], [Sr[1]], dC)
        b.op(ACT, lambda e: e.copy(out=Sbf[1][:], in_=Sst[1][:]), [Sr[1]], [Sr[1]])

        for g in range(12):
            t0 = 2 * g
            prompt = g < 4
            norm_group(t0, 2, l, 0, hT, hTr, xs4, xs4r, ss, ssr)
            s_ = load_blk(0)
            bT = b.bank()
            bC = b.bank()
            for tt in range(2):
                t = t0 + tt
                bk = b.bank()
                for kc in range(8):
                    b.op(PE, lambda e: e.matmul(ps[bk][:, 0:416], lhsT=hT[:, kc, tt * 128:(tt + 1) * 128], rhs=win[s_][:, kc, 0:416],
                                                start=(kc == 0), stop=(kc == 7)), [hTr, winr[s_]], [psr[bk]])
                b.op(ACT, lambda e: e.activation(out=cqn[:], in_=ps[bk][:, 0:256], func=AF.Square, accum_out=ss[:, 0:1]), [psr[bk]], [pr[0], ssr])
                b.op(ACT, lambda e: e.activation(out=ckvn[:, tt, :], in_=ps[bk][:, 256:384], func=AF.Square, accum_out=ss[:, 1:2]),
                     [psr[bk]], [pr[1], ssr])
                b.op(DVE, lambda e: e.tensor_scalar(out=ss[:, 0:1], in0=ss[:, 0:1], scalar1=1.0 / 256, scalar2=EPS, op0=ALU.mult, op1=ALU.add),
                     [ssr], [ssr])
                b.op(DVE, lambda e: e.tensor_scalar(out=ss[:, 1:2], in0=ss[:, 1:2], scalar1=1.0 / 128, scalar2=EPS, op0=ALU.mult, op1=ALU.add),
                     [ssr], [ssr])
                b.op(ACT, lambda e: e.activation(out=ss[:, 0:2], in_=ss[:, 0:2], func=AF.Sqrt), [ssr], [ssr])
                b.op(DVE, lambda e: e.reciprocal(out=ss[:, 0:2], in_=ss[:, 0:2]), [ssr], [ssr])
                b.op(DVE, lambda e: e.tensor_scalar(out=cqn[:], in0=ps[bk][:, 0:256], scalar1=ss[:, 0:1], scalar2=None, op0=ALU.mult),
                     [psr[bk], ssr], [pr[0]])
                b.op(DVE, lambda e: e.scalar_tensor_tensor(out=ckvn[:, tt, :], in0=ps[bk][:, 256:384], scalar=ss[:, 1:2], in1=kvg[:],
                                                           op0=ALU.mult, op1=ALU.mult), [psr[bk], ssr, wr], [pr[1]])
                b.op(ACT, lambda e: e.copy(out=krs[:, tt, :], in_=ps[bk][:, 384:416]), [psr[bk]], [pr[1]])
                for kcq in range(2):
                    b.op(PE, lambda e: e.transpose(out=ps[bT][:, kcq * W + tt * 128:kcq * W + (tt + 1) * 128],
                                                   in_=cqn[:, kcq * 128:(kcq + 1) * 128], identity=ident), [pr[0], cr], [psr[bT]])
                b.op(PE, lambda e: e.transpose(out=ps[bC][:, tt * 128:(tt + 1) * 128], in_=ckvn[:, tt, :], identity=ident),
                     [pr[1], cr], [psr[bC]])
                b.op(PE, lambda e: e.transpose(out=ps[bC][0:32, W + tt * 128:W + (tt + 1) * 128], in_=krs[:, tt, :], identity=ident),
                     [pr[1], cr], [psr[bC]])
            if prompt:
                b.dma(SP, O["o_ckv"][j, t0 * 128:(t0 + 2) * 128, :].rearrange("(t p) c -> p t c", p=128), ckvn[:], [pr[1]], [], dD)
                b.dma(SP, O["o_kr"][j, t0 * 128:(t0 + 2) * 128, :].rearrange("(t p) c -> p t c", p=128), krs[:], [pr[1]], [], dE)
            for kcq in range(2):
                b.op(ACT, lambda e: e.activation(out=cqnT[:, kcq, :], in_=ps[bT][:, kcq * W:(kcq + 1) * W], func=AF.Identity,
                                                 scale=qng[:, kcq:kcq + 1]), [psr[bT], wr], [tr])
            b.op(DVE, lambda e: e.tensor_copy(out=ckvnT[:], in_=ps[bC][:, 0:W]), [psr[bC]], [tr])
            b.op(DVE, lambda e: e.tensor_copy(out=krTf[:], in_=ps[bC][0:32, W:2 * W]), [psr[bC]], [tr])
            if STAGE < 2:
                continue
            if prompt:
                b.op(ACT, lambda e: e.copy(out=krT[:], in_=ps[bC][0:32, W:2 * W]), [psr[bC]], [tr])
            else:
                tok0 = (t0 - 8) * 128
                for rep_ in range(2):
                    b.dma(SP, cs2[:, :, rep_, :], I["ropecs"][:, :, tok0:tok0 + W].rearrange("a p w -> p a w"), [], [csr], csd)
                bk = b.bank()
                b.op(ACT, lambda e: e.copy(out=rhi[:], in_=krTf[:]), [tr], [rhr])
                b.op(DVE, lambda e: e.tensor_tensor(out=rlo[:], in0=krTf[:], in1=rhi[:], op=ALU.subtract), [tr, rhr], [rhr])
                b.op(PE, lambda e: e.matmul(ps[bk][0:32, 0:W], lhsT=rm[:], rhs=rhi[:], start=True, stop=False), [wr, rhr], [psr[bk]])
                b.op(PE, lambda e: e.matmul(ps[bk][0:32, 0:W], lhsT=rm[:], rhs=rlo[:], start=False, stop=True), [wr, rhr], [psr[bk]])
                b.op(DVE, lambda e: e.tensor_tensor(out=rt1[:, 0:W], in0=ps[bk][0:32, 0:W], in1=cs2[:, 1, 0, :], op=ALU.mult), [psr[bk], csr], [rtr])
                b.op(DVE, lambda e: e.tensor_tensor(out=rt2[:, 0:W], in0=krTf[:], in1=cs2[:, 0, 0, :], op=ALU.mult), [tr, csr], [rtr])
                b.op(DVE, lambda e: e.tensor_tensor(out=krT[:], in0=rt1[:, 0:W], in1=rt2[:, 0:W], op=ALU.add), [rtr], [tr])
                b.dma(SP, cc1_in[0:128, tok0:tok0 + W], ckvnT[:], [tr], [cc1i_r], dD)
                b.dma(SP, cc1_in[128:160, tok0:tok0 + W], krT[:], [tr], [cc1i_r], dE)
            if STAGE < 2.2:
                continue
            for hp in range(4):
                bk = b.bank()
                for hh in range(2):
                    h = 2 * hp + hh
                    for kcq in range(2):
                        b.op(PE, lambda e: e.matmul(ps[bk][0:96, hh * W:(hh + 1) * W], lhsT=wuq[:, kcq, h, :], rhs=cqnT[:, kcq, :],
                                                    start=(kcq == 0), stop=(kcq == 1)), [wr, tr], [psr[bk]])
                copy_alt(hp, qT[:, 2 * hp:2 * hp + 2, :], ps[bk][0:96, :].rearrange("p (h w) -> p h w", w=W), [psr[bk]], [qTr])
                if not prompt and STAGE >= 2.4:
                    bk2 = b.bank()
                    for hh in range(2):
                        h = 2 * hp + hh
                        for kcq in range(2):
                            b.op(PE, lambda e: e.matmul(ps[bk2][0:32, hh * W:(hh + 1) * W], lhsT=wuqR[:, kcq, h, :], rhs=cqnT[:, kcq, :],
                                                        start=(kcq == 0), stop=(kcq == 1)), [wr, tr], [psr[bk2]])
                    b.op(DVE, lambda e: e.tensor_tensor(out=rt1[:], in0=ps[bk2][0:32, :], in1=cs2[:, 1].rearrange("p r w -> p (r w)"),
                                                        op=ALU.mult), [psr[bk2], csr], [rtr])
                    b.op(DVE, lambda e: e.tensor_tensor(out=rt2[:], in0=ps[bk][0:32, :], in1=cs2[:, 0].rearrange("p r w -> p (r w)"),
                                                        op=ALU.mult), [psr[bk], csr], [rtr])
                    b.op(DVE, lambda e: e.tensor_tensor(out=KT[0:32, 2 * hp:2 * hp + 2, :].rearrange("p h w -> p (h w)"), in0=rt1[:], in1=rt2[:],
                                                        op=ALU.add), [rtr], [KTr])
            if not prompt and STAGE >= 2.6:
                b.dma(SP, qscr[:, 0, :, tok0:tok0 + W].rearrange("h r w -> r h w"), qT[:], [qTr], [qscr_r], dF)
                b.dma(SP, qscr[:, 1, 32:96, tok0:tok0 + W].rearrange("h r w -> r h w"), qT[32:96, :, :], [qTr], [qscr_r], dG)
                b.dma(SP, qscr[:, 1, 0:32, tok0:tok0 + W].rearrange("h r w -> r h w"), KT[0:32, :, :], [KTr], [qscr_r], dA)
            if STAGE < 3:
                continue
            s_ = load_blk(1)
            for hp in range(2):
                bk = b.bank()
                for hh in range(2):
                    h = 2 * hp + hh
                    for kc in range(8):
                        b.op(PE, lambda e: e.matmul(ps[bk][:, hh * W:(hh + 1) * W], lhsT=win[s_][:, kc, h * 128:(h + 1) * 128], rhs=hT[:, kc, :],
                                                    start=(kc == 0), stop=(kc == 7)), [winr[s_], hTr], [psr[bk]])
                for hh in range(2):
                    h = 2 * hp + hh
                    for tt in range(2):
                        src = ps[bk][:, hh * W + tt * 128:hh * W + (tt + 1) * 128]
                        b.op(ACT, lambda e: e.copy(out=rop[:, tt, 0, h, :], in_=src), [psr[bk]], [ropr[tt]])
                        b.op(DVE, lambda e: e.tensor_tensor(out=rop[:, tt, 1, h, :], in0=src, in1=QD[:, 1, h, :], op=ALU.mult),
                             [psr[bk], dr], [ropr[tt]])
                        b.op(DVE, lambda e: e.tensor_tensor(out=qdl[:, tt, h, :], in0=src, in1=QD[:, 0, h, :], op=ALU.mult),
                             [psr[bk], dr], [rlr[tt]])
            s_ = load_blk(2)
            for hp in range(2):
                bk = b.bank()
                for hh in range(2):
                    h = 2 * hp + hh
                    for kc in range(8):
                        b.op(PE, lambda e: e.matmul(ps[bk][:, hh * W:(hh + 1) * W], lhsT=win[s_][:, kc, h * 128:(h + 1) * 128], rhs=hT[:, kc, :],
                                                    start=(kc == 0), stop=(kc == 7)), [winr[s_], hTr], [psr[bk]])
                for tt in range(2):
                    copy_alt(tt, rop[:, tt, 2, 2 * hp:2 * hp + 2, :],
                             ps[bk][:, :].rearrange("p (h t i) -> p h t i", h=2, t=2)[:, :, tt, :], [psr[bk]], [ropr[tt]])
            for tt in range(2):
                bk = b.bank()
                for kc in range(8):
                    b.op(PE, lambda e: e.matmul(ps[bk][:, :], lhsT=hT[:, kc, tt * 128:(tt + 1) * 128], rhs=win[s_][:, kc, :],
                                                start=(kc == 0), stop=(kc == 7)), [winr[s_], hTr], [psr[bk]])
                for h in range(4):
                    b.op(ACT, lambda e: e.activation(out=rop[:, tt, 3, h, :], in_=ps[bk][:, h * 128:(h + 1) * 128], func=AF.Identity,
                                                     scale=KD[:, 1, h:h + 1]), [psr[bk], dr], [ropr[tt]])
                    b.op(ACT, lambda e: e.activation(out=kdl[:, tt, h, :], in_=ps[bk][:, h * 128:(h + 1) * 128], func=AF.Identity,
                                                     scale=KD[:, 0, h:h + 1]), [psr[bk], dr], [rlr[tt]])
            s_ = load_blk(3)
            for tt in range(2):
                bk = b.bank()
                for kc in range(8):
                    b.op(PE, lambda e: e.matmul(ps[bk][:, :], lhsT=hT[:, kc, tt * 128:(tt + 1) * 128], rhs=win[s_][:, kc, :],
                                                start=(kc == 0), stop=(kc == 7)), [winr[s_], hTr], [psr[bk]])
                copy_alt(tt, rop[:, tt, 4, :, :].rearrange("p h e -> p (h e)"), ps[bk][:, :], [psr[bk]], [ropr[tt]])
            s_ = load_blk(4)
            for tt in range(2):
                bk = b.bank()
                for kc in range(8):
                    b.op(PE, lambda e: e.matmul(ps[bk][:, :], lhsT=hT[:, kc, tt * 128:(tt + 1) * 128], rhs=win[s_][:, kc, :],
                                                start=(kc == 0), stop=(kc == 7)), [winr[s_], hTr], [psr[bk]])
                b.op(ACT, lambda e: e.activation(out=sgt[:, tt, :], in_=ps[bk][:, :], func=AF.Silu), [psr[bk]], [sgr[tt]])

            if STAGE < 4:
                continue
            if prompt:
                seq = g
                for hp in range(4):
                    bk = b.bank()
                    for hh in range(2):
                        h = 2 * hp + hh
                        b.op(PE, lambda e: e.matmul(ps[bk][0:96, hh * W:(hh + 1) * W], lhsT=wkn[:, h, :], rhs=ckvnT[:], start=True, stop=True),
                             [wr, tr], [psr[bk]])
                    for r0 in (32, 64):
                        copy_alt(hp + r0 // 32, KT[r0:r0 + 32, 2 * hp:2 * hp + 2, :], ps[bk][r0:r0 + 32, :].rearrange("p (h w) -> p h w", w=W),
                                 [psr[bk]], [KTr])
                for h in range(8):
                    copy_alt(h, KT[0:32, h, :], krT[:], [tr], [KTr])
                for kt in range(2):
                    bk = b.bank()
                    b.op(PE, lambda e: e.matmul(ps[bk][:, :], lhsT=ckvnT[:, kt * 128:(kt + 1) * 128], rhs=wv[:].rearrange("p h c -> p (h c)"),
                                                start=True, stop=True), [wr, tr], [psr[bk]])
                    copy_alt(kt, V[:, kt, :, 0:64], ps[bk][:, :].rearrange("p (h c) -> p h c", c=64), [psr[bk]], [Vr])
                pi = 0
                for h in range(8):
                    bo = b.bank()
                    for kt in range(2):
                        bk = b.bank()
                        b.op(PE, lambda e: e.matmul(ps[bk][:, 0:W], lhsT=KT[:, h, kt * 128:(kt + 1) * 128], rhs=qT[:, h, :], start=True, stop=True),
                             [KTr, qTr], [psr[bk]])
                        k = pi % 2
                        pi += 1
                        b.op(ACT, lambda e: e.activation(out=PT[k][:], in_=ps[bk][:, 0:W], func=AF.Exp, scale=SCALE), [psr[bk]], [PTr[k]])
                        for qi in range(2):
                            b.op(PE, lambda e: e.matmul(ps[bo][:, qi * 65:(qi + 1) * 65], lhsT=PT[k][:, qi * 128:(qi + 1) * 128], rhs=V[:, kt, h, :],
                                                        start=(kt == 0 and qi == 0), stop=(kt == 1), skip_group_check=True),
                                 [PTr[k], Vr], [psr[bo]])
                    b.op(DVE, lambda e: e.reciprocal(out=rc[:], in_=ps[bo][:, 0:130].rearrange("p (q c) -> p q c", c=65)[:, :, 64]),
                         [psr[bo]], [rcr])
                    for qi in range(2):
                        b.op(ACT, lambda e: e.activation(out=mix[:, qi, h * 64:(h + 1) * 64], in_=ps[bo][:, qi * 65:qi * 65 + 64], func=AF.Identity,
                                                         scale=rc[:, qi:qi + 1]), [psr[bo], rcr], [mixr[qi]])
                if STAGE < 5:
                    continue
                b.op(DVE, lambda e: e.memset(Sst[0][:], 0.0), [], [Sr[0]])
                b.op(DVE, lambda e: e.memset(Sbf[0][:], 0.0), [], [Sr[0]])
                for tt in range(2):
                    bo = ret_step(0, 0, rop[:, tt, 0], qdl[:, tt], rop[:, tt, 2], kdl[:, tt], rop[:, tt, 4], [ropr[tt], rlr[tt]])
                    b.op(ACT, lambda e: e.copy(out=ol[:, tt, :], in_=ps[bo][:, :]), [psr[bo]], [olr[tt]])
                b.dma(SP, O["o_state"][seq, j, 0].rearrange("h d e -> d h e"), Sst[0][:], [Sr[0]], [], dC)
                b.op(DVE, lambda e: e.memset(Sst[0][:], 0.0), [], [Sr[0]])
                b.op(DVE, lambda e: e.memset(Sbf[0][:], 0.0), [], [Sr[0]])
                for tt in (1, 0):
                    bo = ret_step(1, 0, rop[:, tt, 0], rop[:, tt, 1], rop[:, tt, 2], rop[:, tt, 3], rop[:, tt, 4], [ropr[tt]])
                    merge_tile(tt, bo, ol[:, tt, :], olr[tt], sgt[:, tt, :], sgr[tt], mix[:, tt, :], mixr[tt])
                b.dma(SP, O["o_state"][seq, j, 1].rearrange("h d e -> d h e"), Sst[0][:], [Sr[0]], [], dC)
                for tt in range(2):
                    b.dma(SP, mscr[t0 + tt], mix[:, tt, :], [mixr[tt]], [mscr_r[t0 + tt]], dH)
            elif STAGE >= 5:
                for tt in range(2):
                    ti = t0 - 8 + tt
                    bo = ret_step(0, 1, rop[:, tt, 0], qdl[:, tt], rop[:, tt, 2], kdl[:, tt], rop[:, tt, 4], [ropr[tt], rlr[tt]])
                    b.op(ACT, lambda e: e.copy(out=ol[:, tt, :], in_=ps[bo][:, :]), [psr[bo]], [olr[tt]])
                    b.dma(SP, oscr[ti], ol[:, tt, :], [olr[tt]], [oscr_r[ti]], dB)
                    b.dma(SP, rscr[ti], rop[:, tt], [ropr[tt]], [rscr_r[ti]], dC)
                    b.dma(SP, gscr2[ti], sgt[:, tt, :], [sgr[tt]], [gscr2_r[ti]], csd)
        b.dma(SP, cc2_in.ap().rearrange("(h d) e -> d h e", d=128), Sst[1][:], [Sr[1]], [cc2i_r], dC)
        if STAGE >= 6:
            b.allgather(cc1_in, cc1_out, cc1i_r, cc1o_r)
            b.allgather(cc2_in, cc2_out, cc2i_r, cc2o_r)
    if STAGE < 7:
        return

    with b.phase():
        NK = 36
        cT = b.sb("cT", [128, NK * 128], BF16); kR = b.sb("kR", [32, NK * 128], BF16); cTr = Res()
        ctxf = b.sb("ctxf", [128, 4, 160], F32); ctxr = Res()
        wkn = b.sb("wkn", [128, 8, 96], BF16); wv = b.sb("wv", [128, 8, 64], BF16); wr = Res()
        KT = [b.sb("KT", [96, NK * 128], BF16) for _ in range(2)]; KTr = RL(2)
        V = [b.sb("V", [128, NK, 65], BF16) for _ in range(2)]; Vr = RL(2)
        qh = [b.sb("qh", [96, 2, 512], BF16) for _ in range(2)]; qhr = RL(2); qhd = [b.ds() for _ in range(2)]
        PT = [b.sb("PT", [128, 512], BF16) for _ in range(4)]; PTr = RL(4)
        rc = b.sb("rc", [128, 4], F32); rcr = Res()
        ao = [b.sb("ao", [128, 4, 64], F32) for _ in range(2)]; aor = RL(2); aod = [b.ds() for _ in range(2)]
        oT = [b.sb("oT", [65, 512], F32) for _ in range(2)]; oTr = RL(2)
        dA = b.ds(); dB = b.ds(); dS = b.dsw()
        wukv = I["w_ukv"][j].rearrange("p (h c) -> p h c", c=128)
        b.op(DVE, lambda e: e.memset(wkn[:], 0.0), [], [wr])
        b.dma(POOL, wkn[:, :, 32:96], wukv[:, :, 0:64], [], [wr], dS)
        b.dma(POOL, wv[:], wukv[:, :, 64:128], [], [wr], dS)
        b.dma(SP, ctxf[:, :, 0:128], I["ctx_ckv"][j].rearrange("(t p) c -> p t c", p=128), [], [ctxr], dB)
        b.dma(SP, ctxf[:, :, 128:160], I["ctx_kr"][j].rearrange("(t p) c -> p t c", p=128), [], [ctxr], dB)
        bk = b.bank(); bk2 = b.bank()
        for kt in range(4):
            b.op(PE, lambda e: e.transpose(out=ps[bk][:, kt * 128:(kt + 1) * 128], in_=ctxf[:, kt, 0:128], identity=ident), [ctxr, cr], [psr[bk]])
            b.op(PE, lambda e: e.transpose(out=ps[bk2][0:32, kt * 128:(kt + 1) * 128], in_=ctxf[:, kt, 128:160], identity=ident),
                 [ctxr, cr], [psr[bk2]])
        b.op(ACT, lambda e: e.copy(out=cT[:, 0:512], in_=ps[bk][:, :]), [psr[bk]], [cTr])
        b.op(DVE, lambda e: e.tensor_copy(out=kR[:, 0:512], in_=ps[bk2][0:32, :]), [psr[bk2]], [cTr])
        for r_ in range(2):
            b.dma(SP, cT[:, 512 + r_ * 2048:512 + (r_ + 1) * 2048], cc1_out[r_ * 160:r_ * 160 + 128, :], [cc1o_r], [cTr], dA)
            b.dma(SP, kR[:, 512 + r_ * 2048:512 + (r_ + 1) * 2048], cc1_out[r_ * 160 + 128:r_ * 160 + 160, :], [cc1o_r], [cTr], dB)
        for s_ in range(2):
            b.op(DVE, lambda e: e.memset(V[s_][:, :, 64:65], 1.0), [], [Vr[s_]])
        sbanks = [0, 1, 2, 3, 4, 5]
        sbi = [0]

        def sbank():
            i = sbanks[sbi[0] % 6]
            sbi[0] += 1
            return i
        obanks = [6, 7]
        oi = 0
        pi = 0
        qi_ = 0
        def build_kv(h):
            s_ = h % 2
            for kb in range(9):
                bk = sbank()
                b.op(PE, lambda e: e.matmul(ps[bk][0:96, :], lhsT=wkn[:, h, :], rhs=cT[:, kb * 512:(kb + 1) * 512], start=True, stop=True),
                     [wr, cTr], [psr[bk]])
                for r0 in (32, 64):
                    b.op(DVE, lambda e: e.tensor_copy(out=KT[s_][r0:r0 + 32, kb * 512:(kb + 1) * 512], in_=ps[bk][r0:r0 + 32, :]),
                         [psr[bk]], [KTr[s_]])
            b.op(DVE, lambda e: e.tensor_copy(out=KT[s_][0:32, :], in_=kR[:]), [cTr], [KTr[s_]])
            for k8 in range(5):
                bk = sbank()
                nkt = min(8, NK - 8 * k8)
                for q in range(nkt):
                    kt = 8 * k8 + q
                    b.op(PE, lambda e: e.matmul(ps[bk][:, q * 64:(q + 1) * 64], lhsT=cT[:, kt * 128:(kt + 1) * 128], rhs=wv[:, h, :],
                                                start=True, stop=True), [wr, cTr], [psr[bk]])
                b.op(DVE, lambda e: e.tensor_copy(out=V[s_][:, 8 * k8:8 * k8 + nkt, 0:64],
                                                  in_=ps[bk][:, 0:nkt * 64].rearrange("p (q c) -> p q c", c=64)), [psr[bk]], [Vr[s_]])

        pending = []

        def make_epilogue(bo_, a_, h_, qb_):
            def run():
                b.op(DVE, lambda e: e.tensor_copy(out=oT[a_][:], in_=ps[bo_][0:65, :]), [psr[bo_]], [oTr[a_]])
                bt = sbank()
                for qi in range(4):
                    b.op(PE, lambda e: e.transpose(out=ps[bt][:, qi * 65:(qi + 1) * 65], in_=oT[a_][:, qi * 128:(qi + 1) * 128],
                                                   identity=ident[0:65, 0:65]), [oTr[a_], cr], [psr[bt]])
                b.op(DVE, lambda e: e.reciprocal(out=rc[:], in_=ps[bt][:, 0:260].rearrange("p (q c) -> p q c", c=65)[:, :, 64]),
                     [psr[bt]], [rcr])
                for qi in range(4):
                    b.op(DVE, lambda e: e.tensor_scalar(out=ao[a_][:, qi, :], in0=ps[bt][:, qi * 65:qi * 65 + 64], scalar1=rc[:, qi:qi + 1],
                                                        scalar2=None, op0=ALU.mult), [psr[bt], rcr], [aor[a_]])
                b.dma(SP, ascr[qb_ * 4:qb_ * 4 + 4, :, h_ * 64:(h_ + 1) * 64].rearrange("t p c -> p t c"), ao[a_][:], [aor[a_]],
                      ascr_r[qb_ * 4:qb_ * 4 + 4], aod[a_])
            return run

        build_kv(0)
        for h in range(8):
            s_ = h % 2
            for qb in range(4):
                qs = qi_ % 2
                qi_ += 1
                b.dma(SP, qh[qs][:], qscr[h, :, :, qb * 512:(qb + 1) * 512].rearrange("v r w -> r v w"), [qscr_r], [qhr[qs]], qhd[qs])
                bo = obanks[oi % 2]
                oi += 1
                if qb == 1 and h + 1 < 8:
                    build_kv(h + 1)
                pk = [None] * NK
                SKEW = 2
                for kt in range(NK + SKEW):
                    if kt < NK:
                        ver = 0 if kt < 4 else 1
                        bk = sbank()
                        b.op(PE, lambda e: e.matmul(ps[bk][:, :], lhsT=KT[s_][:, kt * 128:(kt + 1) * 128], rhs=qh[qs][:, ver, :],
                                                    start=True, stop=True), [KTr[s_], qhr[qs]], [psr[bk]])
                        k = pi % 4
                        pi += 1
                        pk[kt] = k
                        b.op(ACT, lambda e: e.activation(out=PT[k][:], in_=ps[bk][:, :], func=AF.Exp, scale=SCALE), [psr[bk]], [PTr[k]])
                    if kt == SKEW and pending:
                        pending.pop(0)()
                    if kt >= SKEW:
                        kp = kt - SKEW
                        k = pk[kp]
                        b.op(PE, lambda e: e.matmul(ps[bo][0:65, :], lhsT=V[s_][:, kp, :], rhs=PT[k][:, :],
                                                    start=(kp == 0), stop=(kp == NK - 1)), [PTr[k], Vr[s_]], [psr[bo]])
                pending.append(make_epilogue(bo, (oi - 1) % 2, h, qb))
        for ep in pending:
            ep()
        pending.clear()

    if STAGE < 8:
        return
    with b.phase():
        ss = b.sb("ss", [128, 8], F32); ssr = Res()
        ldb = b.sb("ldb", [128, 16], F32)
        DT = b.sb("DT", [128, 2, 4, 128], BF16)
        CDc = b.sb("CD", [128, 2, 4], F32)
        etmp = b.sb("etmp", [128, 128], F32)
        dr = Res()
        wo = b.sb("wo", [128, 8, D], BF16); wor = Res()
        rop = [b.sb("rop", [128, 5, 4, 128], BF16) for _ in range(2)]; ropr = RL(2); ropd = [b.ds() for _ in range(2)]
        ol = [b.sb("ol", [128, 512], F32) for _ in range(2)]; olr = RL(2); old = [b.ds() for _ in range(2)]
        sgt = [b.sb("sgt", [128, 512], BF16) for _ in range(2)]; sgr = RL(2); sgd = [b.ds() for _ in range(2)]
        mix = [b.sb("mix", [128, D], F32) for _ in range(2)]; mixr = RL(2); mixd = [b.ds() for _ in range(2)]
        Sst = b.sb("Sst", [128, 4, 128], F32); Sbf = b.sb("Sbf", [128, 4, 128], BF16); Sr = Res()
        R0 = b.sb("R0", [128, 2, 4, 128], F32); R0r = Res()
        pw = b.sb("pw", [128, 2], F32)
        AD = b.sb("AD", [128, 4, 128], BF16); ADr = Res()
        osum = b.sb("osum", [128, 512], F32); osr = Res()
        mixT = b.sb("mixT", [128, 8, 128], BF16); mixTr = RL(8)
        tmp = [b.sb("tmp", [128, 512], F32) for _ in range(2)]; tmpr = RL(2)
        dA = b.ds(); dB = b.ds()
        b.dma(POOL, wo[:], I["w_o_ab"][j].rearrange("(kc p) n -> p kc n", p=128), [], [wor], b.dsw())
        b.dma(SP, ldb[:], I["ldec"][0].partition_broadcast(128), [], [dr], dA)
        b.dma(SP, pw[:], I["pw"][:, :], [], [dr], dA)
        b.op(ACT, lambda e: e.activation(out=ldb[:], in_=ldb[:], func=AF.Exp), [dr], [dr])
        b.op(DVE, lambda e: e.tensor_scalar(out=ldb[:], in0=ldb[:], scalar1=-1.0, scalar2=None, op0=ALU.mult), [dr], [dr])
        for h in range(4):
            lg = ldb[:, j * 8 + 4 + h:j * 8 + 4 + h + 1]
            b.op(ACT, lambda e: e.activation(out=etmp[:], in_=consts[:, 3, :], func=AF.Exp, scale=lg), [dr, cr], [dr])
            b.op(DVE, lambda e: e.scalar_tensor_tensor(out=DT[:, 1, h, :], in0=etmp[:], scalar=RS, in1=consts[:, 4, :],
                                                       op0=ALU.mult, op1=ALU.mult), [dr, cr], [dr])
            b.op(ACT, lambda e: e.activation(out=CDc[:, 1, h:h + 1], in_=consts[:, 5, 4:5], func=AF.Exp, scale=lg), [dr, cr], [dr])
        for r_ in range(2):
            b.dma(SP, R0[:, r_], cc2_out[r_ * 512:(r_ + 1) * 512, :].rearrange("(h d) e -> d h e", d=128), [cc2o_r], [R0r], dB)
        b.op(DVE, lambda e: e.tensor_scalar(out=Sst[:], in0=R0[:, 0], scalar1=pw[:, 0:1], scalar2=None, op0=ALU.mult), [R0r, dr], [Sr])
        b.op(DVE, lambda e: e.scalar_tensor_tensor(out=Sst[:], in0=R0[:, 1], scalar=pw[:, 1:2], in1=Sst[:], op0=ALU.mult, op1=ALU.add),
             [R0r, dr, Sr], [Sr])
        b.op(ACT, lambda e: e.copy(out=Sbf[:], in_=Sst[:]), [Sr], [Sr])
        def proj_tile(t, k):
            for kc in range(8):
                bk = b.bank()
                b.op(PE, lambda e: e.transpose(out=ps[bk][:, 0:128], in_=mix[k][:, kc * 128:(kc + 1) * 128], identity=ident),
                     [mixr[k], cr], [psr[bk]])
                copy_alt(kc, mixT[:, kc, :], ps[bk][:, 0:128], [psr[bk]], [mixTr[kc]])
            for half in range(2):
                bk = b.bank()
                for kc in range(8):
                    b.op(PE, lambda e: e.matmul(ps[bk][:, :], lhsT=mixT[:, kc, :], rhs=wo[:, kc, half * 512:(half + 1) * 512],
                                                start=(kc == 0), stop=(kc == 7)), [mixTr[kc], wor], [psr[bk]])
                resid_update(t, half, bk, 0, tmp[half], tmpr[half])

        for t in range(8):
            k = t % 2
            b.dma(SP, mix[k][:], mscr[t], [mscr_r[t]], [mixr[k]], mixd[k])
            proj_tile(t, k)
        for n_, ti in enumerate(range(15, -1, -1)):
            k = n_ % 2
            t = 8 + ti
            b.dma(SP, rop[k][:], rscr[ti], [rscr_r[ti]], [ropr[k]], ropd[k])
            b.dma(SP, ol[k][:], oscr[ti], [oscr_r[ti]], [olr[k]], old[k])
            b.dma(SP, sgt[k][:], gscr2[ti], [gscr2_r[ti]], [sgr[k]], sgd[k])
            b.dma(SP, mix[k][:, 0:512], ascr[ti], [ascr_r[ti]], [mixr[k]], mixd[k])
            rd = [ropr[k]]
            ba = b.bank()
            for h in range(4):
                b.op(PE, lambda e: e.matmul(ps[ba][:, h * 128:(h + 1) * 128], lhsT=rop[k][:, 2, h, :], rhs=rop[k][:, 0, h, :], start=True, stop=True),
                     rd, [psr[ba]])
            b.op(DVE, lambda e: e.tensor_tensor(out=AD[:].rearrange("p h i -> p (h i)"), in0=ps[ba][:, :],
                                                in1=DT[:, 1].rearrange("p h i -> p (h i)"), op=ALU.mult), [psr[ba], dr], [ADr])
            bo = b.bank()
            for h in range(4):
                b.op(PE, lambda e: e.matmul(ps[bo][:, h * 128:(h + 1) * 128], lhsT=AD[:, h, :], rhs=rop[k][:, 4, h, :], start=True, stop=False),
                     rd + [ADr], [psr[bo]])
                b.op(PE, lambda e: e.matmul(ps[bo][:, h * 128:(h + 1) * 128], lhsT=rop[k][:, 1, h, :], rhs=Sbf[:, h, :], start=False, stop=True),
                     rd + [Sr], [psr[bo]])
            bs_ = b.bank()
            for h in range(4):
                b.op(PE, lambda e: e.matmul(ps[bs_][:, h * 128:(h + 1) * 128], lhsT=rop[k][:, 3, h, :], rhs=rop[k][:, 4, h, :], start=True, stop=True),
                     rd, [psr[bs_]])
            for h in range(4):
                b.op(DVE, lambda e: e.scalar_tensor_tensor(out=Sst[:, h, :], in0=Sst[:, h, :], scalar=CDc[:, 1, h:h + 1],
                                                           in1=ps[bs_][:, h * 128:(h + 1) * 128], op0=ALU.mult, op1=ALU.add),
                     [psr[bs_], dr, Sr], [Sr])
            b.op(ACT, lambda e: e.copy(out=Sbf[:], in_=Sst[:]), [Sr], [Sr])
            b.op(DVE, lambda e: e.tensor_tensor(out=osum[:], in0=ps[bo][:, :], in1=ol[k][:], op=ALU.add), [psr[bo], olr[k]], [osr])
            for h in range(4):
                b.op(ACT, lambda e: e.activation(out=etmp[:], in_=osum[:, h * 128:(h + 1) * 128], func=AF.Square,
                                                 accum_out=ss[:, 4 + h:5 + h]), [osr], [dr, ssr])
            b.op(DVE, lambda e: e.tensor_scalar(out=ss[:, 4:8], in0=ss[:, 4:8], scalar1=1.0 / 128, scalar2=EPS, op0=ALU.mult, op1=ALU.add),
                 [ssr], [ssr])
            b.op(ACT, lambda e: e.activation(out=ss[:, 4:8], in_=ss[:, 4:8], func=AF.Sqrt), [ssr], [ssr])
            b.op(DVE, lambda e: e.reciprocal(out=ss[:, 4:8], in_=ss[:, 4:8]), [ssr], [ssr])
            for h in range(4):
                b.op(DVE, lambda e: e.scalar_tensor_tensor(out=mix[k][:, 512 + h * 128:512 + (h + 1) * 128], in0=osum[:, h * 128:(h + 1) * 128],
                                                           scalar=ss[:, 4 + h:5 + h], in1=sgt[k][:, h * 128:(h + 1) * 128],
                                                           op0=ALU.mult, op1=ALU.mult), [osr, ssr, sgr[k]], [mixr[k]])
            proj_tile(t, k)


_NC = None


def _rope_tables(pos):
    half = 16
    inv = 1.0 / np.power(np.float32(10000.0), np.arange(0, half, 2, dtype=np.float32) / np.float32(half))
    row = (pos // 64).astype(np.float32)
    col = (pos % 64).astype(np.float32)
    ang = np.stack([row[:, None] * inv, col[:, None] * inv], axis=1)
    ang = np.stack([ang, ang], axis=2).reshape(len(pos), 32)
    return np.cos(ang).astype(np.float32), np.sin(ang).astype(np.float32)


def _consts():
    c = np.zeros((128, 8, 128), np.float32)
    p = np.arange(128, dtype=np.float32)[:, None]
    f = np.arange(128, dtype=np.float32)[None, :]
    c[:, 0] = (p == f)
    c[:, 1] = np.maximum(f - p, 0)
    c[:, 2] = (f >= p)
    c[:, 3] = np.maximum(p - f, 0)
    c[:, 4] = (p >= f)
    c[:, 5, 0] = p[:, 0] + 1.0
    c[:, 5, 1] = 127.0 - p[:, 0]
    c[:, 5, 2] = 128.0 - p[:, 0]
    c[:, 5, 3] = p[:, 0]
    c[:, 5, 4] = 128.0
    c[:, 5, 5] = 1.0
    c[:, 6] = f + 1.0
    c[:, 7] = 128.0 - f
    return c


def _rmT():
    rm = np.zeros((32, 32), np.float32)
    for a in range(2):
        o = 16 * a
        for i in range(8):
            rm[o + 8 + i, o + i] = -1.0
            rm[o + i, o + 8 + i] = 1.0
    return rm


def kernel(x_prompt, x_sample, cache_mla_ckv, cache_mla_krope, state_ret, c, c_ctx,
           w_ada, b_ada, norm_g, w_in_ab, q_norm_g, w_uq, kv_norm_g, w_ukv, ret_log_decay, w_o_ab,
           w_in_c, ln_g_c, ln_b_c, w_s_c, b_s_c, w_out_c, w_ffn_gate, w_ffn_up, w_ffn_down, final_norm_g):
    global _NC
    f = lambda a: np.ascontiguousarray(np.asarray(a, dtype=np.float32))
    x_prompt, x_sample = f(x_prompt), f(x_sample)
    shared = {
        "w_ada": f(w_ada), "b_ada": f(b_ada),
        "b_adaT": f(np.asarray(b_ada).reshape(4, 48, 128).transpose(0, 2, 1)),
        "norm_gT": f(np.asarray(norm_g).reshape(4, 2, 8, 128).transpose(0, 1, 3, 2)),
        "w_in_ab": f(w_in_ab), "q_norm_gT": f(np.asarray(q_norm_g).reshape(2, 2, 128).transpose(0, 2, 1)),
        "w_uq": f(w_uq), "kv_norm_g": f(kv_norm_g), "w_ukv": f(w_ukv), "w_o_ab": f(w_o_ab),
        "w_in_c": f(w_in_c), "ln_g_c": f(ln_g_c), "ln_b_c": f(ln_b_c), "w_out_c": f(w_out_c),
        "w_ffn_gate": f(w_ffn_gate), "w_ffn_up": f(w_ffn_up), "w_ffn_down": f(w_ffn_down),
        "final_norm_g": f(final_norm_g), "consts": _consts(), "rmT": _rmT(),
    }
    ws = np.asarray(w_s_c, dtype=np.float32)
    bs = np.asarray(b_s_c, dtype=np.float32)
    wsT_nat = f(ws.transpose(0, 3, 1, 2))
    wsT_rev = f(ws[:, :, ::-1, ::-1].transpose(0, 3, 1, 2))
    bsT_nat = f(bs.transpose(0, 2, 1))
    bsT_rev = f(bs[:, :, ::-1].transpose(0, 2, 1))
    in_maps = []
    for r in range(8):
        p, par = r // 2, r % 2
        xp = x_prompt[4 * r:4 * r + 4]
        xs = x_sample[p, par * 2048:(par + 1) * 2048]
        pos = np.arange(par * 2048, (par + 1) * 2048)
        if par:
            xp = xp[:, ::-1]
            xs = xs[::-1]
            pos = pos[::-1]
        cos, sin = _rope_tables(pos)
        cond2 = np.stack([np.asarray(c_ctx, np.float32), np.asarray(c, np.float32)[p]])
        pw = np.zeros((128, 2), np.float32)
        pw[:, 1 - par] = 1.0
        ld = np.asarray(ret_log_decay, np.float32)[:, [par, 1 - par], :]
        m = dict(shared)
        m.update({
            "x_in": f(np.concatenate([xp.reshape(1024, D), xs], axis=0)),
            "cond2T": f(cond2.reshape(2, 8, 128).transpose(2, 1, 0)),
            "ctx_ckv": f(np.asarray(cache_mla_ckv)[p]), "ctx_kr": f(np.asarray(cache_mla_krope)[p]),
            "s0_lead": f(np.asarray(state_ret)[p, :, par]),
            "ldec": f(ld.reshape(1, 16)),
            "ropecs": f(np.stack([cos.T, sin.T])),
            "pw": pw,
            "wsT": wsT_rev if par else wsT_nat, "b_sT": bsT_rev if par else bsT_nat,
        })
        in_maps.append(m)
    if _NC is None:
        _NC = build_program()
    res = run_bass_kernel_spmd(_NC, in_maps, core_ids=list(range(8)))
    y_prompt = np.zeros((32, 256, D), np.float32)
    y_sample = np.zeros((4, 4096, D), np.float32)
    o_ckv = np.zeros((32, 2, 256, 128), np.float32)
    o_kr = np.zeros((32, 2, 256, 32), np.float32)
    o_st = np.zeros((32, 2, 2, 4, 128, 128), np.float32)
    for r in range(8):
        p, par = r // 2, r % 2
        o = res.results[r]
        yp = o["y"][:1024].reshape(4, 256, D)
        ys = o["y"][1024:]
        ck = o["o_ckv"].reshape(2, 4, 256, 128).transpose(1, 0, 2, 3)
        kr = o["o_kr"].reshape(2, 4, 256, 32).transpose(1, 0, 2, 3)
        stt = o["o_state"]
        if par:
            yp, ys, ck, kr = yp[:, ::-1], ys[::-1], ck[:, :, ::-1], kr[:, :, ::-1]
            stt = stt[:, :, ::-1]
        o_st[4 * r:4 * r + 4] = stt
        y_prompt[4 * r:4 * r + 4] = yp
        y_sample[p, par * 2048:(par + 1) * 2048] = ys
        o_ckv[4 * r:4 * r + 4] = ck
        o_kr[4 * r:4 * r + 4] = kr
    return (y_prompt, y_sample, o_ckv, o_kr, o_st)
```

```python
import contextlib
import os
import numpy as np
import concourse.bass as bass
import concourse.mybir as mybir
from concourse.bass_utils import run_bass_kernel_spmd

F32 = mybir.dt.float32
BF16 = mybir.dt.bfloat16
AF = mybir.ActivationFunctionType
ALU = mybir.AluOpType
AX = mybir.AxisListType

D = 1024
DEPTH = 4
DFF = 2816
NF = 22
NT = 24
NG = 6
EPS = 1e-6
ABIN = 2464
SCALE = 96 ** -0.5
WINDOW = 3


class Res:
    __slots__ = ("w", "r", "excl")

    def __init__(self, excl=False):
        self.w = None
        self.r = {}
        self.excl = excl


def RL(n):
    return [Res() for _ in range(n)]


class Eng:
    def __init__(self, key, eng, sems):
        self.key = key
        self.eng = eng
        self.sems = sems
        self.sem = sems[0]
        self.count = 0
        self.waited = {}


class DSem:
    def __init__(self, key, sem):
        self.key = key
        self.sem = sem
        self.total = 0


class Builder:
    def __init__(self, nc, es):
        self.nc = nc
        self.es = es
        mk = lambda n: es.enter_context(nc.semaphore(n))
        self.PE = Eng("pe", nc.tensor, [mk("s_pe0"), mk("s_pe1"), mk("s_pe2")])
        self.ACT = Eng("act", nc.scalar, [mk("s_act0"), mk("s_act1"), mk("s_act2")])
        self.DVE = Eng("dve", nc.vector, [mk("s_dve0"), mk("s_dve1"), mk("s_dve2")])
        self.POOL = Eng("pool", nc.gpsimd, [mk("s_pool0"), mk("s_pool1"), mk("s_pool2")])
        self.SP = Eng("sp", nc.sync, [mk("s_sp0"), mk("s_sp1"), mk("s_sp2")])
        self.engs = [self.PE, self.ACT, self.DVE, self.POOL, self.SP]
        self.ekeys = {E.key for E in self.engs}
        self.epoch = 0
        self.dsems = [DSem(f"d{i}", mk(f"s_d{i}")) for i in range(72)]
        self.dfree_hw = self.dsems[:44]
        self.dfree_sw = self.dsems[44:]
        self.cc_sem = mk("s_cc")
        self.cc_count = 0
        self.ps = [es.enter_context(nc.psum_tensor(f"ps{i}", [128, 512], F32)) for i in range(8)]
        self.psr = [Res(excl=True) for _ in range(8)]
        self.bank_i = 0
        self.uid = 0
        self.phase_es = None
        self.phase_ds = []

    def sb(self, name, shape, dtype, persistent=False):
        self.uid += 1
        stack = self.es if (persistent or self.phase_es is None) else self.phase_es
        return stack.enter_context(self.nc.sbuf_tensor(f"{name}_{self.uid}", shape, dtype))

    def ds(self, persistent=False, sw=False):
        pool = self.dfree_sw if sw else self.dfree_hw
        d = pool.pop()
        if not persistent and self.phase_es is not None:
            self.phase_ds.append((d, pool))
        return d

    def dsw(self):
        return self.ds(sw=True)

    def bank(self):
        i = self.bank_i
        self.bank_i = (i + 1) % 8
        return i

    @contextlib.contextmanager
    def phase(self):
        assert self.phase_es is None
        with contextlib.ExitStack() as pes:
            self.phase_es = pes
            self.phase_ds = []
            yield
            self.barrier()
            for d, pool in self.phase_ds:
                pool.append(d)
            self.phase_ds = []
            self.phase_es = None

    def _wait(self, E, deps):
        for key, (sem, val) in deps.items():
            if key == E.key:
                if E is self.PE or E is self.SP:
                    continue
                if val <= E.count - WINDOW:
                    continue
            if E.waited.get(key, 0) >= val:
                continue
            E.eng.wait_ge(sem, val)
            E.waited[key] = val

    def _collect(self, reads, writes):
        deps = {}

        def add(rec):
            if rec is None:
                return
            key, sem, val, ep = rec
            if ep is not None and ep < self.epoch:
                return
            if key not in deps or deps[key][1] < val:
                deps[key] = (sem, val)

        for r in reads:
            add(r.w)
            if r.excl:
                for rec in r.r.values():
                    add(rec)
        for w in writes:
            add(w.w)
            for rec in w.r.values():
                add(rec)
        return deps

    @staticmethod
    def _commit(rec, reads, writes):
        for w in writes:
            w.w = rec
            w.r = {}
        for r in reads:
            r.r[rec[0]] = rec

    def op(self, E, fn, reads=(), writes=()):
        self._wait(E, self._collect(reads, writes))
        ins = fn(E.eng)
        E.count += 1
        ins.then_inc(E.sem, 1)
        self._commit((E.key, E.sem, E.count, self.epoch), reads, writes)

    def dma(self, Q, out, in_, reads, writes, d):
        assert (Q is self.POOL) == (d in self.dsems[44:]), "SW/HW DMA semaphore pools must not mix"
        deps = self._collect(reads, writes)
        if d.total:
            deps[d.key] = (d.sem, d.total)
        self._wait(Q, deps)
        ins = Q.eng.dma_start(out=out, in_=in_)
        d.total += 16
        ins.then_inc(d.sem, 16)
        self._commit((d.key, d.sem, d.total, None), reads, writes)

    def barrier(self):
        for E in self.engs:
            deps = {}
            for o in self.engs:
                if o is not E and o.count:
                    deps[o.key] = (o.sem, o.count)
            for d in self.dsems:
                if d.total:
                    deps[d.key] = (d.sem, d.total)
            if self.cc_count:
                deps["cc"] = (self.cc_sem, self.cc_count)
            self._wait(E, deps)
        self.epoch += 1
        for E in self.engs:
            old = E.sem
            E.sem = E.sems[self.epoch % 3]
            E.count = 0
            for k in self.ekeys:
                E.waited.pop(k, None)
        for E in self.engs:
            E.eng.sem_clear(E.sems[(self.epoch + 1) % 3])

    def allgather(self, in_t, out_t, in_res, out_res):
        P = self.POOL
        self._wait(P, self._collect([in_res], [out_res]))
        ins = P.eng.collective_compute(
            "AllGather", ALU.bypass, replica_groups=[[0, 1], [2, 3], [4, 5], [6, 7]],
            ins=[in_t.ap().opt()], outs=[out_t.ap().opt()])
        self.cc_count += 1
        ins.then_inc(self.cc_sem, 1)
        self._commit(("cc", self.cc_sem, self.cc_count, None), [in_res], [out_res])


def build_program():
    nc = bass.Bass("TRN2", target_bir_lowering=False)

    def din(name, shape):
        return nc.dram_tensor(name, list(shape), F32, kind="ExternalInput").ap()

    def dout(name, shape):
        return nc.dram_tensor(name, list(shape), F32, kind="ExternalOutput").ap()

    I = {}
    for name, shape in [
        ("x_in", (3072, D)), ("cond2T", (128, 8, 2)), ("ctx_ckv", (2, 512, 128)), ("ctx_kr", (2, 512, 32)),
        ("s0_lead", (2, 4, 128, 128)), ("ldec", (1, 16)), ("ropecs", (2, 32, 2048)), ("pw", (128, 2)),
        ("w_ada", (4, D, 6 * D)), ("b_ada", (4, 6 * D)), ("b_adaT", (4, 128, 48)), ("norm_gT", (4, 2, 128, 8)),
        ("w_in_ab", (2, D, ABIN)), ("q_norm_gT", (2, 128, 2)), ("w_uq", (2, 256, 768)), ("kv_norm_g", (2, 128)),
        ("w_ukv", (2, 128, 1024)), ("w_o_ab", (2, D, D)), ("w_in_c", (2, D, 2 * D)), ("ln_g_c", (2, D)),
        ("ln_b_c", (2, D)), ("wsT", (2, 128, 8, 128)), ("b_sT", (2, 128, 8)), ("w_out_c", (2, D, D)),
        ("w_ffn_gate", (4, D, DFF)), ("w_ffn_up", (4, D, DFF)), ("w_ffn_down", (4, DFF, D)),
        ("final_norm_g", (D,)), ("consts", (128, 8, 128)), ("rmT", (32, 32)),
    ]:
        I[name] = din(name, shape)
    O = {
        "y": dout("y", (3072, D)), "o_ckv": dout("o_ckv", (2, 1024, 128)), "o_kr": dout("o_kr", (2, 1024, 32)),
        "o_state": dout("o_state", (4, 2, 2, 4, 128, 128)),
    }
    gscr = nc.dram_tensor("gscr", [4, 2, 2, D], F32)
    gscr_r = Res()
    cc1_in = nc.dram_tensor("cc1_in", [160, 2048], BF16)
    cc1_out = nc.dram_tensor("cc1_out", [320, 2048], BF16)
    cc2_in = nc.dram_tensor("cc2_in", [512, 128], F32)
    cc2_out = nc.dram_tensor("cc2_out", [1024, 128], F32)
    qscr = nc.dram_tensor("qscr", [8, 2, 96, 2048], BF16)
    rscr = nc.dram_tensor("rscr", [16, 128, 5, 4, 128], BF16)
    oscr = nc.dram_tensor("oscr", [16, 128, 512], F32)
    gscr2 = nc.dram_tensor("gscr2", [16, 128, 512], BF16)
    ascr = nc.dram_tensor("ascr", [16, 128, 512], F32)
    mscr = nc.dram_tensor("mscr", [8, 128, D], F32)
    wc_gu = nc.dram_tensor("wc_gu", [11, 2, 128, 8 * 256], BF16)
    wc_d = nc.dram_tensor("wc_d", [2, 11, 128, 2 * 512], BF16)
    wc_gu_r = [[Res(), Res()] for _ in range(11)]
    wc_d_r = [[Res() for _ in range(11)] for _ in range(2)]

    with contextlib.ExitStack() as es:
        b = Builder(nc, es)
        PE, ACT, DVE, POOL, SP = b.PE, b.ACT, b.DVE, b.POOL, b.SP
        ps, psr = b.ps, b.psr

        x = b.sb("x", [128, NT, D], F32, True)
        xr = RL(NT)
        consts = b.sb("consts", [128, 8, 128], F32, True)
        cr = Res()
        modc = b.sb("modc", [128, 4, 2, 4, 8], F32, True)
        modr = RL(4)
        gb = b.sb("gb", [128, 2, D], F32, True)
        gbr = Res()
        ident = consts[:, 0, :]

        d0 = b.ds(True)
        b.dma(SP, consts[:], I["consts"][:, :, :], [], [cr], d0)
        xin = I["x_in"].rearrange("(t p) d -> p t d", p=128)
        xds = [b.ds(True) for _ in range(NG)]
        for g in range(NG):
            b.dma(SP, x[:, 4 * g:4 * g + 4, :], xin[:, 4 * g:4 * g + 4, :], [], xr[4 * g:4 * g + 4], xds[g])

        def cond_of(t):
            return 0 if t < 8 else 1

        def rstd_from_ss(ss, n, inv_n, rs_res):
            b.op(DVE, lambda e: e.tensor_scalar(out=ss[:, :n], in0=ss[:, :n], scalar1=inv_n, scalar2=EPS,
                                               op0=ALU.mult, op1=ALU.add), [rs_res], [rs_res])
            b.op(ACT, lambda e: e.activation(out=ss[:, :n], in_=ss[:, :n], func=AF.Sqrt), [rs_res], [rs_res])
            b.op(DVE, lambda e: e.reciprocal(out=ss[:, :n], in_=ss[:, :n]), [rs_res], [rs_res])

        def norm_group(t0, n, l, which, hT, hTres, xs4, xs4r, ss, ssr):
            c = cond_of(t0)
            for tt in range(n):
                t = t0 + tt
                b.op(ACT, lambda e: e.activation(out=xs4[:, tt, :], in_=x[:, t, :], func=AF.Square,
                                                 accum_out=ss[:, tt:tt + 1]), [xr[t]], [xs4r[tt], ssr])
            rstd_from_ss(ss, n, 1.0 / D, ssr)
            for tt in range(n):
                t = t0 + tt
                b.op(DVE, lambda e: e.tensor_scalar(out=xs4[:, tt, :], in0=x[:, t, :], scalar1=ss[:, tt:tt + 1],
                                                    scalar2=None, op0=ALU.mult), [xr[t], ssr], [xs4r[tt]])
            for kc in range(8):
                bk = b.bank()
                for tt in range(n):
                    b.op(PE, lambda e: e.transpose(out=ps[bk][:, tt * 128:(tt + 1) * 128],
                                                   in_=xs4[:, tt, kc * 128:(kc + 1) * 128], identity=ident),
                         [xs4r[tt], cr], [psr[bk]])
                b.op(ACT, lambda e: e.activation(out=hT[:, kc, :n * 128], in_=ps[bk][:, :n * 128], func=AF.Identity,
                                                 scale=modc[:, l, c, 2 * which + 1, kc:kc + 1],
                                                 bias=modc[:, l, c, 2 * which, kc:kc + 1]),
                     [psr[bk], modr[l]], [hTres])

        def resid_update(t, half, bk, gate, tmp, tmpr):
            c = cond_of(t)
            sl = slice(half * 512, (half + 1) * 512)
            b.op(DVE, lambda e: e.tensor_tensor(out=tmp[:], in0=ps[bk][:, :], in1=gb[:, c, sl], op=ALU.mult),
                 [psr[bk], gbr], [tmpr])
            b.op(DVE, lambda e: e.tensor_tensor(out=x[:, t, sl], in0=x[:, t, sl], in1=tmp[:], op=ALU.add),
                 [tmpr, xr[t]], [xr[t]])

        with b.phase():
            scT = b.sb("scT", [128, 8, 2], BF16)
            scTf = b.sb("scTf", [128, 8, 2], F32)
            scr = Res()
            wsl = [b.sb("wada", [128, 8, 512], BF16) for _ in range(3)]
            wslr = RL(3)
            wsd = [b.dsw() for _ in range(3)]
            badT = b.sb("badT", [128, 4, 48], F32)
            ngT = b.sb("ngT", [128, 4, 2, 8], F32)
            brow = b.sb("brow", [2, 4, 2, D], F32)
            grow = b.sb("grow", [2, 4, 2, D], F32)
            miscr = Res()
            growr = Res()
            dd = b.ds()
            b.dma(SP, scTf[:], I["cond2T"][:, :, :], [], [scr], dd)
            b.dma(SP, badT[:], I["b_adaT"].rearrange("l p c -> p l c"), [], [miscr], dd)
            b.dma(SP, ngT[:], I["norm_gT"].rearrange("l w p k -> p l w k"), [], [miscr], dd)
            for j in range(2):
                for gi in range(2):
                    b.dma(SP, brow[j:j + 1, :, gi, :], I["b_ada"][None, :, (2 + 3 * gi) * D:(3 + 3 * gi) * D], [], [miscr], dd)
            b.op(ACT, lambda e: e.activation(out=scT[:], in_=scTf[:], func=AF.Silu), [scr], [scr])
            ui = 0
            for l in range(DEPTH):
                wv = I["w_ada"][l].rearrange("(kc p) n -> p kc n", p=128)
                for cb in range(12):
                    s = ui % 3
                    ui += 1
                    b.dma(POOL, wsl[s][:], wv[:, :, cb * 512:(cb + 1) * 512], [], [wslr[s]], wsd[s])
                    v = cb // 2
                    if v in (2, 5):
                        bk = b.bank()
                        for kc in range(8):
                            b.op(PE, lambda e: e.matmul(ps[bk][0:2, :], lhsT=scT[:, kc, :], rhs=wsl[s][:, kc, :],
                                                        start=(kc == 0), stop=(kc == 7)), [scr, wslr[s]], [psr[bk]])
                        gi = 0 if v == 2 else 1
                        hs = (cb % 2) * 512
                        b.op(DVE, lambda e: e.tensor_tensor(out=grow[:, l, gi, hs:hs + 512], in0=ps[bk][0:2, :],
                                                            in1=brow[:, l, gi, hs:hs + 512], op=ALU.add),
                             [psr[bk], miscr], [growr])
                    else:
                        vi = {0: 0, 1: 1, 3: 2, 4: 3}[v]
                        bk = b.bank()
                        for q in range(4):
                            for kc in range(8):
                                b.op(PE, lambda e: e.matmul(ps[bk][:, 2 * q:2 * q + 2],
                                                            lhsT=wsl[s][:, kc, q * 128:(q + 1) * 128],
                                                            rhs=scT[:, kc, :], start=(kc == 0), stop=(kc == 7)),
                                     [scr, wslr[s]], [psr[bk]])
                        k0 = (cb % 2) * 4
                        for c in range(2):
                            b.op(DVE, lambda e: e.tensor_tensor(
                                out=modc[:, l, c, vi, k0:k0 + 4],
                                in0=ps[bk][:, 0:8].rearrange("p (q c) -> p q c", c=2)[:, :, c],
                                in1=badT[:, l, cb * 4:cb * 4 + 4], op=ALU.add), [psr[bk], miscr], [modr[l]])
                for c in range(2):
                    for w in range(2):
                        b.op(DVE, lambda e: e.scalar_tensor_tensor(
                            out=modc[:, l, c, 2 * w + 1, :], in0=modc[:, l, c, 2 * w + 1, :], scalar=1.0,
                            in1=ngT[:, l, w, :], op0=ALU.add, op1=ALU.mult), [modr[l], miscr], [modr[l]])
            for j in range(2):
                b.dma(SP, gscr[:, j, :, :], grow[j:j + 1, :, :, :], [growr], [gscr_r], dd)

        gbd = b.ds(True)

        def load_gates(l, gi):
            for c in range(2):
                b.dma(SP, gb[:, c, :], gscr[l, c, gi, :].partition_broadcast(128), [gscr_r], [gbr], gbd)

        def ffn_phase(l):
            with b.phase():
                hT = [b.sb("hT", [128, 8, 512], BF16) for _ in range(2)]
                hTr = RL(2)
                xs4 = b.sb("xs4", [128, 4, D], F32)
                xs4r = RL(4)
                junk = junkr = None
                ss = b.sb("ss", [128, 4], F32)
                ssr = Res()
                aT = b.sb("aT", [128, NF, 512], BF16)
                aTr = RL(NF)
                wgu = [b.sb("wgu", [128, 2, 8, 256], BF16) for _ in range(3)]
                wgr = [RL(2) for _ in range(3)]
                wgd = [[b.dsw(), b.dsw()] for _ in range(3)]
                wd = [b.sb("wd", [128, 2, 512], BF16) for _ in range(3)]
                wdr = RL(3)
                wdd = [b.dsw() for _ in range(3)]
                wgh = [[b.ds(), b.ds()] for _ in range(3)]
                wdh = [b.ds() for _ in range(3)]
                wgo = [[b.ds(), b.ds()] for _ in range(3)]
                wdo = [b.ds() for _ in range(3)]
                sg = [b.sb("sg", [128, 512], BF16) for _ in range(2)]
                sgr = RL(2)
                tmp = [b.sb("tmp", [128, 512], F32) for _ in range(2)]
                tmpr = RL(2)
                wgv = I["w_ffn_gate"][l].rearrange("(kc p) n -> p kc n", p=128)
                wuv = I["w_ffn_up"][l].rearrange("(kc p) n -> p kc n", p=128)
                wdv = I["w_ffn_down"][l].rearrange("(f p) n -> p f n", p=128)
                norm_group(0, 4, l, 1, hT[0], hTr[0], xs4, xs4r, ss, ssr)
                ug = 0
                ud = 0
                si = 0
                for g in range(NG):
                    h = hT[g % 2]
                    hr = hTr[g % 2]
                    for u in range(11):
                        s = ug % 3
                        ug += 1
                        if g == 0:
                            b.dma(POOL, wgu[s][:, 0], wgv[:, :, u * 256:(u + 1) * 256], [], [wgr[s][0]], wgd[s][0])
                            b.dma(POOL, wgu[s][:, 1], wuv[:, :, u * 256:(u + 1) * 256], [], [wgr[s][1]], wgd[s][1])
                            for w_ in range(2):
                                b.dma(SP, wc_gu[u, w_], wgu[s][:, w_].rearrange("p k n -> p (k n)"), [wgr[s][w_]], [wc_gu_r[u][w_]], wgo[s][w_])
                        else:
                            for w_ in range(2):
                                b.dma(SP, wgu[s][:, w_].rearrange("p k n -> p (k n)"), wc_gu[u, w_], [wc_gu_r[u][w_]], [wgr[s][w_]], wgh[s][w_])
                        for fi in range(2):
                            f = 2 * u + fi
                            bg = b.bank()
                            bu = b.bank()
                            for kc in range(8):
                                b.op(PE, lambda e: e.matmul(ps[bg][:, :], lhsT=wgu[s][:, 0, kc, fi * 128:(fi + 1) * 128],
                                                            rhs=h[:, kc, :], start=(kc == 0), stop=(kc == 7)),
                                     [wgr[s][0], hr], [psr[bg]])
                            for kc in range(8):
                                b.op(PE, lambda e: e.matmul(ps[bu][:, :], lhsT=wgu[s][:, 1, kc, fi * 128:(fi + 1) * 128],
                                                            rhs=h[:, kc, :], start=(kc == 0), stop=(kc == 7)),
                                     [wgr[s][1], hr], [psr[bu]])
                            k = si % 2
                            si += 1
                            b.op(ACT, lambda e: e.activation(out=sg[k][:], in_=ps[bg][:, :], func=AF.Silu),
                                 [psr[bg]], [sgr[k]])
                            b.op(DVE, lambda e: e.tensor_tensor(out=aT[:, f, :], in0=sg[k][:], in1=ps[bu][:, :],
                                                                op=ALU.mult), [sgr[k], psr[bu]], [aTr[f]])
                        if u == 5 and g + 1 < NG:
                            norm_group(4 * (g + 1), 4, l, 1, hT[(g + 1) % 2], hTr[(g + 1) % 2], xs4, xs4r, ss, ssr)
                    for half in range(2):
                        bks = [b.bank() for _ in range(4)]
                        for u in range(11):
                            s = ud % 3
                            ud += 1
                            if g == 0:
                                b.dma(POOL, wd[s][:], wdv[:, 2 * u:2 * u + 2, half * 512:(half + 1) * 512], [], [wdr[s]], wdd[s])
                                b.dma(SP, wc_d[half, u], wd[s][:].rearrange("p f n -> p (f n)"), [wdr[s]], [wc_d_r[half][u]], wdo[s])
                            else:
                                b.dma(SP, wd[s][:].rearrange("p f n -> p (f n)"), wc_d[half, u], [wc_d_r[half][u]], [wdr[s]], wdh[s])
                            for fi in range(2):
                                f = 2 * u + fi
                                for tt in range(4):
                                    b.op(PE, lambda e: e.matmul(ps[bks[tt]][:, :], lhsT=aT[:, f, tt * 128:(tt + 1) * 128],
                                                                rhs=wd[s][:, fi, :], start=(f == 0), stop=(f == NF - 1)),
                                         [aTr[f], wdr[s]], [psr[bks[tt]]])
                        for tt in range(4):
                            resid_update(4 * g + tt, half, bks[tt], 1, tmp[tt % 2], tmpr[tt % 2])

        def c_phase(l):
            j = l // 2
            NTC = 2
            W = NTC * 128
            NGC = NT // NTC
            with b.phase():
                hT = [b.sb("hT", [128, 8, W], BF16) for _ in range(2)]
                hTr = RL(2)
                ss = b.sb("ss", [128, 4], F32)
                ssr = Res()
                win = [b.sb("winc", [128, 8, 512], BF16) for _ in range(2)]
                winr = RL(2)
                wind = [b.dsw() for _ in range(2)]
                wout = b.sb("woutc", [128, 8, D], BF16)
                woutr = Res()
                wst = b.sb("wst", [128, 8, 128], BF16)
                bst = b.sb("bst", [128, 8], F32)
                lng = b.sb("lng", [128, D], F32)
                lnb = b.sb("lnb", [128, D], F32)
                cwr = Res()
                u_sb = [b.sb("u_sb", [128, NTC, D], BF16) for _ in range(2)]
                ur = [RL(NTC) for _ in range(2)]
                v_sb = [b.sb("v_sb", [128, NTC, D], F32) for _ in range(2)]
                vr = [RL(NTC) for _ in range(2)]
                vln = b.sb("vln", [128, NTC, D], BF16)
                vlr = RL(NTC)
                aT = b.sb("aTc", [128, 8, W], BF16)
                aTr = RL(8)
                st = b.sb("bnst", [128, 2, 6], F32)
                mv = b.sb("bnmv", [128, 2], F32)
                str_ = Res()
                tmp = [b.sb("tmp", [128, 512], F32) for _ in range(2)]
                tmpr = RL(2)
                dd = b.ds()
                b.dma(POOL, wout[:], I["w_out_c"][j].rearrange("(kc p) n -> p kc n", p=128), [], [woutr], b.dsw())
                b.dma(POOL, wst[:], I["wsT"][j], [], [cwr], b.dsw())
                b.dma(SP, bst[:], I["b_sT"][j], [], [cwr], dd)
                b.dma(SP, lng[:], I["ln_g_c"][j].partition_broadcast(128), [], [cwr], dd)
                b.dma(SP, lnb[:], I["ln_b_c"][j].partition_broadcast(128), [], [cwr], dd)
                wv = I["w_in_c"][j].rearrange("(kc p) n -> p kc n", p=128)
                ui = [0]

                def front(g):
                    k = g % 2
                    t0 = g * NTC
                    norm_group(t0, NTC, l, 0, hT[k], hTr[k], v_sb[k], vr[k], ss, ssr)
                    yield
                    for cb in range(4):
                        s = ui[0] % 2
                        ui[0] += 1
                        b.dma(POOL, win[s][:], wv[:, :, cb * 512:(cb + 1) * 512], [], [winr[s]], wind[s])
                        for tt in range(NTC):
                            bk = b.bank()
                            for kc in range(8):
                                b.op(PE, lambda e: e.matmul(ps[bk][:, :], lhsT=hT[k][:, kc, tt * 128:(tt + 1) * 128],
                                                            rhs=win[s][:, kc, :], start=(kc == 0), stop=(kc == 7)),
                                     [hTr[k], winr[s]], [psr[bk]])
                            if cb < 2:
                                b.op(ACT, lambda e: e.activation(out=u_sb[k][:, tt, cb * 512:(cb + 1) * 512], in_=ps[bk][:, :],
                                                                 func=AF.Gelu_apprx_tanh), [psr[bk]], [ur[k][tt]])
                            else:
                                b.op(ACT, lambda e: e.activation(out=v_sb[k][:, tt, (cb - 2) * 512:(cb - 1) * 512],
                                                                 in_=ps[bk][:, :], func=AF.Gelu_apprx_tanh),
                                     [psr[bk]], [vr[k][tt]])
                            yield

                def back(g):
                    k = g % 2
                    t0 = g * NTC
                    v = v_sb[k]
                    a_sb, ar = v_sb[k], vr[k]
                    for tt in range(NTC):
                        for hh in range(2):
                            b.op(DVE, lambda e: e.bn_stats(out=st[:, hh, :], in_=v[:, tt, hh * 512:(hh + 1) * 512]),
                                 [vr[k][tt]], [str_])
                        b.op(DVE, lambda e: e.bn_aggr(out=mv[:], in_=st[:].rearrange("p a s -> p (a s)")), [str_], [str_])
                        b.op(DVE, lambda e: e.tensor_scalar(out=mv[:, 1:2], in0=mv[:, 1:2], scalar1=EPS, scalar2=None,
                                                            op0=ALU.add), [str_], [str_])
                        b.op(ACT, lambda e: e.activation(out=mv[:, 1:2], in_=mv[:, 1:2], func=AF.Sqrt), [str_], [str_])
                        b.op(DVE, lambda e: e.reciprocal(out=mv[:, 1:2], in_=mv[:, 1:2]), [str_], [str_])
                        b.op(DVE, lambda e: e.tensor_scalar(out=v[:, tt, :], in0=v[:, tt, :], scalar1=mv[:, 0:1],
                                                            scalar2=mv[:, 1:2], op0=ALU.subtract, op1=ALU.mult),
                             [vr[k][tt], str_], [vr[k][tt]])
                        b.op(DVE, lambda e: e.tensor_tensor(out=v[:, tt, :], in0=v[:, tt, :], in1=lng[:], op=ALU.mult),
                             [vr[k][tt], cwr], [vr[k][tt]])
                        b.op(DVE, lambda e: e.tensor_tensor(out=vln[:, tt, :], in0=v[:, tt, :], in1=lnb[:], op=ALU.add),
                             [vr[k][tt], cwr], [vlr[tt]])
                        yield
                        bk2 = [b.bank(), b.bank()]
                        for gg in range(8):
                            bk = bk2[gg // 4]
                            b.op(PE, lambda e: e.matmul(ps[bk][:, (gg % 4) * 128:(gg % 4 + 1) * 128], lhsT=wst[:, gg, :],
                                                        rhs=vln[:, tt, gg * 128:(gg + 1) * 128], start=True, stop=True),
                                 [cwr, vlr[tt]], [psr[bk]])
                        yield
                        for gg in range(8):
                            bk = bk2[gg // 4]
                            b.op(DVE, lambda e: e.scalar_tensor_tensor(
                                out=a_sb[:, tt, gg * 128:(gg + 1) * 128], in0=ps[bk][:, (gg % 4) * 128:(gg % 4 + 1) * 128],
                                scalar=bst[:, gg:gg + 1], in1=u_sb[k][:, tt, gg * 128:(gg + 1) * 128],
                                op0=ALU.add, op1=ALU.mult), [psr[bk], cwr, ur[k][tt]], [ar[tt]])
                        yield
                    for kc in range(8):
                        bk = b.bank()
                        for tt in range(NTC):
                            b.op(PE, lambda e: e.transpose(out=ps[bk][:, tt * 128:(tt + 1) * 128],
                                                           in_=a_sb[:, tt, kc * 128:(kc + 1) * 128], identity=ident),
                                 [ar[tt], cr], [psr[bk]])
                        b.op(ACT, lambda e: e.copy(out=aT[:, kc, :], in_=ps[bk][:, :W]), [psr[bk]], [aTr[kc]])
                        if kc % 2 == 1:
                            yield
                    for tt in range(NTC):
                        for half in range(2):
                            bk = b.bank()
                            for kc in range(8):
                                b.op(PE, lambda e: e.matmul(ps[bk][:, :], lhsT=aT[:, kc, tt * 128:(tt + 1) * 128],
                                                            rhs=wout[:, kc, half * 512:(half + 1) * 512],
                                                            start=(kc == 0), stop=(kc == 7)), [aTr[kc], woutr], [psr[bk]])
                            resid_update(t0 + tt, half, bk, 0, tmp[half], tmpr[half])
                            yield

                def interleave(*gens):
                    gens = [g_ for g_ in gens if g_ is not None]
                    while gens:
                        for g_ in list(gens):
                            try:
                                next(g_)
                            except StopIteration:
                                gens.remove(g_)

                interleave(front(0))
                for g in range(NGC):
                    interleave(back(g), front(g + 1) if g + 1 < NGC else None)

        def final_phase():
            with b.phase():
                fg = b.sb("fg", [128, D], F32)
                fgr = Res()
                junk = b.sb("junk", [128, D], F32)
                junkr = Res()
                ss = b.sb("ssf", [128, NT], F32)
                ssr = Res()
                yo = [b.sb("yo", [128, D], F32) for _ in range(3)]
                yor = RL(3)
                yd = [b.ds() for _ in range(3)]
                b.dma(SP, fg[:], I["final_norm_g"].partition_broadcast(128), [], [fgr], b.ds())
                for t in range(NT):
                    b.op(ACT, lambda e: e.activation(out=junk[:], in_=x[:, t, :], func=AF.Square,
                                                     accum_out=ss[:, t:t + 1]), [xr[t]], [junkr, ssr])
                rstd_from_ss(ss, NT, 1.0 / D, ssr)
                yv = O["y"].rearrange("(t p) d -> p t d", p=128)
                for t in range(NT):
                    k = t % 3
                    b.op(DVE, lambda e: e.scalar_tensor_tensor(out=yo[k][:], in0=x[:, t, :], scalar=ss[:, t:t + 1],
                                                               in1=fg[:], op0=ALU.mult, op1=ALU.mult),
                         [xr[t], ssr, fgr], [yor[k]])
                    b.dma(SP, yv[:, t, :], yo[k][:], [yor[k]], [], yd[k])

        def ab_phase(l):
            j = l // 2
            ab_layer(b, I, O, j, l, x, xr, consts, cr, ident, norm_group, resid_update,
                     dict(cc1_in=cc1_in, cc1_out=cc1_out, cc2_in=cc2_in, cc2_out=cc2_out, qscr=qscr, rscr=rscr,
                          oscr=oscr, gscr2=gscr2, ascr=ascr, mscr=mscr))

        for l in range(int(os.environ.get("NLAYERS", DEPTH))):
            load_gates(l, 0)
            if l % 2 == 0:
                ab_phase(l)
            else:
                c_phase(l)
            load_gates(l, 1)
            ffn_phase(l)
        final_phase()
        b.barrier()
    return nc


def ab_layer(b, I, O, j, l, x, xr, consts, cr, ident, norm_group, resid_update, S):
    PE, ACT, DVE, POOL, SP = b.PE, b.ACT, b.DVE, b.POOL, b.SP
    ps, psr = b.ps, b.psr
    cc1_in, cc1_out, cc2_in, cc2_out = S["cc1_in"], S["cc1_out"], S["cc2_in"], S["cc2_out"]
    qscr, rscr, oscr, gscr2, ascr, mscr = S["qscr"], S["rscr"], S["oscr"], S["gscr2"], S["ascr"], S["mscr"]
    mscr_r = RL(8)
    RS = 128 ** -0.5
    STAGE = float(os.environ.get("ABDBG", "9"))
    if STAGE < 1:
        return
    cc1i_r, cc1o_r, cc2i_r, cc2o_r = Res(), Res(), Res(), Res()
    qscr_r, rscr_r, oscr_r, gscr2_r, ascr_r = Res(), RL(16), RL(16), RL(16), RL(16)

    def copy_alt(k, out, in_, reads, writes):
        if k % 2 == 0:
            b.op(ACT, lambda e: e.copy(out=out, in_=in_), reads, writes)
        else:
            b.op(DVE, lambda e: e.tensor_copy(out=out, in_=in_), reads, writes)

    with b.phase():
        W = 256
        hT = b.sb("hT", [128, 8, W], BF16); hTr = Res()
        xs4 = b.sb("xs4", [128, 2, D], F32); xs4r = RL(2)
        mix, mixr = xs4, xs4r
        ss = b.sb("ss", [128, 8], F32); ssr = Res()
        win = [b.sb("winab", [128, 8, 512], BF16) for _ in range(2)]; winr = RL(2); wind = [b.dsw() for _ in range(2)]
        wuq = b.sb("wuq", [128, 2, 8, 96], BF16)
        wkn = b.sb("wkn", [128, 8, 96], BF16)
        wv = b.sb("wv", [128, 8, 64], BF16)
        qng = b.sb("qng", [128, 2], F32)
        kvg = b.sb("kvg", [128, 128], F32)
        rm = b.sb("rm", [32, 32], BF16)
        rhi = b.sb("rhi", [32, W], BF16); rlo = b.sb("rlo", [32, W], BF16); rhr = Res()
        wr = Res()
        ldb = b.sb("ldb", [128, 16], F32)
        DT = b.sb("DT", [128, 2, 4, 128], BF16)
        QD = b.sb("QD", [128, 2, 4, 128], F32)
        KD = b.sb("KD", [128, 2, 4], F32)
        CDc = b.sb("CD", [128, 2, 4], F32)
        etmp = b.sb("etmp", [128, 128], F32)
        dr = Res()
        cqn = b.sb("cqn", [128, 256], F32); ckvn = b.sb("ckvn", [128, 2, 128], F32); krs = b.sb("krs", [128, 2, 32], F32)
        pr = RL(2)
        cqnT = b.sb("cqnT", [128, 2, W], BF16); ckvnT = b.sb("ckvnT", [128, W], BF16)
        krT = b.sb("krT", [32, W], BF16); krTf = b.sb("krTf", [32, W], F32)
        tr = Res()
        cs2 = b.sb("cs2", [32, 2, 2, W], F32); csr = Res(); csd = b.ds()
        wuqR = b.sb("wuqR", [128, 2, 8, 32], BF16)
        qT = b.sb("qT", [96, 8, W], BF16); qTr = Res()
        qrr = Res()
        rt1 = b.sb("rt1", [32, 2 * W], F32); rt2 = b.sb("rt2", [32, 2 * W], F32); rtr = Res()
        KT = b.sb("KT", [96, 8, W], BF16); KTr = Res()
        V = b.sb("V", [128, 2, 8, 65], BF16); Vr = Res()
        PT = [b.sb("PT", [128, W], BF16) for _ in range(2)]; PTr = RL(2)
        rc = b.sb("rc", [128, 2], F32); rcr = Res()
        rop = b.sb("rop", [128, 2, 5, 4, 128], BF16); ropr = RL(2)
        qdl = b.sb("qdl", [128, 2, 4, 128], BF16); kdl = b.sb("kdl", [128, 2, 4, 128], BF16); rlr = RL(2)
        sgt = b.sb("sgt", [128, 2, 512], BF16); sgr = RL(2)
        Sst = [b.sb("Sst", [128, 4, 128], F32) for _ in range(2)]; Sbf = [b.sb("Sbf", [128, 4, 128], BF16) for _ in range(2)]
        Sr = RL(2)
        AD = b.sb("AD", [128, 4, 128], BF16); ADr = Res()
        ol = b.sb("ol", [128, 2, 512], F32); olr = RL(2)
        osum = b.sb("osum", [128, 512], F32); osr = Res()
        dA = b.ds(); dB = b.ds(); dC = b.ds(); dD = b.ds(); dE = b.ds(); dF = b.ds(); dG = b.ds(); dH = b.ds(); dS = b.dsw()

        wuqv = I["w_uq"][j].rearrange("(kc p) (h c) -> p kc h c", p=128, c=96)
        for kcq in range(2):
            b.dma(POOL, wuq[:, kcq, :, 0:32], wuqv[:, kcq, :, 64:96], [], [wr], dS)
            b.dma(POOL, wuq[:, kcq, :, 32:96], wuqv[:, kcq, :, 0:64], [], [wr], dS)
        wukv = I["w_ukv"][j].rearrange("p (h c) -> p h c", c=128)
        b.op(DVE, lambda e: e.memset(wkn[:], 0.0), [], [wr])
        b.dma(POOL, wkn[:, :, 32:96], wukv[:, :, 0:64], [], [wr], dS)
        b.dma(POOL, wv[:], wukv[:, :, 64:128], [], [wr], dS)
        for kcq in range(2):
            for o_, i_, sgn in ((0, 8, -1.0), (8, 0, 1.0), (16, 24, -1.0), (24, 16, 1.0)):
                b.op(DVE, lambda e: e.tensor_scalar(out=wuqR[:, kcq, :, o_:o_ + 8], in0=wuq[:, kcq, :, i_:i_ + 8], scalar1=sgn,
                                                    scalar2=None, op0=ALU.mult), [wr], [wr])
        b.dma(SP, qng[:], I["q_norm_gT"][j], [], [wr], dB)
        b.dma(SP, kvg[:], I["kv_norm_g"][j].partition_broadcast(128), [], [wr], dB)
        b.dma(POOL, rm[:], I["rmT"][:, :], [], [wr], dS)
        b.dma(SP, ldb[:], I["ldec"][0].partition_broadcast(128), [], [dr], dB)
        b.op(ACT, lambda e: e.activation(out=ldb[:], in_=ldb[:], func=AF.Exp), [dr], [dr])
        b.op(DVE, lambda e: e.tensor_scalar(out=ldb[:], in0=ldb[:], scalar1=-1.0, scalar2=None, op0=ALU.mult), [dr], [dr])
        for d_ in range(2):
            for h in range(4):
                lg = ldb[:, j * 8 + d_ * 4 + h:j * 8 + d_ * 4 + h + 1]
                b.op(ACT, lambda e: e.activation(out=etmp[:], in_=consts[:, 1 + 2 * d_, :], func=AF.Exp, scale=lg), [dr, cr], [dr])
                b.op(DVE, lambda e: e.scalar_tensor_tensor(out=DT[:, d_, h, :], in0=etmp[:], scalar=RS, in1=consts[:, 2 + 2 * d_, :],
                                                           op0=ALU.mult, op1=ALU.mult), [dr, cr], [dr])
                b.op(ACT, lambda e: e.activation(out=QD[:, d_, h, :], in_=consts[:, 6 + d_, :], func=AF.Exp, scale=lg), [dr, cr], [dr])
                b.op(ACT, lambda e: e.activation(out=KD[:, d_, h:h + 1], in_=consts[:, 5, 1 + 2 * d_:2 + 2 * d_], func=AF.Exp, scale=lg),
                     [dr, cr], [dr])
                b.op(ACT, lambda e: e.activation(out=CDc[:, d_, h:h + 1], in_=consts[:, 5, 4:5], func=AF.Exp, scale=lg), [dr, cr], [dr])
        b.op(DVE, lambda e: e.tensor_scalar(out=KD[:], in0=KD[:], scalar1=RS, scalar2=None, op0=ALU.mult), [dr], [dr])
        b.op(DVE, lambda e: e.memset(V[:, :, :, 64:65], 1.0), [], [Vr])

        winv = I["w_in_ab"][j].rearrange("(kc p) n -> p kc n", p=128)
        blocks = [(0, 416), (416, 512), (928, 512), (1440, 512), (1952, 512)]
        ui = [0]

        def load_blk(bi):
            s_ = ui[0] % 2
            ui[0] += 1
            c0, n = blocks[bi]
            b.dma(POOL, win[s_][:, :, 0:n], winv[:, :, c0:c0 + n], [], [winr[s_]], wind[s_])
            return s_

        def ret_step(d_, si, qT_, qdT_, kT_, kd_, v_, rd):
            ba = b.bank()
            for h in range(4):
                b.op(PE, lambda e: e.matmul(ps[ba][:, h * 128:(h + 1) * 128], lhsT=kT_[:, h, :], rhs=qT_[:, h, :], start=True, stop=True),
                     rd, [psr[ba]])
            b.op(DVE, lambda e: e.tensor_tensor(out=AD[:].rearrange("p h i -> p (h i)"), in0=ps[ba][:, :],
                                                in1=DT[:, d_].rearrange("p h i -> p (h i)"), op=ALU.mult), [psr[ba], dr], [ADr])
            bo = b.bank()
            for h in range(4):
                b.op(PE, lambda e: e.matmul(ps[bo][:, h * 128:(h + 1) * 128], lhsT=AD[:, h, :], rhs=v_[:, h, :], start=True, stop=False),
                     rd + [ADr], [psr[bo]])
                b.op(PE, lambda e: e.matmul(ps[bo][:, h * 128:(h + 1) * 128], lhsT=qdT_[:, h, :], rhs=Sbf[si][:, h, :], start=False, stop=True),
                     rd + [Sr[si]], [psr[bo]])
            bs_ = b.bank()
            for h in range(4):
                b.op(PE, lambda e: e.matmul(ps[bs_][:, h * 128:(h + 1) * 128], lhsT=kd_[:, h, :], rhs=v_[:, h, :], start=True, stop=True),
                     rd, [psr[bs_]])
            for h in range(4):
                b.op(DVE, lambda e: e.scalar_tensor_tensor(out=Sst[si][:, h, :], in0=Sst[si][:, h, :], scalar=CDc[:, d_, h:h + 1],
                                                           in1=ps[bs_][:, h * 128:(h + 1) * 128], op0=ALU.mult, op1=ALU.add),
                     [psr[bs_], dr, Sr[si]], [Sr[si]])
            b.op(ACT, lambda e: e.copy(out=Sbf[si][:], in_=Sst[si][:]), [Sr[si]], [Sr[si]])
            return bo

        def merge_tile(tt, bo_trail, o_lead_ap, o_lead_res, sg_ap, sg_res, mix_t, mix_res):
            b.op(DVE, lambda e: e.tensor_tensor(out=osum[:], in0=ps[bo_trail][:, :], in1=o_lead_ap, op=ALU.add),
                 [psr[bo_trail], o_lead_res], [osr])
            for h in range(4):
                b.op(ACT, lambda e: e.activation(out=etmp[:], in_=osum[:, h * 128:(h + 1) * 128], func=AF.Square,
                                                 accum_out=ss[:, 4 + h:5 + h]), [osr], [dr, ssr])
            b.op(DVE, lambda e: e.tensor_scalar(out=ss[:, 4:8], in0=ss[:, 4:8], scalar1=1.0 / 128, scalar2=EPS, op0=ALU.mult, op1=ALU.add),
                 [ssr], [ssr])
            b.op(ACT, lambda e: e.activation(out=ss[:, 4:8], in_=ss[:, 4:8], func=AF.Sqrt), [ssr], [ssr])
            b.op(DVE, lambda e: e.reciprocal(out=ss[:, 4:8], in_=ss[:, 4:8]), [ssr], [ssr])
            for h in range(4):
                b.op(DVE, lambda e: e.scalar_tensor_tensor(out=mix_t[:, 512 + h * 128:512 + (h + 1) * 128], in0=osum[:, h * 128:(h + 1) * 128],
                                                           scalar=ss[:, 4 + h:5 + h], in1=sg_ap[:, h * 128:(h + 1) * 128],
                                                           op0=ALU.mult, op1=ALU.mult), [osr, ssr, sg_res], [mix_res])

        S["ret_step_defs"] = None
        b.dma(SP, Sst[1][:], I["s0_lead"][j].rearrange("h d e -> d h e"), [], [Sr[1]], dC)
        b.op(ACT, lambda e: e.copy(out=Sbf[1][:], in_=Sst[1][:]), [Sr[1]], [Sr[1]])

        for g in range(12):
            t0 = 2 * g
            prompt = g < 4
            norm_group(t0, 2, l, 0, hT, hTr, xs4, xs4r, ss, ssr)
            s_ = load_blk(0)
            bT = b.bank()
            bC = b.bank()
            for tt in range(2):
                t = t0 + tt
                bk = b.bank()
                for kc in range(8):
                    b.op(PE, lambda e: e.matmul(ps[bk][:, 0:416], lhsT=hT[:, kc, tt * 128:(tt + 1) * 128], rhs=win[s_][:, kc, 0:416],
                                                start=(kc == 0), stop=(kc == 7)), [hTr, winr[s_]], [psr[bk]])
                b.op(ACT, lambda e: e.activation(out=cqn[:], in_=ps[bk][:, 0:256], func=AF.Square, accum_out=ss[:, 0:1]), [psr[bk]], [pr[0], ssr])
                b.op(ACT, lambda e: e.activation(out=ckvn[:, tt, :], in_=ps[bk][:, 256:384], func=AF.Square, accum_out=ss[:, 1:2]),
                     [psr[bk]], [pr[1], ssr])
                b.op(DVE, lambda e: e.tensor_scalar(out=ss[:, 0:1], in0=ss[:, 0:1], scalar1=1.0 / 256, scalar2=EPS, op0=ALU.mult, op1=ALU.add),
                     [ssr], [ssr])
                b.op(DVE, lambda e: e.tensor_scalar(out=ss[:, 1:2], in0=ss[:, 1:2], scalar1=1.0 / 128, scalar2=EPS, op0=ALU.mult, op1=ALU.add),
                     [ssr], [ssr])
                b.op(ACT, lambda e: e.activation(out=ss[:, 0:2], in_=ss[:, 0:2], func=AF.Sqrt), [ssr], [ssr])
                b.op(DVE, lambda e: e.reciprocal(out=ss[:, 0:2], in_=ss[:, 0:2]), [ssr], [ssr])
                b.op(DVE, lambda e: e.tensor_scalar(out=cqn[:], in0=ps[bk][:, 0:256], scalar1=ss[:, 0:1], scalar2=None, op0=ALU.mult),
                     [psr[bk], ssr], [pr[0]])
                b.op(DVE, lambda e: e.scalar_tensor_tensor(out=ckvn[:, tt, :], in0=ps[bk][:, 256:384], scalar=ss[:, 1:2], in1=kvg[:],
                                                           op0=ALU.mult, op1=ALU.mult), [psr[bk], ssr, wr], [pr[1]])
                b.op(ACT, lambda e: e.copy(out=krs[:, tt, :], in_=ps[bk][:, 384:416]), [psr[bk]], [pr[1]])
                for kcq in range(2):
                    b.op(PE, lambda e: e.transpose(out=ps[bT][:, kcq * W + tt * 128:kcq * W + (tt + 1) * 128],
                                                   in_=cqn[:, kcq * 128:(kcq + 1) * 128], identity=ident), [pr[0], cr], [psr[bT]])
                b.op(PE, lambda e: e.transpose(out=ps[bC][:, tt * 128:(tt + 1) * 128], in_=ckvn[:, tt, :], identity=ident),
                     [pr[1], cr], [psr[bC]])
                b.op(PE, lambda e: e.transpose(out=ps[bC][0:32, W + tt * 128:W + (tt + 1) * 128], in_=krs[:, tt, :], identity=ident),
                     [pr[1], cr], [psr[bC]])
            if prompt:
                b.dma(SP, O["o_ckv"][j, t0 * 128:(t0 + 2) * 128, :].rearrange("(t p) c -> p t c", p=128), ckvn[:], [pr[1]], [], dD)
                b.dma(SP, O["o_kr"][j, t0 * 128:(t0 + 2) * 128, :].rearrange("(t p) c -> p t c", p=128), krs[:], [pr[1]], [], dE)
            for kcq in range(2):
                b.op(ACT, lambda e: e.activation(out=cqnT[:, kcq, :], in_=ps[bT][:, kcq * W:(kcq + 1) * W], func=AF.Identity,
                                                 scale=qng[:, kcq:kcq + 1]), [psr[bT], wr], [tr])
            b.op(DVE, lambda e: e.tensor_copy(out=ckvnT[:], in_=ps[bC][:, 0:W]), [psr[bC]], [tr])
            b.op(DVE, lambda e: e.tensor_copy(out=krTf[:], in_=ps[bC][0:32, W:2 * W]), [psr[bC]], [tr])
            if STAGE < 2:
                continue
            if prompt:
                b.op(ACT, lambda e: e.copy(out=krT[:], in_=ps[bC][0:32, W:2 * W]), [psr[bC]], [tr])
            else:
                tok0 = (t0 - 8) * 128
                for rep_ in range(2):
                    b.dma(SP, cs2[:, :, rep_, :], I["ropecs"][:, :, tok0:tok0 + W].rearrange("a p w -> p a w"), [], [csr], csd)
                bk = b.bank()
                b.op(ACT, lambda e: e.copy(out=rhi[:], in_=krTf[:]), [tr], [rhr])
                b.op(DVE, lambda e: e.tensor_tensor(out=rlo[:], in0=krTf[:], in1=rhi[:], op=ALU.subtract), [tr, rhr], [rhr])
                b.op(PE, lambda e: e.matmul(ps[bk][0:32, 0:W], lhsT=rm[:], rhs=rhi[:], start=True, stop=False), [wr, rhr], [psr[bk]])
                b.op(PE, lambda e: e.matmul(ps[bk][0:32, 0:W], lhsT=rm[:], rhs=rlo[:], start=False, stop=True), [wr, rhr], [psr[bk]])
                b.op(DVE, lambda e: e.tensor_tensor(out=rt1[:, 0:W], in0=ps[bk][0:32, 0:W], in1=cs2[:, 1, 0, :], op=ALU.mult), [psr[bk], csr], [rtr])
                b.op(DVE, lambda e: e.tensor_tensor(out=rt2[:, 0:W], in0=krTf[:], in1=cs2[:, 0, 0, :], op=ALU.mult), [tr, csr], [rtr])
                b.op(DVE, lambda e: e.tensor_tensor(out=krT[:], in0=rt1[:, 0:W], in1=rt2[:, 0:W], op=ALU.add), [rtr], [tr])
                b.dma(SP, cc1_in[0:128, tok0:tok0 + W], ckvnT[:], [tr], [cc1i_r], dD)
                b.dma(SP, cc1_in[128:160, tok0:tok0 + W], krT[:], [tr], [cc1i_r], dE)
            if STAGE < 2.2:
                continue
            for hp in range(4):
                bk = b.bank()
                for hh in range(2):
                    h = 2 * hp + hh
                    for kcq in range(2):
                        b.op(PE, lambda e: e.matmul(ps[bk][0:96, hh * W:(hh + 1) * W], lhsT=wuq[:, kcq, h, :], rhs=cqnT[:, kcq, :],
                                                    start=(kcq == 0), stop=(kcq == 1)), [wr, tr], [psr[bk]])
                copy_alt(hp, qT[:, 2 * hp:2 * hp + 2, :], ps[bk][0:96, :].rearrange("p (h w) -> p h w", w=W), [psr[bk]], [qTr])
                if not prompt and STAGE >= 2.4:
                    bk2 = b.bank()
                    for hh in range(2):
                        h = 2 * hp + hh
                        for kcq in range(2):
                            b.op(PE, lambda e: e.matmul(ps[bk2][0:32, hh * W:(hh + 1) * W], lhsT=wuqR[:, kcq, h, :], rhs=cqnT[:, kcq, :],
                                                        start=(kcq == 0), stop=(kcq == 1)), [wr, tr], [psr[bk2]])
                    b.op(DVE, lambda e: e.tensor_tensor(out=rt1[:], in0=ps[bk2][0:32, :], in1=cs2[:, 1].rearrange("p r w -> p (r w)"),
                                                        op=ALU.mult), [psr[bk2], csr], [rtr])
                    b.op(DVE, lambda e: e.tensor_tensor(out=rt2[:], in0=ps[bk][0:32, :], in1=cs2[:, 0].rearrange("p r w -> p (r w)"),
                                                        op=ALU.mult), [psr[bk], csr], [rtr])
                    b.op(DVE, lambda e: e.tensor_tensor(out=KT[0:32, 2 * hp:2 * hp + 2, :].rearrange("p h w -> p (h w)"), in0=rt1[:], in1=rt2[:],
                                                        op=ALU.add), [rtr], [KTr])
            if not prompt and STAGE >= 2.6:
                b.dma(SP, qscr[:, 0, :, tok0:tok0 + W].rearrange("h r w -> r h w"), qT[:], [qTr], [qscr_r], dF)
                b.dma(SP, qscr[:, 1, 32:96, tok0:tok0 + W].rearrange("h r w -> r h w"), qT[32:96, :, :], [qTr], [qscr_r], dG)
                b.dma(SP, qscr[:, 1, 0:32, tok0:tok0 + W].rearrange("h r w -> r h w"), KT[0:32, :, :], [KTr], [qscr_r], dA)
            if STAGE < 3:
                continue
            s_ = load_blk(1)
            for hp in range(2):
                bk = b.bank()
                for hh in range(2):
                    h = 2 * hp + hh
                    for kc in range(8):
                        b.op(PE, lambda e: e.matmul(ps[bk][:, hh * W:(hh + 1) * W], lhsT=win[s_][:, kc, h * 128:(h + 1) * 128], rhs=hT[:, kc, :],
                                                    start=(kc == 0), stop=(kc == 7)), [winr[s_], hTr], [psr[bk]])
                for hh in range(2):
                    h = 2 * hp + hh
                    for tt in range(2):
                        src = ps[bk][:, hh * W + tt * 128:hh * W + (tt + 1) * 128]
                        b.op(ACT, lambda e: e.copy(out=rop[:, tt, 0, h, :], in_=src), [psr[bk]], [ropr[tt]])
                        b.op(DVE, lambda e: e.tensor_tensor(out=rop[:, tt, 1, h, :], in0=src, in1=QD[:, 1, h, :], op=ALU.mult),
                             [psr[bk], dr], [ropr[tt]])
                        b.op(DVE, lambda e: e.tensor_tensor(out=qdl[:, tt, h, :], in0=src, in1=QD[:, 0, h, :], op=ALU.mult),
                             [psr[bk], dr], [rlr[tt]])
            s_ = load_blk(2)
            for hp in range(2):
                bk = b.bank()
                for hh in range(2):
                    h = 2 * hp + hh
                    for kc in range(8):
                        b.op(PE, lambda e: e.matmul(ps[bk][:, hh * W:(hh + 1) * W], lhsT=win[s_][:, kc, h * 128:(h + 1) * 128], rhs=hT[:, kc, :],
                                                    start=(kc == 0), stop=(kc == 7)), [winr[s_], hTr], [psr[bk]])
                for tt in range(2):
                    copy_alt(tt, rop[:, tt, 2, 2 * hp:2 * hp + 2, :],
                             ps[bk][:, :].rearrange("p (h t i) -> p h t i", h=2, t=2)[:, :, tt, :], [psr[bk]], [ropr[tt]])
            for tt in range(2):
                bk = b.bank()
                for kc in range(8):
                    b.op(PE, lambda e: e.matmul(ps[bk][:, :], lhsT=hT[:, kc, tt * 128:(tt + 1) * 128], rhs=win[s_][:, kc, :],
                                                start=(kc == 0), stop=(kc == 7)), [winr[s_], hTr], [psr[bk]])
                for h in range(4):
                    b.op(ACT, lambda e: e.activation(out=rop[:, tt, 3, h, :], in_=ps[bk][:, h * 128:(h + 1) * 128], func=AF.Identity,
                                                     scale=KD[:, 1, h:h + 1]), [psr[bk], dr], [ropr[tt]])
                    b.op(ACT, lambda e: e.activation(out=kdl[:, tt, h, :], in_=ps[bk][:, h * 128:(h + 1) * 128], func=AF.Identity,
                                                     scale=KD[:, 0, h:h + 1]), [psr[bk], dr], [rlr[tt]])
            s_ = load_blk(3)
            for tt in range(2):
                bk = b.bank()
                for kc in range(8):
                    b.op(PE, lambda e: e.matmul(ps[bk][:, :], lhsT=hT[:, kc, tt * 128:(tt + 1) * 128], rhs=win[s_][:, kc, :],
                                                start=(kc == 0), stop=(kc == 7)), [winr[s_], hTr], [psr[bk]])
                copy_alt(tt, rop[:, tt, 4, :, :].rearrange("p h e -> p (h e)"), ps[bk][:, :], [psr[bk]], [ropr[tt]])
            s_ = load_blk(4)
            for tt in range(2):
                bk = b.bank()
                for kc in range(8):
                    b.op(PE, lambda e: e.matmul(ps[bk][:, :], lhsT=hT[:, kc, tt * 128:(tt + 1) * 128], rhs=win[s_][:, kc, :],
                                                start=(kc == 0), stop=(kc == 7)), [winr[s_], hTr], [psr[bk]])
                b.op(ACT, lambda e: e.activation(out=sgt[:, tt, :], in_=ps[bk][:, :], func=AF.Silu), [psr[bk]], [sgr[tt]])

            if STAGE < 4:
                continue
            if prompt:
                seq = g
                for hp in range(4):
                    bk = b.bank()
                    for hh in range(2):
                        h = 2 * hp + hh
                        b.op(PE, lambda e: e.matmul(ps[bk][0:96, hh * W:(hh + 1) * W], lhsT=wkn[:, h, :], rhs=ckvnT[:], start=True, stop=True),
                             [wr, tr], [psr[bk]])
                    for r0 in (32, 64):
                        copy_alt(hp + r0 // 32, KT[r0:r0 + 32, 2 * hp:2 * hp + 2, :], ps[bk][r0:r0 + 32, :].rearrange("p (h w) -> p h w", w=W),
                                 [psr[bk]], [KTr])
                for h in range(8):
                    copy_alt(h, KT[0:32, h, :], krT[:], [tr], [KTr])
                for kt in range(2):
                    bk = b.bank()
                    b.op(PE, lambda e: e.matmul(ps[bk][:, :], lhsT=ckvnT[:, kt * 128:(kt + 1) * 128], rhs=wv[:].rearrange("p h c -> p (h c)"),
                                                start=True, stop=True), [wr, tr], [psr[bk]])
                    copy_alt(kt, V[:, kt, :, 0:64], ps[bk][:, :].rearrange("p (h c) -> p h c", c=64), [psr[bk]], [Vr])
                pi = 0
                for h in range(8):
                    bo = b.bank()
                    for kt in range(2):
                        bk = b.bank()
                        b.op(PE, lambda e: e.matmul(ps[bk][:, 0:W], lhsT=KT[:, h, kt * 128:(kt + 1) * 128], rhs=qT[:, h, :], start=True, stop=True),
                             [KTr, qTr], [psr[bk]])
                        k = pi % 2
                        pi += 1
                        b.op(ACT, lambda e: e.activation(out=PT[k][:], in_=ps[bk][:, 0:W], func=AF.Exp, scale=SCALE), [psr[bk]], [PTr[k]])
                        for qi in range(2):
                            b.op(PE, lambda e: e.matmul(ps[bo][:, qi * 65:(qi + 1) * 65], lhsT=PT[k][:, qi * 128:(qi + 1) * 128], rhs=V[:, kt, h, :],
                                                        start=(kt == 0 and qi == 0), stop=(kt == 1), skip_group_check=True),
                                 [PTr[k], Vr], [psr[bo]])
                    b.op(DVE, lambda e: e.reciprocal(out=rc[:], in_=ps[bo][:, 0:130].rearrange("p (q c) -> p q c", c=65)[:, :, 64]),
                         [psr[bo]], [rcr])
                    for qi in range(2):
                        b.op(ACT, lambda e: e.activation(out=mix[:, qi, h * 64:(h + 1) * 64], in_=ps[bo][:, qi * 65:qi * 65 + 64], func=AF.Identity,
                                                         scale=rc[:, qi:qi + 1]), [psr[bo], rcr], [mixr[qi]])
                if STAGE < 5:
                    continue
                b.op(DVE, lambda e: e.memset(Sst[0][:], 0.0), [], [Sr[0]])
                b.op(DVE, lambda e: e.memset(Sbf[0][:], 0.0), [], [Sr[0]])
                for tt in range(2):
                    bo = ret_step(0, 0, rop[:, tt, 0], qdl[:, tt], rop[:, tt, 2], kdl[:, tt], rop[:, tt, 4], [ropr[tt], rlr[tt]])
                    b.op(ACT, lambda e: e.copy(out=ol[:, tt, :], in_=ps[bo][:, :]), [psr[bo]], [olr[tt]])
                b.dma(SP, O["o_state"][seq, j, 0].rearrange("h d e -> d h e"), Sst[0][:], [Sr[0]], [], dC)
                b.op(DVE, lambda e: e.memset(Sst[0][:], 0.0), [], [Sr[0]])
                b.op(DVE, lambda e: e.memset(Sbf[0][:], 0.0), [], [Sr[0]])
                for tt in (1, 0):
                    bo = ret_step(1, 0, rop[:, tt, 0], rop[:, tt, 1], rop[:, tt, 2], rop[:, tt, 3], rop[:, tt, 4], [ropr[tt]])
                    merge_tile(tt, bo, ol[:, tt, :], olr[tt], sgt[:, tt, :], sgr[tt], mix[:, tt, :], mixr[tt])
                b.dma(SP, O["o_state"][seq, j, 1].rearrange("h d e -> d h e"), Sst[0][:], [Sr[0]], [], dC)
                for tt in range(2):
                    b.dma(SP, mscr[t0 + tt], mix[:, tt, :], [mixr[tt]], [mscr_r[t0 + tt]], dH)
            elif STAGE >= 5:
                for tt in range(2):
                    ti = t0 - 8 + tt
                    bo = ret_step(0, 1, rop[:, tt, 0], qdl[:, tt], rop[:, tt, 2], kdl[:, tt], rop[:, tt, 4], [ropr[tt], rlr[tt]])
                    b.op(ACT, lambda e: e.copy(out=ol[:, tt, :], in_=ps[bo][:, :]), [psr[bo]], [olr[tt]])
                    b.dma(SP, oscr[ti], ol[:, tt, :], [olr[tt]], [oscr_r[ti]], dB)
                    b.dma(SP, rscr[ti], rop[:, tt], [ropr[tt]], [rscr_r[ti]], dC)
                    b.dma(SP, gscr2[ti], sgt[:, tt, :], [sgr[tt]], [gscr2_r[ti]], csd)
        b.dma(SP, cc2_in.ap().rearrange("(h d) e -> d h e", d=128), Sst[1][:], [Sr[1]], [cc2i_r], dC)
        if STAGE >= 6:
            b.allgather(cc1_in, cc1_out, cc1i_r, cc1o_r)
            b.allgather(cc2_in, cc2_out, cc2i_r, cc2o_r)
    if STAGE < 7:
        return

    with b.phase():
        NK = 36
        cT = b.sb("cT", [128, NK * 128], BF16); kR = b.sb("kR", [32, NK * 128], BF16); cTr = Res()
        ctxf = b.sb("ctxf", [128, 4, 160], F32); ctxr = Res()
        wkn = b.sb("wkn", [128, 8, 96], BF16); wv = b.sb("wv", [128, 8, 64], BF16); wr = Res()
        KT = [b.sb("KT", [96, NK * 128], BF16) for _ in range(2)]; KTr = RL(2)
        V = [b.sb("V", [128, NK, 65], BF16) for _ in range(2)]; Vr = RL(2)
        qh = [b.sb("qh", [96, 2, 512], BF16) for _ in range(2)]; qhr = RL(2); qhd = [b.ds() for _ in range(2)]
        PT = [b.sb("PT", [128, 512], BF16) for _ in range(4)]; PTr = RL(4)
        rc = b.sb("rc", [128, 4], F32); rcr = Res()
        ao = [b.sb("ao", [128, 4, 64], F32) for _ in range(2)]; aor = RL(2); aod = [b.ds() for _ in range(2)]
        oT = [b.sb("oT", [65, 512], F32) for _ in range(2)]; oTr = RL(2)
        dA = b.ds(); dB = b.ds(); dS = b.dsw()
        wukv = I["w_ukv"][j].rearrange("p (h c) -> p h c", c=128)
        b.op(DVE, lambda e: e.memset(wkn[:], 0.0), [], [wr])
        b.dma(POOL, wkn[:, :, 32:96], wukv[:, :, 0:64], [], [wr], dS)
        b.dma(POOL, wv[:], wukv[:, :, 64:128], [], [wr], dS)
        b.dma(SP, ctxf[:, :, 0:128], I["ctx_ckv"][j].rearrange("(t p) c -> p t c", p=128), [], [ctxr], dB)
        b.dma(SP, ctxf[:, :, 128:160], I["ctx_kr"][j].rearrange("(t p) c -> p t c", p=128), [], [ctxr], dB)
        bk = b.bank(); bk2 = b.bank()
        for kt in range(4):
            b.op(PE, lambda e: e.transpose(out=ps[bk][:, kt * 128:(kt + 1) * 128], in_=ctxf[:, kt, 0:128], identity=ident), [ctxr, cr], [psr[bk]])
            b.op(PE, lambda e: e.transpose(out=ps[bk2][0:32, kt * 128:(kt + 1) * 128], in_=ctxf[:, kt, 128:160], identity=ident),
                 [ctxr, cr], [psr[bk2]])
        b.op(ACT, lambda e: e.copy(out=cT[:, 0:512], in_=ps[bk][:, :]), [psr[bk]], [cTr])
        b.op(DVE, lambda e: e.tensor_copy(out=kR[:, 0:512], in_=ps[bk2][0:32, :]), [psr[bk2]], [cTr])
        for r_ in range(2):
            b.dma(SP, cT[:, 512 + r_ * 2048:512 + (r_ + 1) * 2048], cc1_out[r_ * 160:r_ * 160 + 128, :], [cc1o_r], [cTr], dA)
            b.dma(SP, kR[:, 512 + r_ * 2048:512 + (r_ + 1) * 2048], cc1_out[r_ * 160 + 128:r_ * 160 + 160, :], [cc1o_r], [cTr], dB)
        for s_ in range(2):
            b.op(DVE, lambda e: e.memset(V[s_][:, :, 64:65], 1.0), [], [Vr[s_]])
        sbanks = [0, 1, 2, 3, 4, 5]
        sbi = [0]

        def sbank():
            i = sbanks[sbi[0] % 6]
            sbi[0] += 1
            return i
        obanks = [6, 7]
        oi = 0
        pi = 0
        qi_ = 0
        def build_kv(h):
            s_ = h % 2
            for kb in range(9):
                bk = sbank()
                b.op(PE, lambda e: e.matmul(ps[bk][0:96, :], lhsT=wkn[:, h, :], rhs=cT[:, kb * 512:(kb + 1) * 512], start=True, stop=True),
                     [wr, cTr], [psr[bk]])
                for r0 in (32, 64):
                    b.op(DVE, lambda e: e.tensor_copy(out=KT[s_][r0:r0 + 32, kb * 512:(kb + 1) * 512], in_=ps[bk][r0:r0 + 32, :]),
                         [psr[bk]], [KTr[s_]])
            b.op(DVE, lambda e: e.tensor_copy(out=KT[s_][0:32, :], in_=kR[:]), [cTr], [KTr[s_]])
            for k8 in range(5):
                bk = sbank()
                nkt = min(8, NK - 8 * k8)
                for q in range(nkt):
                    kt = 8 * k8 + q
                    b.op(PE, lambda e: e.matmul(ps[bk][:, q * 64:(q + 1) * 64], lhsT=cT[:, kt * 128:(kt + 1) * 128], rhs=wv[:, h, :],
                                                start=True, stop=True), [wr, cTr], [psr[bk]])
                b.op(DVE, lambda e: e.tensor_copy(out=V[s_][:, 8 * k8:8 * k8 + nkt, 0:64],
                                                  in_=ps[bk][:, 0:nkt * 64].rearrange("p (q c) -> p q c", c=64)), [psr[bk]], [Vr[s_]])

        pending = []

        def make_epilogue(bo_, a_, h_, qb_):
            def run():
                b.op(DVE, lambda e: e.tensor_copy(out=oT[a_][:], in_=ps[bo_][0:65, :]), [psr[bo_]], [oTr[a_]])
                bt = sbank()
                for qi in range(4):
                    b.op(PE, lambda e: e.transpose(out=ps[bt][:, qi * 65:(qi + 1) * 65], in_=oT[a_][:, qi * 128:(qi + 1) * 128],
                                                   identity=ident[0:65, 0:65]), [oTr[a_], cr], [psr[bt]])
                b.op(DVE, lambda e: e.reciprocal(out=rc[:], in_=ps[bt][:, 0:260].rearrange("p (q c) -> p q c", c=65)[:, :, 64]),
                     [psr[bt]], [rcr])
                for qi in range(4):
                    b.op(DVE, lambda e: e.tensor_scalar(out=ao[a_][:, qi, :], in0=ps[bt][:, qi * 65:qi * 65 + 64], scalar1=rc[:, qi:qi + 1],
                                                        scalar2=None, op0=ALU.mult), [psr[bt], rcr], [aor[a_]])
                b.dma(SP, ascr[qb_ * 4:qb_ * 4 + 4, :, h_ * 64:(h_ + 1) * 64].rearrange("t p c -> p t c"), ao[a_][:], [aor[a_]],
                      ascr_r[qb_ * 4:qb_ * 4 + 4], aod[a_])
            return run

        build_kv(0)
        for h in range(8):
            s_ = h % 2
            for qb in range(4):
                qs = qi_ % 2
                qi_ += 1
                b.dma(SP, qh[qs][:], qscr[h, :, :, qb * 512:(qb + 1) * 512].rearrange("v r w -> r v w"), [qscr_r], [qhr[qs]], qhd[qs])
                bo = obanks[oi % 2]
                oi += 1
                if qb == 1 and h + 1 < 8:
                    build_kv(h + 1)
                pk = [None] * NK
                SKEW = 2
                for kt in range(NK + SKEW):
                    if kt < NK:
                        ver = 0 if kt < 4 else 1
                        bk = sbank()
                        b.op(PE, lambda e: e.matmul(ps[bk][:, :], lhsT=KT[s_][:, kt * 128:(kt + 1) * 128], rhs=qh[qs][:, ver, :],
                                                    start=True, stop=True), [KTr[s_], qhr[qs]], [psr[bk]])
                        k = pi % 4
                        pi += 1
                        pk[kt] = k
                        b.op(ACT, lambda e: e.activation(out=PT[k][:], in_=ps[bk][:, :], func=AF.Exp, scale=SCALE), [psr[bk]], [PTr[k]])
                    if kt == SKEW and pending:
                        pending.pop(0)()
                    if kt >= SKEW:
                        kp = kt - SKEW
                        k = pk[kp]
                        b.op(PE, lambda e: e.matmul(ps[bo][0:65, :], lhsT=V[s_][:, kp, :], rhs=PT[k][:, :],
                                                    start=(kp == 0), stop=(kp == NK - 1)), [PTr[k], Vr[s_]], [psr[bo]])
                pending.append(make_epilogue(bo, (oi - 1) % 2, h, qb))
        for ep in pending:
            ep()
        pending.clear()

    if STAGE < 8:
        return
    with b.phase():
        ss = b.sb("ss", [128, 8], F32); ssr = Res()
        ldb = b.sb("ldb", [128, 16], F32)
        DT = b.sb("DT", [128, 2, 4, 128], BF16)
        CDc = b.sb("CD", [128, 2, 4], F32)
        etmp = b.sb("etmp", [128, 128], F32)
        dr = Res()
        wo = b.sb("wo", [128, 8, D], BF16); wor = Res()
        rop = [b.sb("rop", [128, 5, 4, 128], BF16) for _ in range(2)]; ropr = RL(2); ropd = [b.ds() for _ in range(2)]
        ol = [b.sb("ol", [128, 512], F32) for _ in range(2)]; olr = RL(2); old = [b.ds() for _ in range(2)]
        sgt = [b.sb("sgt", [128, 512], BF16) for _ in range(2)]; sgr = RL(2); sgd = [b.ds() for _ in range(2)]
        mix = [b.sb("mix", [128, D], F32) for _ in range(2)]; mixr = RL(2); mixd = [b.ds() for _ in range(2)]
        Sst = b.sb("Sst", [128, 4, 128], F32); Sbf = b.sb("Sbf", [128, 4, 128], BF16); Sr = Res()
        R0 = b.sb("R0", [128, 2, 4, 128], F32); R0r = Res()
        pw = b.sb("pw", [128, 2], F32)
        AD = b.sb("AD", [128, 4, 128], BF16); ADr = Res()
        osum = b.sb("osum", [128, 512], F32); osr = Res()
        mixT = b.sb("mixT", [128, 8, 128], BF16); mixTr = RL(8)
        tmp = [b.sb("tmp", [128, 512], F32) for _ in range(2)]; tmpr = RL(2)
        dA = b.ds(); dB = b.ds()
        b.dma(POOL, wo[:], I["w_o_ab"][j].rearrange("(kc p) n -> p kc n", p=128), [], [wor], b.dsw())
        b.dma(SP, ldb[:], I["ldec"][0].partition_broadcast(128), [], [dr], dA)
        b.dma(SP, pw[:], I["pw"][:, :], [], [dr], dA)
        b.op(ACT, lambda e: e.activation(out=ldb[:], in_=ldb[:], func=AF.Exp), [dr], [dr])
        b.op(DVE, lambda e: e.tensor_scalar(out=ldb[:], in0=ldb[:], scalar1=-1.0, scalar2=None, op0=ALU.mult), [dr], [dr])
        for h in range(4):
            lg = ldb[:, j * 8 + 4 + h:j * 8 + 4 + h + 1]
            b.op(ACT, lambda e: e.activation(out=etmp[:], in_=consts[:, 3, :], func=AF.Exp, scale=lg), [dr, cr], [dr])
            b.op(DVE, lambda e: e.scalar_tensor_tensor(out=DT[:, 1, h, :], in0=etmp[:], scalar=RS, in1=consts[:, 4, :],
                                                       op0=ALU.mult, op1=ALU.mult), [dr, cr], [dr])
            b.op(ACT, lambda e: e.activation(out=CDc[:, 1, h:h + 1], in_=consts[:, 5, 4:5], func=AF.Exp, scale=lg), [dr, cr], [dr])
        for r_ in range(2):
            b.dma(SP, R0[:, r_], cc2_out[r_ * 512:(r_ + 1) * 512, :].rearrange("(h d) e -> d h e", d=128), [cc2o_r], [R0r], dB)
        b.op(DVE, lambda e: e.tensor_scalar(out=Sst[:], in0=R0[:, 0], scalar1=pw[:, 0:1], scalar2=None, op0=ALU.mult), [R0r, dr], [Sr])
        b.op(DVE, lambda e: e.scalar_tensor_tensor(out=Sst[:], in0=R0[:, 1], scalar=pw[:, 1:2], in1=Sst[:], op0=ALU.mult, op1=ALU.add),
             [R0r, dr, Sr], [Sr])
        b.op(ACT, lambda e: e.copy(out=Sbf[:], in_=Sst[:]), [Sr], [Sr])
        def proj_tile(t, mx, mxr, mT, mTr, tp, tpr):
            for kc in range(8):
                bk = b.bank()
                b.op(PE, lambda e: e.transpose(out=ps[bk][:, 0:128], in_=mx[:, kc * 128:(kc + 1) * 128], identity=ident),
                     [mxr, cr], [psr[bk]])
                copy_alt(kc, mT[:, kc, :], ps[bk][:, 0:128], [psr[bk]], [mTr[kc]])
            for half in range(2):
                bk = b.bank()
                for kc in range(8):
                    b.op(PE, lambda e: e.matmul(ps[bk][:, :], lhsT=mT[:, kc, :], rhs=wo[:, kc, half * 512:(half + 1) * 512],
                                                start=(kc == 0), stop=(kc == 7)), [mTr[kc], wor], [psr[bk]])
                resid_update(t, half, bk, 0, tp[half], tpr[half])

        mixp = [b.sb("mixp", [128, D], F32) for _ in range(2)]; mixpr = RL(2); mixpd = [b.ds() for _ in range(2)]
        mixTp = b.sb("mixTp", [128, 8, 128], BF16); mixTpr = RL(8)
        tmpp = [b.sb("tmpp", [128, 512], F32) for _ in range(2)]; tmppr = RL(2)

        def prompt_tile(t):
            k = t % 2
            b.dma(SP, mixp[k][:], mscr[t], [mscr_r[t]], [mixpr[k]], mixpd[k])
            proj_tile(t, mixp[k], mixpr[k], mixTp, mixTpr, tmpp, tmppr)

        osum2 = [b.sb("osum2", [128, 512], F32) for _ in range(2)]; osr2 = RL(2)

        def scan_tile(n_):
            ti = 15 - n_
            k = n_ % 2
            b.dma(SP, rop[k][:], rscr[ti], [rscr_r[ti]], [ropr[k]], ropd[k])
            b.dma(SP, ol[k][:], oscr[ti], [oscr_r[ti]], [olr[k]], old[k])
            b.dma(SP, sgt[k][:], gscr2[ti], [gscr2_r[ti]], [sgr[k]], sgd[k])
            b.dma(SP, mix[k][:, 0:512], ascr[ti], [ascr_r[ti]], [mixr[k]], mixd[k])
            rd = [ropr[k]]
            ba = b.bank()
            for h in range(4):
                b.op(PE, lambda e: e.matmul(ps[ba][:, h * 128:(h + 1) * 128], lhsT=rop[k][:, 2, h, :], rhs=rop[k][:, 0, h, :], start=True, stop=True),
                     rd, [psr[ba]])
            b.op(DVE, lambda e: e.tensor_tensor(out=AD[:].rearrange("p h i -> p (h i)"), in0=ps[ba][:, :],
                                                in1=DT[:, 1].rearrange("p h i -> p (h i)"), op=ALU.mult), [psr[ba], dr], [ADr])
            bo = b.bank()
            for h in range(4):
                b.op(PE, lambda e: e.matmul(ps[bo][:, h * 128:(h + 1) * 128], lhsT=AD[:, h, :], rhs=rop[k][:, 4, h, :], start=True, stop=False),
                     rd + [ADr], [psr[bo]])
                b.op(PE, lambda e: e.matmul(ps[bo][:, h * 128:(h + 1) * 128], lhsT=rop[k][:, 1, h, :], rhs=Sbf[:, h, :], start=False, stop=True),
                     rd + [Sr], [psr[bo]])
            bs_ = b.bank()
            for h in range(4):
                b.op(PE, lambda e: e.matmul(ps[bs_][:, h * 128:(h + 1) * 128], lhsT=rop[k][:, 3, h, :], rhs=rop[k][:, 4, h, :], start=True, stop=True),
                     rd, [psr[bs_]])
            for h in range(4):
                b.op(DVE, lambda e: e.scalar_tensor_tensor(out=Sst[:, h, :], in0=Sst[:, h, :], scalar=CDc[:, 1, h:h + 1],
                                                           in1=ps[bs_][:, h * 128:(h + 1) * 128], op0=ALU.mult, op1=ALU.add),
                     [psr[bs_], dr, Sr], [Sr])
            b.op(ACT, lambda e: e.copy(out=Sbf[:], in_=Sst[:]), [Sr], [Sr])
            b.op(DVE, lambda e: e.tensor_tensor(out=osum2[k][:], in0=ps[bo][:, :], in1=ol[k][:], op=ALU.add), [psr[bo], olr[k]], [osr2[k]])

        def finish_tile(n_):
            ti = 15 - n_
            k = n_ % 2
            t = 8 + ti
            osum_, osr_ = osum2[k], osr2[k]
            for h in range(4):
                b.op(ACT, lambda e: e.activation(out=etmp[:], in_=osum_[:, h * 128:(h + 1) * 128], func=AF.Square,
                                                 accum_out=ss[:, 4 + h:5 + h]), [osr_], [dr, ssr])
            b.op(DVE, lambda e: e.tensor_scalar(out=ss[:, 4:8], in0=ss[:, 4:8], scalar1=1.0 / 128, scalar2=EPS, op0=ALU.mult, op1=ALU.add),
                 [ssr], [ssr])
            b.op(ACT, lambda e: e.activation(out=ss[:, 4:8], in_=ss[:, 4:8], func=AF.Sqrt), [ssr], [ssr])
            b.op(DVE, lambda e: e.reciprocal(out=ss[:, 4:8], in_=ss[:, 4:8]), [ssr], [ssr])
            for h in range(4):
                b.op(DVE, lambda e: e.scalar_tensor_tensor(out=mix[k][:, 512 + h * 128:512 + (h + 1) * 128], in0=osum_[:, h * 128:(h + 1) * 128],
                                                           scalar=ss[:, 4 + h:5 + h], in1=sgt[k][:, h * 128:(h + 1) * 128],
                                                           op0=ALU.mult, op1=ALU.mult), [osr_, ssr, sgr[k]], [mixr[k]])
            proj_tile(t, mix[k], mixr[k], mixT, mixTr, tmp, tmpr)

        scan_tile(0)
        for n_ in range(16):
            if n_ + 1 < 16:
                scan_tile(n_ + 1)
            if n_ % 2 == 0:
                prompt_tile(n_ // 2)
            finish_tile(n_)


_NC = None


def _rope_tables(pos):
    half = 16
    inv = 1.0 / np.power(np.float32(10000.0), np.arange(0, half, 2, dtype=np.float32) / np.float32(half))
    row = (pos // 64).astype(np.float32)
    col = (pos % 64).astype(np.float32)
    ang = np.stack([row[:, None] * inv, col[:, None] * inv], axis=1)
    ang = np.stack([ang, ang], axis=2).reshape(len(pos), 32)
    return np.cos(ang).astype(np.float32), np.sin(ang).astype(np.float32)


def _consts():
    c = np.zeros((128, 8, 128), np.float32)
    p = np.arange(128, dtype=np.float32)[:, None]
    f = np.arange(128, dtype=np.float32)[None, :]
    c[:, 0] = (p == f)
    c[:, 1] = np.maximum(f - p, 0)
    c[:, 2] = (f >= p)
    c[:, 3] = np.maximum(p - f, 0)
    c[:, 4] = (p >= f)
    c[:, 5, 0] = p[:, 0] + 1.0
    c[:, 5, 1] = 127.0 - p[:, 0]
    c[:, 5, 2] = 128.0 - p[:, 0]
    c[:, 5, 3] = p[:, 0]
    c[:, 5, 4] = 128.0
    c[:, 5, 5] = 1.0
    c[:, 6] = f + 1.0
    c[:, 7] = 128.0 - f
    return c


def _rmT():
    rm = np.zeros((32, 32), np.float32)
    for a in range(2):
        o = 16 * a
        for i in range(8):
            rm[o + 8 + i, o + i] = -1.0
            rm[o + i, o + 8 + i] = 1.0
    return rm


def kernel(x_prompt, x_sample, cache_mla_ckv, cache_mla_krope, state_ret, c, c_ctx,
           w_ada, b_ada, norm_g, w_in_ab, q_norm_g, w_uq, kv_norm_g, w_ukv, ret_log_decay, w_o_ab,
           w_in_c, ln_g_c, ln_b_c, w_s_c, b_s_c, w_out_c, w_ffn_gate, w_ffn_up, w_ffn_down, final_norm_g):
    global _NC
    f = lambda a: np.ascontiguousarray(np.asarray(a, dtype=np.float32))
    x_prompt, x_sample = f(x_prompt), f(x_sample)
    shared = {
        "w_ada": f(w_ada), "b_ada": f(b_ada),
        "b_adaT": f(np.asarray(b_ada).reshape(4, 48, 128).transpose(0, 2, 1)),
        "norm_gT": f(np.asarray(norm_g).reshape(4, 2, 8, 128).transpose(0, 1, 3, 2)),
        "w_in_ab": f(w_in_ab), "q_norm_gT": f(np.asarray(q_norm_g).reshape(2, 2, 128).transpose(0, 2, 1)),
        "w_uq": f(w_uq), "kv_norm_g": f(kv_norm_g), "w_ukv": f(w_ukv), "w_o_ab": f(w_o_ab),
        "w_in_c": f(w_in_c), "ln_g_c": f(ln_g_c), "ln_b_c": f(ln_b_c), "w_out_c": f(w_out_c),
        "w_ffn_gate": f(w_ffn_gate), "w_ffn_up": f(w_ffn_up), "w_ffn_down": f(w_ffn_down),
        "final_norm_g": f(final_norm_g), "consts": _consts(), "rmT": _rmT(),
    }
    ws = np.asarray(w_s_c, dtype=np.float32)
    bs = np.asarray(b_s_c, dtype=np.float32)
    wsT_nat = f(ws.transpose(0, 3, 1, 2))
    wsT_rev = f(ws[:, :, ::-1, ::-1].transpose(0, 3, 1, 2))
    bsT_nat = f(bs.transpose(0, 2, 1))
    bsT_rev = f(bs[:, :, ::-1].transpose(0, 2, 1))
    in_maps = []
    for r in range(8):
        p, par = r // 2, r % 2
        xp = x_prompt[4 * r:4 * r + 4]
        xs = x_sample[p, par * 2048:(par + 1) * 2048]
        pos = np.arange(par * 2048, (par + 1) * 2048)
        if par:
            xp = xp[:, ::-1]
            xs = xs[::-1]
            pos = pos[::-1]
        cos, sin = _rope_tables(pos)
        cond2 = np.stack([np.asarray(c_ctx, np.float32), np.asarray(c, np.float32)[p]])
        pw = np.zeros((128, 2), np.float32)
        pw[:, 1 - par] = 1.0
        ld = np.asarray(ret_log_decay, np.float32)[:, [par, 1 - par], :]
        m = dict(shared)
        m.update({
            "x_in": f(np.concatenate([xp.reshape(1024, D), xs], axis=0)),
            "cond2T": f(cond2.reshape(2, 8, 128).transpose(2, 1, 0)),
            "ctx_ckv": f(np.asarray(cache_mla_ckv)[p]), "ctx_kr": f(np.asarray(cache_mla_krope)[p]),
            "s0_lead": f(np.asarray(state_ret)[p, :, par]),
            "ldec": f(ld.reshape(1, 16)),
            "ropecs": f(np.stack([cos.T, sin.T])),
            "pw": pw,
            "wsT": wsT_rev if par else wsT_nat, "b_sT": bsT_rev if par else bsT_nat,
        })
        in_maps.append(m)
    if _NC is None:
        _NC = build_program()
    res = run_bass_kernel_spmd(_NC, in_maps, core_ids=list(range(8)))
    y_prompt = np.zeros((32, 256, D), np.float32)
    y_sample = np.zeros((4, 4096, D), np.float32)
    o_ckv = np.zeros((32, 2, 256, 128), np.float32)
    o_kr = np.zeros((32, 2, 256, 32), np.float32)
    o_st = np.zeros((32, 2, 2, 4, 128, 128), np.float32)
    for r in range(8):
        p, par = r // 2, r % 2
        o = res.results[r]
        yp = o["y"][:1024].reshape(4, 256, D)
        ys = o["y"][1024:]
        ck = o["o_ckv"].reshape(2, 4, 256, 128).transpose(1, 0, 2, 3)
        kr = o["o_kr"].reshape(2, 4, 256, 32).transpose(1, 0, 2, 3)
        stt = o["o_state"]
        if par:
            yp, ys, ck, kr = yp[:, ::-1], ys[::-1], ck[:, :, ::-1], kr[:, :, ::-1]
            stt = stt[:, :, ::-1]
        o_st[4 * r:4 * r + 4] = stt
        y_prompt[4 * r:4 * r + 4] = yp
        y_sample[p, par * 2048:(par + 1) * 2048] = ys
        o_ckv[4 * r:4 * r + 4] = ck
        o_kr[4 * r:4 * r + 4] = kr
    return (y_prompt, y_sample, o_ckv, o_kr, o_st)
```
